# Optimizing a Trainium2 kernel written in Bass

```python
import math
import jax, jax.numpy as jnp
from jax import lax
import numpy as np

D_MODEL = 1024
BATCH = 8
SEQ = 2048
DEPTH = 2
DEC_BATCH = 128
DEC_SEQ = 8
PAST_LEN = 16384
PAGE_SIZE = 128

BRANCH_W = D_MODEL // 2
N_BRANCH = 4
HG_HEADS = 4
HG_DK = BRANCH_W // HG_HEADS
HG_DV = BRANCH_W // HG_HEADS
HG_CHUNK = 32
RW_HEAD = 64
RW_HEADS = BRANCH_W // RW_HEAD
RW_W_RANK = 64
RW_A_RANK = 64
RW_G_RANK = 128
RW_DECAY_SCALE = 0.606531
RW_GN_EPS = 64e-5
CF_WIDTH = 31
LRU_BLOCKS = 8
LRU_BW = BRANCH_W // LRU_BLOCKS
LRU_CONV = 4
LRU_C = 8.0
MLP_HIDDEN = 4 * D_MODEL
EPS = 1e-6

HG_COLS = 4 * BRANCH_W
RW_COLS = 3 * BRANCH_W + RW_W_RANK + RW_A_RANK + RW_G_RANK
CF_COLS = 2 * BRANCH_W
LRU_COLS = 2 * BRANCH_W
IN_COLS = HG_COLS + RW_COLS + CF_COLS + LRU_COLS

kernel_name = "hgrn2_rwkv7_conformer_rglru_parallel_decoder_step"


def rmsnorm(x, g):
    xf = x.astype(jnp.float32)
    y = xf * lax.rsqrt(jnp.mean(xf * xf, axis=-1, keepdims=True) + EPS)
    return (y * g.astype(jnp.float32)).astype(x.dtype)


def layernorm(x, g, b):
    xf = x.astype(jnp.float32)
    mu = jnp.mean(xf, axis=-1, keepdims=True)
    var = jnp.mean(jnp.square(xf - mu), axis=-1, keepdims=True)
    y = (xf - mu) * lax.rsqrt(var + 1e-5) * g.astype(jnp.float32) + b.astype(jnp.float32)
    return y.astype(x.dtype)


def causal_dwconv(buf, u, w, b):
    k = w.shape[0]
    full = jnp.concatenate([buf.astype(u.dtype), u], axis=1)
    y = lax.conv_general_dilated(full, w[:, None, :].astype(u.dtype), window_strides=(1,),
                                 padding="VALID", dimension_numbers=("NWC", "WIO", "NWC"),
                                 feature_group_count=u.shape[-1])
    return y + b.astype(u.dtype), full[:, -(k - 1):]


def hgrn2_chunked(q, k, logf, v, s0):
    bsz, t, h, _ = q.shape
    dv = v.shape[-1]
    c = math.gcd(t, HG_CHUNK)
    n = t // c

    def to_chunks(a):
        return a.reshape(bsz, n, c, h, a.shape[-1]).transpose(1, 0, 3, 2, 4)

    causal = jnp.tril(jnp.ones((c, c), dtype=bool))[:, :, None]

    def step(s, inp):
        qi, ki, fi, vi = inp
        b = jnp.cumsum(fi, axis=2)
        o_inter = jnp.einsum("bhtd,bhdv->bhtv", qi * jnp.exp(b), s)
        diff = b[:, :, :, None, :] - b[:, :, None, :, :]
        decay = jnp.exp(jnp.where(causal, diff, -jnp.inf))
        scores = jnp.sum(qi[:, :, :, None, :] * ki[:, :, None, :, :] * decay, axis=-1)
        o = o_inter + jnp.einsum("bhts,bhsv->bhtv", scores, vi)
        b_last = b[:, :, -1:, :]
        s_new = jnp.exp(b_last[:, :, 0, :])[..., None] * s + \
            jnp.einsum("bhsd,bhsv->bhdv", ki * jnp.exp(b_last - b), vi)
        return s_new, o

    s_fin, o = lax.scan(step, s0, tuple(map(to_chunks, (q, k, logf, v))))
    return o.transpose(1, 0, 3, 2, 4).reshape(bsz, t, h, dv), s_fin


def rwkv7_scan(r, w, k, v, kk, a, s0):
    def step(s, inp):
        rt, wt, kt, vt, kkt, at = inp
        sa = jnp.einsum("bhvk,bhk->bhv", s, -kkt)
        s = s * wt[:, :, None, :] + sa[..., None] * (kkt * at)[:, :, None, :] + vt[..., None] * kt[:, :, None, :]
        return s, jnp.einsum("bhvk,bhk->bhv", s, rt)

    xs = tuple(jnp.moveaxis(z, 1, 0) for z in (r, w, k, v, kk, a))
    s_fin, y = lax.scan(step, s0, xs)
    return jnp.moveaxis(y, 0, 1), s_fin


def _lin_combine(e1, e2):
    a1, b1 = e1
    a2, b2 = e2
    return a1 * a2, a2 * b1 + b2


def token_mixers(h, p, l, lb_l, s_hg, s_rw, s_shift, s_cf, s_lh, s_lc):
    bsz, t, _ = h.shape
    dt = h.dtype
    f32 = jnp.float32
    W = BRANCH_W
    proj = h @ p["w_in"][l]
    hg, rw, cf, lr = jnp.split(proj, [HG_COLS, HG_COLS + RW_COLS, HG_COLS + RW_COLS + CF_COLS], axis=-1)

    q, fz, iv, og = jnp.split(hg.astype(f32), 4, axis=-1)
    lb_l = lb_l.astype(f32)
    logf = jnp.log(lb_l + (1.0 - lb_l) * jax.nn.sigmoid(fz))
    kf = (1.0 - lb_l) * jax.nn.sigmoid(-fz)
    o, s_hg_new = hgrn2_chunked(jax.nn.silu(q).reshape(bsz, t, HG_HEADS, HG_DK),
                                kf.reshape(bsz, t, HG_HEADS, HG_DK),
                                logf.reshape(bsz, t, HG_HEADS, HG_DK),
                                iv.reshape(bsz, t, HG_HEADS, HG_DV), s_hg.astype(f32))
    o = o * lax.rsqrt(jnp.mean(o * o, axis=-1, keepdims=True) + EPS)
    y_hg = o.reshape(bsz, t, W) * p["hg_norm_g"][l] * jax.nn.silu(og)

    prev = jnp.concatenate([s_shift[:, None, :].astype(dt), rw[:, :-1]], axis=1)
    rwm = (rw + (prev - rw) * p["rw_mu"][l]).astype(f32)
    r, k, v, wd, ad, gd = jnp.split(rwm, [W, 2 * W, 3 * W, 3 * W + RW_W_RANK, 3 * W + RW_W_RANK + RW_A_RANK], axis=-1)
    log_w = -RW_DECAY_SCALE * jax.nn.sigmoid(p["rw_w0"][l] + jnp.tanh(wd) @ p["rw_w_up"][l])
    a = jax.nn.sigmoid(p["rw_a0"][l] + ad @ p["rw_a_up"][l])
    gate = jax.nn.sigmoid(gd) @ p["rw_g_up"][l]
    hr = lambda z: z.reshape(bsz, t, RW_HEADS, RW_HEAD)
    kk = hr(k * p["rw_k_k"][l])
    kk = kk / jnp.maximum(jnp.sqrt(jnp.sum(kk * kk, axis=-1, keepdims=True)), 1e-12)
    k = k * (1.0 + (a - 1.0) * p["rw_k_a"][l])
    y, s_rw_new = rwkv7_scan(hr(r), hr(jnp.exp(log_w)), hr(k), hr(v), kk, hr(a), s_rw.astype(f32))
    mu = jnp.mean(y, axis=-1, keepdims=True)
    var = jnp.mean(jnp.square(y - mu), axis=-1, keepdims=True)
    y = ((y - mu) * lax.rsqrt(var + RW_GN_EPS)).reshape(bsz, t, W) * p["rw_ln_g"][l] + p["rw_ln_b"][l]
    bonus = jnp.sum(hr(r * k * p["rw_r_k"][l]), axis=-1, keepdims=True) * hr(v)
    y_rw = (y + bonus.reshape(bsz, t, W)) * gate

    u = cf[..., :W] * jax.nn.sigmoid(cf[..., W:])
    yc, s_cf_new = causal_dwconv(s_cf, u, p["cf_dw"][l], p["cf_dw_b"][l])
    y_cf = jax.nn.silu(layernorm(yc, p["cf_ln_g"][l], p["cf_ln_b"][l]))

    xl, gl = lr[..., :W], lr[..., W:]
    xc, s_lc_new = causal_dwconv(s_lc, xl, p["lru_conv_w"][l], p["lru_conv_b"][l])
    xcf = xc.astype(f32)
    xb = xcf.reshape(bsz, t, LRU_BLOCKS, LRU_BW)
    rg = jax.nn.sigmoid(jnp.einsum("btnc,ncd->btnd", xb, p["lru_wa"][l]).reshape(bsz, t, W) + p["lru_ba"][l])
    ig = jax.nn.sigmoid(jnp.einsum("btnc,ncd->btnd", xb, p["lru_wx"][l]).reshape(bsz, t, W) + p["lru_bx"][l])
    log_a = -LRU_C * rg * jax.nn.softplus(-p["lru_lambda"][l].astype(f32))
    a_t = jnp.exp(log_a)
    b_t = jnp.sqrt(-jnp.expm1(2.0 * log_a)) * (ig * xcf)
    b_t = b_t.at[:, 0].add(a_t[:, 0] * s_lh.astype(f32))
    _, hs = lax.associative_scan(_lin_combine, (a_t, b_t), axis=1)
    y_lru = hs * jax.nn.gelu(gl.astype(f32))

    branches = jnp.stack([y_hg, y_rw, y_cf, y_lru], axis=2).astype(dt)
    bo = jnp.einsum("btkc,kcd->btkd", branches, p["w_branch"][l])
    gates = jax.nn.sigmoid((h @ p["w_gate"][l] + p["b_gate"][l]).reshape(bsz, t, N_BRANCH, D_MODEL))
    out = jnp.sum(gates * bo, axis=2) @ p["w_out"][l]
    new_states = (s_hg_new.astype(s_hg.dtype), s_rw_new.astype(s_rw.dtype), rw[:, -1].astype(s_shift.dtype),
                  s_cf_new.astype(s_cf.dtype), hs[:, -1].astype(s_lh.dtype), s_lc_new.astype(s_lc.dtype))
    return out, new_states


def trunk(x, c, states, p):
    sm = jax.nn.softmax(p["hg_lower"].astype(jnp.float32), axis=0)
    lb = jnp.cumsum(sm, axis=0) - sm[0]
    mod_c = jax.nn.silu(c)
    collected = [[] for _ in range(6)]
    for l in range(DEPTH):
        mod = mod_c @ p["ada_w"][l] + p["ada_b"][l]
        sh1, sc1, g1, sh2, sc2, g2 = [m[:, None, :] for m in jnp.split(mod, 6, axis=-1)]
        h = rmsnorm(x, p["norm_mix_g"][l]) * (1.0 + sc1) + sh1
        out, st_new = token_mixers(h, p, l, lb[l], *[s[l] for s in states])
        x = x + g1 * out
        h2 = rmsnorm(x, p["norm_mlp_g"][l]) * (1.0 + sc2) + sh2
        x = x + g2 * (jnp.square(jax.nn.relu(h2 @ p["w_mlp1"][l])) @ p["w_mlp2"][l])
        for lst, s in zip(collected, st_new):
            lst.append(s)
    y = rmsnorm(x, p["norm_final_g"])
    return y, [jnp.stack(lst, axis=0) for lst in collected]


def setup_inputs(seed: int = 0) -> dict:
    key = jax.random.key(seed)
    keys = jax.random.split(key, 64)
    ctr = [0]

    def nrm(shape, s):
        kk = keys[ctr[0]]
        ctr[0] += 1
        return s * jax.random.normal(kk, shape, jnp.float32)

    def unif(shape, lo, hi):
        kk = keys[ctr[0]]
        ctr[0] += 1
        return jax.random.uniform(kk, shape, jnp.float32, lo, hi)

    D, W, L = D_MODEL, BRANCH_W, DEPTH
    inp = {}
    inp["x_prompt"] = nrm((BATCH, SEQ, D), 1.0)
    inp["x_sample"] = nrm((DEC_BATCH, DEC_SEQ, D), 1.0)
    inp["state_hgrn"] = nrm((L, DEC_BATCH, HG_HEADS, HG_DK, HG_DV), 0.5)
    inp["state_rwkv"] = nrm((L, DEC_BATCH, RW_HEADS, RW_HEAD, RW_HEAD), 0.3)
    inp["state_rwkv_shift"] = nrm((L, DEC_BATCH, RW_COLS), 1.0)
    inp["state_conv"] = nrm((L, DEC_BATCH, CF_WIDTH - 1, W), 0.5)
    inp["state_lru_h"] = nrm((L, DEC_BATCH, W), 0.5)
    inp["state_lru_conv"] = nrm((L, DEC_BATCH, LRU_CONV - 1, W), 1.0)
    inp["c_prompt"] = nrm((BATCH, D), 1.0)
    inp["c_sample"] = nrm((DEC_BATCH, D), 1.0)
    inp["ada_w"] = nrm((L, D, 6 * D), 0.5 * D ** -0.5)
    inp["ada_b"] = nrm((L, 6 * D), 0.02)
    inp["norm_mix_g"] = 1.0 + nrm((L, D), 0.05)
    inp["norm_mlp_g"] = 1.0 + nrm((L, D), 0.05)
    inp["norm_final_g"] = 1.0 + nrm((D,), 0.05)
    inp["w_in"] = nrm((L, D, IN_COLS), D ** -0.5)
    inp["hg_lower"] = nrm((L, W), 0.1)
    inp["hg_norm_g"] = 1.0 + nrm((L, W), 0.05)
    inp["rw_mu"] = unif((L, RW_COLS), 0.0, 1.0)
    inp["rw_w0"] = nrm((L, W), 0.5)
    inp["rw_w_up"] = nrm((L, RW_W_RANK, W), 0.5 * RW_W_RANK ** -0.5)
    inp["rw_a0"] = nrm((L, W), 0.1)
    inp["rw_a_up"] = nrm((L, RW_A_RANK, W), RW_A_RANK ** -0.5)
    inp["rw_g_up"] = nrm((L, RW_G_RANK, W), RW_G_RANK ** -0.5)
    inp["rw_k_k"] = 0.85 + nrm((L, W), 0.05)
    inp["rw_k_a"] = 1.0 + nrm((L, W), 0.05)
    inp["rw_r_k"] = nrm((L, W), 0.1)
    inp["rw_ln_g"] = 1.0 + nrm((L, W), 0.05)
    inp["rw_ln_b"] = nrm((L, W), 0.02)
    inp["cf_dw"] = nrm((L, CF_WIDTH, W), CF_WIDTH ** -0.5)
    inp["cf_dw_b"] = nrm((L, W), 0.02)
    inp["cf_ln_g"] = 1.0 + nrm((L, W), 0.05)
    inp["cf_ln_b"] = nrm((L, W), 0.02)
    inp["lru_conv_w"] = nrm((L, LRU_CONV, W), LRU_CONV ** -0.5)
    inp["lru_conv_b"] = nrm((L, W), 0.02)
    inp["lru_wa"] = nrm((L, LRU_BLOCKS, LRU_BW, LRU_BW), LRU_BW ** -0.5)
    inp["lru_ba"] = nrm((L, W), 0.02)
    inp["lru_wx"] = nrm((L, LRU_BLOCKS, LRU_BW, LRU_BW), LRU_BW ** -0.5)
    inp["lru_bx"] = nrm((L, W), 0.02)
    a0 = unif((L, W), 0.9, 0.999) ** (1.0 / LRU_C)
    inp["lru_lambda"] = jnp.log(a0) - jnp.log1p(-a0)
    inp["w_branch"] = nrm((L, N_BRANCH, W, D), W ** -0.5)
    inp["w_gate"] = nrm((L, D, N_BRANCH * D), D ** -0.5)
    inp["b_gate"] = nrm((L, N_BRANCH * D), 0.02)
    inp["w_out"] = nrm((L, D, D), D ** -0.5)
    inp["w_mlp1"] = nrm((L, D, MLP_HIDDEN), D ** -0.5)
    inp["w_mlp2"] = nrm((L, MLP_HIDDEN, D), MLP_HIDDEN ** -0.5)
    return inp


def reference(x_prompt, x_sample, state_hgrn, state_rwkv, state_rwkv_shift, state_conv, state_lru_h,
              state_lru_conv, c_prompt, c_sample, ada_w, ada_b, norm_mix_g, norm_mlp_g, norm_final_g,
              w_in, hg_lower, hg_norm_g, rw_mu, rw_w0, rw_w_up, rw_a0, rw_a_up, rw_g_up, rw_k_k, rw_k_a,
              rw_r_k, rw_ln_g, rw_ln_b, cf_dw, cf_dw_b, cf_ln_g, cf_ln_b, lru_conv_w, lru_conv_b, lru_wa,
              lru_ba, lru_wx, lru_bx, lru_lambda, w_branch, w_gate, b_gate, w_out, w_mlp1, w_mlp2):
    p = dict(ada_w=ada_w, ada_b=ada_b, norm_mix_g=norm_mix_g, norm_mlp_g=norm_mlp_g,
             norm_final_g=norm_final_g, w_in=w_in, hg_lower=hg_lower, hg_norm_g=hg_norm_g, rw_mu=rw_mu,
             rw_w0=rw_w0, rw_w_up=rw_w_up, rw_a0=rw_a0, rw_a_up=rw_a_up, rw_g_up=rw_g_up, rw_k_k=rw_k_k,
             rw_k_a=rw_k_a, rw_r_k=rw_r_k, rw_ln_g=rw_ln_g, rw_ln_b=rw_ln_b, cf_dw=cf_dw, cf_dw_b=cf_dw_b,
             cf_ln_g=cf_ln_g, cf_ln_b=cf_ln_b, lru_conv_w=lru_conv_w, lru_conv_b=lru_conv_b, lru_wa=lru_wa,
             lru_ba=lru_ba, lru_wx=lru_wx, lru_bx=lru_bx, lru_lambda=lru_lambda, w_branch=w_branch,
             w_gate=w_gate, b_gate=b_gate, w_out=w_out, w_mlp1=w_mlp1, w_mlp2=w_mlp2)
    dt = x_prompt.dtype
    bp = x_prompt.shape[0]
    zero_states = (
        jnp.zeros((DEPTH, bp, HG_HEADS, HG_DK, HG_DV), dt),
        jnp.zeros((DEPTH, bp, RW_HEADS, RW_HEAD, RW_HEAD), dt),
        jnp.zeros((DEPTH, bp, RW_COLS), dt),
        jnp.zeros((DEPTH, bp, CF_WIDTH - 1, BRANCH_W), dt),
        jnp.zeros((DEPTH, bp, BRANCH_W), dt),
        jnp.zeros((DEPTH, bp, LRU_CONV - 1, BRANCH_W), dt),
    )
    y_prompt, st_p = trunk(x_prompt, c_prompt, zero_states, p)
    sample_states = (state_hgrn, state_rwkv, state_rwkv_shift, state_conv, state_lru_h, state_lru_conv)
    y_sample, st_s = trunk(x_sample, c_sample, sample_states, p)
    hgrn_p, rwkv_p, shift_p, conv_p, lru_h_p, lru_conv_p = st_p
    hgrn_s, rwkv_s, shift_s, conv_s, lru_h_s, lru_conv_s = st_s
    return (y_prompt, y_sample, hgrn_p, rwkv_p, shift_p, conv_p, lru_h_p, lru_conv_p,
            hgrn_s, rwkv_s, shift_s, conv_s, lru_h_s, lru_conv_s)
```

```python
import contextlib
import numpy as np
import concourse.bass as bass
import concourse.mybir as mybir
from concourse.bass_utils import run_bass_kernel_spmd

F32 = mybir.dt.float32
BF16 = mybir.dt.bfloat16
AF = mybir.ActivationFunctionType
ALU = mybir.AluOpType
AX = mybir.AxisListType

ENGS = ("pe", "act", "dve", "pool", "sp")
D = 1024
W = 512
NCORE = 8
SEQ = 2048
NTILE = 4
NPT = 512
NSQ = 16
TS = 8
NSM = NSQ * TS
NTMAX = NPT + NSM
L = 2
IN_COLS = 5888
RW_COLS = 1792
CH = 64
EPS = 1e-6
DEBUG_SITES = False
SITES = {}


class Buf:
    __slots__ = ("name", "last_w", "readers")

    def __init__(self, name, readers=None):
        self.name = name
        self.last_w = None
        self.readers = list(readers) if readers else []


class Op:
    __slots__ = ("eng", "idx", "fn", "waits", "inc", "semval", "dma_sem", "dma_val", "clock", "site")

    def __init__(self, eng, idx, fn):
        self.eng = eng
        self.idx = idx
        self.fn = fn
        self.waits = []
        self.inc = False
        self.semval = 0
        self.dma_sem = None
        self.dma_val = 0
        self.clock = None


class Prog:
    def __init__(self, nc):
        self.nc = nc
        self.ops = {e: [] for e in ENGS}
        self.clock = {e: {} for e in ENGS}
        self.dma_sems = {}
        self.last_dma = {}

    @staticmethod
    def _key(prod):
        if prod.dma_sem is not None:
            return ("dma", prod.dma_sem), prod.dma_val
        return prod.eng, prod.idx

    def _record(self, eng, fn, reads, writes, dma_sem=None):
        lst = self.ops[eng]
        op = Op(eng, len(lst), fn)
        if DEBUG_SITES:
            import sys as _s
            f_ = _s._getframe(3)
            op.site = (f_.f_lineno, f_.f_back.f_lineno if f_.f_back else 0)
        cands = []
        for b in reads:
            if b.last_w is not None:
                cands.append(b.last_w)
        for b in writes:
            if b.last_w is not None:
                cands.append(b.last_w)
            cands.extend(b.readers)
        ck = self.clock[eng]
        best = {}
        if dma_sem is not None:
            prev = self.last_dma.get(dma_sem)
            if prev is not None and ck.get(("dma", dma_sem), -1) < prev.dma_val:
                best[("dma", dma_sem)] = prev
        for p in cands:
            if p.dma_sem is None and p.eng == "pe" and eng == "pe":
                continue
            k, v = self._key(p)
            if ck.get(k, -1) >= v:
                continue
            if k not in best or self._key(best[k])[1] < v:
                best[k] = p
        for k, p in best.items():
            ck[k] = self._key(p)[1]
            op.waits.append(p)
            if p.dma_sem is None:
                p.inc = True
            if p.clock is not None:
                for kk, vv in p.clock.items():
                    if ck.get(kk, -1) < vv:
                        ck[kk] = vv
        if dma_sem is not None:
            cnt = self.dma_sems.setdefault(dma_sem, [0])
            cnt[0] += 16
            op.dma_sem = dma_sem
            op.dma_val = cnt[0]
            self.last_dma[dma_sem] = op
        for b in reads:
            b.readers.append(op)
        for b in writes:
            b.last_w = op
            b.readers = []
        op.clock = dict(ck)
        lst.append(op)
        return op

    def op(self, eng, fn, reads=(), writes=()):
        return self._record(eng, fn, reads, writes)

    def dma(self, eng, fn, sem, reads=(), writes=()):
        return self._record(eng, fn, reads, writes, dma_sem=sem)

    def emit(self, final_bufs=()):
        nc = self.nc
        self._record("sp", None, list(final_bufs), [])
        with contextlib.ExitStack() as st:
            sems = {e: st.enter_context(nc.semaphore("s_" + e)) for e in ENGS}
            dsem = {n: st.enter_context(nc.semaphore("d_" + str(n))) for n in self.dma_sems}
            for e in ENGS:
                c = 0
                for op in self.ops[e]:
                    if op.dma_sem is None and op.inc:
                        c += 1
                        op.semval = c
            block = st.enter_context(nc.Block())

            def run(eng_name):
                def body(eng):
                    for op in self.ops[eng_name]:
                        for p in op.waits:
                            if p.dma_sem is not None:
                                eng.wait_ge(dsem[p.dma_sem], p.dma_val)
                            else:
                                eng.wait_ge(sems[p.eng], p.semval)
                        if op.fn is None:
                            continue
                        ins = op.fn(eng)
                        if DEBUG_SITES:
                            try:
                                SITES[ins.ins.name] = op.site
                            except Exception:
                                pass
                        if op.dma_sem is not None:
                            ins.then_inc(dsem[op.dma_sem], 16)
                        elif op.inc:
                            ins.then_inc(sems[eng_name], 1)
                return body

            block.tensor(run("pe"))
            block.scalar(run("act"))
            block.vector(run("dve"))
            block.gpsimd(run("pool"))
            block.sync(run("sp"))


class T:
    __slots__ = ("ap", "b")

    def __init__(self, ap, b):
        self.ap = ap
        self.b = b

    def __getitem__(self, k):
        return self.ap[k]


def build(debug=None, ntile=NTILE, nlayer=L, stage=9):
    NTILE = ntile
    SEQ = NTILE * NPT
    nc = bass.Bass("TRN2", target_bir_lowering=False)
    P = Prog(nc)
    st = contextlib.ExitStack()
    dbg_outs = {}

    def din(name, shape):
        return nc.dram_tensor(name, list(shape), F32, kind="ExternalInput").ap()

    def dout(name, shape):
        return nc.dram_tensor(name, list(shape), F32, kind="ExternalOutput").ap()

    xp = din("xp", [SEQ, D]); xs = din("xs", [NSM, D])
    s_hg = din("s_hg", [L, NSQ, 4, 128, 128]); s_rw = din("s_rw", [L, NSQ, 8, 64, 64])
    s_sh = din("s_sh", [L, NSQ, RW_COLS]); s_cf = din("s_cf", [L, NSQ, 30, W])
    s_lh = din("s_lh", [L, NSQ, W]); s_lc = din("s_lc", [L, NSQ, 3, W])
    cc = din("cc", [1 + NSQ, D])
    ada_w = din("ada_w", [L, D, 6 * D]); ada_b = din("ada_b", [L, 6 * D])
    norm_mix_g = din("norm_mix_g", [L, D]); norm_mlp_g = din("norm_mlp_g", [L, D]); norm_final_g = din("norm_final_g", [D])
    w_in = din("w_in", [L, D, IN_COLS])
    hg_lower = din("hg_lower", [L, W]); hg_norm_g = din("hg_norm_g", [L, W])
    rw_mu = din("rw_mu", [L, RW_COLS]); rw_w0 = din("rw_w0", [L, W]); rw_w_up = din("rw_w_up", [L, 64, W])
    rw_a0 = din("rw_a0", [L, W]); rw_a_up = din("rw_a_up", [L, 64, W]); rw_g_up = din("rw_g_up", [L, 128, W])
    rw_k_k = din("rw_k_k", [L, W]); rw_k_a = din("rw_k_a", [L, W]); rw_r_k = din("rw_r_k", [L, W])
    rw_ln_g = din("rw_ln_g", [L, W]); rw_ln_b = din("rw_ln_b", [L, W])
    cf_dw = din("cf_dw", [L, 31, W]); cf_dw_b = din("cf_dw_b", [L, W]); cf_ln_g = din("cf_ln_g", [L, W]); cf_ln_b = din("cf_ln_b", [L, W])
    lru_conv_w = din("lru_conv_w", [L, 4, W]); lru_conv_b = din("lru_conv_b", [L, W])
    lru_wa = din("lru_wa", [L, 8, 64, 64]); lru_ba = din("lru_ba", [L, W]); lru_wx = din("lru_wx", [L, 8, 64, 64]); lru_bx = din("lru_bx", [L, W])
    lru_lambda = din("lru_lambda", [L, W])
    w_branch = din("w_branch", [L, 4, W, D]); w_gate = din("w_gate", [L, D, 4 * D]); b_gate = din("b_gate", [L, 4 * D])
    w_out = din("w_out", [L, D, D]); w_mlp1 = din("w_mlp1", [L, D, 4 * D]); w_mlp2 = din("w_mlp2", [L, 4 * D, D])

    o_yp = dout("o_yp", [SEQ, D]); o_ys = dout("o_ys", [NSM, D])
    o_hg = dout("o_hg", [L, 1 + NSQ, 4, 128, 128]); o_rw = dout("o_rw", [L, 1 + NSQ, 8, 64, 64])
    o_sh = dout("o_sh", [L, 1 + NSQ, RW_COLS]); o_cf = dout("o_cf", [L, 1 + NSQ, 30, W])
    o_lh = dout("o_lh", [L, 1 + NSQ, W]); o_lc = dout("o_lc", [L, 1 + NSQ, 3, W])
    out_bufs = []

    def sb(name, shape, dt):
        return T(st.enter_context(nc.sbuf_tensor(name, list(shape), dt)), Buf(name))

    ARENA_E = 38 * 1024
    arena = st.enter_context(nc.sbuf_tensor("arena", [128, ARENA_E], BF16))
    ar = {"off": 0, "bufs": [], "fence": []}

    def aalloc(name, shape, dt, parts=128):
        n = 1
        for s_ in shape:
            n *= s_
        ne = n * (2 if dt == F32 else 1)
        ne = (ne + 15) // 16 * 16
        off = ar["off"]
        assert off + ne <= ARENA_E, (name, off, ne)
        ar["off"] = off + ne
        ar["hw"] = max(ar.get("hw", 0), off + ne)
        v = arena[0:parts, off:off + ne]
        if dt == F32:
            v = v.bitcast(F32)
        v = v[:, 0:n]
        if len(shape) == 1:
            v = v.rearrange("p (a b) -> p a b", a=1)
        elif len(shape) == 2:
            v = v.rearrange("p (a b) -> p a b", a=shape[0])
        elif len(shape) == 3:
            v = v.rearrange("p (a b c) -> p a b c", a=shape[0], b=shape[1])
        b = Buf(name, ar["fence"])
        ar["bufs"].append(b)
        return T(v, b)

    def areset():
        ops = []
        for b in ar["bufs"]:
            if b.last_w is not None:
                ops.append(b.last_w)
            ops.extend(b.readers)
        best = {}
        for o in ops + ar["fence"]:
            k, v = Prog._key(o)
            if k not in best or Prog._key(best[k])[1] < v:
                best[k] = o
        ar["fence"] = list(best.values())
        ar["bufs"] = []
        ar["off"] = 0

    psum = []
    for i in range(8):
        psum.append(T(st.enter_context(nc.psum_tensor("ps%d" % i, [128, 512], F32)), Buf("ps%d" % i)))
    pctr = [0]

    pheld = set()

    def PS(hold=False):
        assert len(pheld) < 8, 'all PSUM banks held'
        while (pctr[0] % 8) in pheld:
            pctr[0] += 1
        i = pctr[0] % 8
        pctr[0] += 1
        if hold:
            pheld.add(i)
        return psum[i]

    def PREL(t):
        pheld.discard(psum.index(t))

    def bl(ts_):
        return [t.b if isinstance(t, T) else t for t in ts_]

    def MM(out, lhsT, rhs, start, stop, R, Wr):
        P.op("pe", lambda e: e.matmul(out, lhsT=lhsT, rhs=rhs, start=start, stop=stop), bl(R), bl(Wr))

    def TR(out, in_, ident, R, Wr):
        P.op("pe", lambda e: e.transpose(out=out, in_=in_, identity=ident), bl(R), bl(Wr))

    def ACT(out, in_, func, R, Wr, scale=1.0, bias=0.0, eng="act"):
        P.op(eng, lambda e: e.activation(out=out, in_=in_, func=func, bias=bias, scale=scale), bl(R), bl(Wr))

    def TT(out, a, b, op, R, Wr, eng="dve"):
        P.op(eng, lambda e: e.tensor_tensor(out=out, in0=a, in1=b, op=op), bl(R), bl(Wr))

    def TSC(out, a, s1, s2, op0, op1, R, Wr, eng="dve"):
        if op1 is None:
            P.op(eng, lambda e: e.tensor_scalar(out=out, in0=a, scalar1=s1, scalar2=None, op0=op0), bl(R), bl(Wr))
        else:
            P.op(eng, lambda e: e.tensor_scalar(out=out, in0=a, scalar1=s1, scalar2=s2, op0=op0, op1=op1), bl(R), bl(Wr))

    def STT(out, a, s, b, op0, op1, R, Wr):
        P.op("dve", lambda e: e.scalar_tensor_tensor(out=out, in0=a, scalar=s, in1=b, op0=op0, op1=op1), bl(R), bl(Wr))

    def CP(out, in_, R, Wr, eng="dve"):
        if eng == "act":
            P.op("act", lambda e: e.copy(out=out, in_=in_), bl(R), bl(Wr))
        else:
            P.op(eng, lambda e: e.tensor_copy(out=out, in_=in_), bl(R), bl(Wr))

    def MSET(ap, val, Wr, eng="pool"):
        P.op(eng, lambda e: e.memset(ap, val), [], bl(Wr))

    def SCAN(out, d0, d1, init, R, Wr):
        P.op("dve", lambda e: e.tensor_tensor_scan(out=out, data0=d0, data1=d1, initial=init, op0=ALU.mult, op1=ALU.add), bl(R), bl(Wr))

    def RED(out, in_, R, Wr):
        P.op("dve", lambda e: e.tensor_reduce(out=out, in_=in_, axis=AX.X, op=ALU.add), bl(R), bl(Wr))

    def RCP(out, in_, R, Wr):
        P.op("dve", lambda e: e.reciprocal(out=out, in_=in_), bl(R), bl(Wr))

    dctr = [0]

    def DMA(out, in_, R, Wr, eng="sp", sem=None):
        if sem is None:
            sem = "g%d" % (dctr[0] % 12)
            dctr[0] += 1
        P.dma(eng, lambda e: e.dma_start(out=out, in_=in_), sem, bl(R), bl(Wr))

    def OUT(out, in_, R, name):
        b = Buf("out_" + name)
        out_bufs.append(b)
        sem = "o%d" % (dctr[0] % 8)
        dctr[0] += 1
        P.dma("sp", lambda e: e.dma_start(out=out, in_=in_), sem, bl(R), [b])

    def DBG(name, t, ap, shape):
        if debug is None or name not in debug:
            return
        d = dout("dbg_" + name, shape)
        dbg_outs[name] = shape
        OUT(d, ap, [t], "dbg_" + name)

    ident_f = sb("ident_f", [128, 128], F32)
    ident_b = sb("ident_b", [128, 128], BF16)
    ones_b = sb("ones_b", [128, 128], BF16)
    MSET(ident_f[:], 0.0, [ident_f])
    P.op("pool", lambda e: e.affine_select(out=ident_f[:], in_=ident_f[:], pattern=[[-1, 128]], compare_op=ALU.not_equal,
                                           fill=1.0, base=0, channel_multiplier=1), [ident_f.b], [ident_f.b])
    CP(ident_b[:], ident_f[:], [ident_f], [ident_b], eng="pool")
    MSET(ones_b[:], 1.0, [ones_b])
    mask_incl = sb("mask_incl", [64, 64], F32)
    mask_strict = sb("mask_strict", [64, 64], F32)
    for mt, base in ((mask_incl, 0), (mask_strict, -1)):
        MSET(mt[:], 1.0, [mt])
        P.op("pool", lambda e, mt=mt, base=base: e.affine_select(out=mt[:], in_=mt[:], pattern=[[1, 64]], compare_op=ALU.is_ge,
                                                                 fill=0.0, base=base, channel_multiplier=-1), [mt.b], [mt.b])
    cmask = sb("cmask", [128, NTMAX], BF16)
    MSET(cmask[:], 1.0, [cmask])
    MSET(cmask[:, 0:NPT].rearrange("p (c t) -> p c t", t=CH)[:, :, 0:1], 0.0, [cmask])
    MSET(cmask[:, NPT:NTMAX].rearrange("p (c t) -> p c t", t=TS)[:, :, 0:1], 0.0, [cmask])

    cmask_h = sb("cmask_h", [128, NTMAX], BF16)
    MSET(cmask_h[:], 1.0, [cmask_h])
    MSET(cmask_h[:, 0:NPT].rearrange("p (c t) -> p c t", t=32)[:, :, 0:1], 0.0, [cmask_h])
    MSET(cmask_h[:, NPT:NTMAX].rearrange("p (c t) -> p c t", t=TS)[:, :, 0:1], 0.0, [cmask_h])
    mask_incl_i = sb("mask_incl_i", [32, 8, 32], mybir.dt.uint8)
    CP(mask_incl_i[:], mask_incl[0:32, 0:32].unsqueeze(1).to_broadcast([32, 8, 32]), [mask_incl], [mask_incl_i], eng="pool")
    x = sb("x", [128, 8, NTMAX], F32)
    h = sb("h", [128, 8, NTMAX], BF16)
    y = sb("y", [128, 16, NTMAX], BF16)
    NSLOT = 4
    wslots = [sb("wslot%d" % i, [128, 4096], BF16) for i in range(NSLOT)]
    wctr = [0]
    mod = sb("mod", [128, L, 48, 1 + NSQ], F32)
    PCOLS = {}
    pc = [0]

    def pcol(name, n):
        PCOLS[name] = (pc[0], n)
        pc[0] += n

    for nm, n in (("ada_b", 48), ("norm_mix_g", 8), ("norm_mlp_g", 8), ("hg_lower", 4), ("hg_norm_g", 4), ("rw_mu", 14),
                  ("rw_w0", 4), ("rw_a0", 4), ("rw_k_k", 4), ("rw_k_a", 4), ("rw_r_k", 4), ("cf_dw", 124), ("cf_dw_b", 4),
                  ("cf_ln_g", 4), ("cf_ln_b", 4), ("lru_conv_w", 16), ("lru_conv_b", 4), ("lru_ba", 4), ("lru_bx", 4),
                  ("lru_lambda", 4), ("b_gate", 32), ("rw_ln_g", 4), ("rw_ln_b", 4)):
        pcol(nm, n)
    NPC = pc[0]
    ptab = sb("ptab", [128, L, NPC + 40], F32)
    DER = NPC

    def pv(l, name, j=0, n=1):
        o, _ = PCOLS[name]
        return ptab[:, l, o + j:o + j + n]

    def dv(l, j, n=1):
        return ptab[:, l, DER + j:DER + j + n]

    def WLOAD(src3, kc, cols):
        sl = wslots[wctr[0] % NSLOT]
        wctr[0] += 1
        v = sl.ap[:, 0:kc * cols].rearrange("p (k c) -> p k c", k=kc)
        P.dma("pool", lambda e: e.dma_start(out=v, in_=src3), "w%d" % ((wctr[0] - 1) % NSLOT), [], [sl.b])
        return T(v, sl.b)

    def kview(w2d, c0, cols):
        return w2d.rearrange("(k p) c -> p k c", p=128)[:, :, c0:c0 + cols]

    prow = [sb("prow%d" % i, [128, 128], F32) for i in range(3)]
    for l in range(L):
        plist = [("ada_b", ada_b[l]), ("norm_mix_g", norm_mix_g[l]), ("norm_mlp_g", norm_mlp_g[l]), ("hg_lower", hg_lower[l]),
                 ("hg_norm_g", hg_norm_g[l]), ("rw_mu", rw_mu[l]), ("rw_w0", rw_w0[l]), ("rw_a0", rw_a0[l]), ("rw_k_k", rw_k_k[l]),
                 ("rw_k_a", rw_k_a[l]), ("rw_r_k", rw_r_k[l]), ("cf_dw", cf_dw[l].rearrange("j c -> (j c)")), ("cf_dw_b", cf_dw_b[l]),
                 ("cf_ln_g", cf_ln_g[l]), ("cf_ln_b", cf_ln_b[l]), ("lru_conv_w", lru_conv_w[l].rearrange("j c -> (j c)")),
                 ("lru_conv_b", lru_conv_b[l]), ("lru_ba", lru_ba[l]), ("lru_bx", lru_bx[l]), ("lru_lambda", lru_lambda[l]),
                 ("b_gate", b_gate[l]), ("rw_ln_g", rw_ln_g[l]), ("rw_ln_b", rw_ln_b[l])]
        for nm, src in plist:
            o, n = PCOLS[nm]
            rows = src.rearrange("(r p) -> r p", p=128)
            r = 0
            while r < n:
                g = (o + r) // 128
                take = min(n - r, 128 - (o + r) % 128)
                DMA(prow[g][(o + r) % 128:(o + r) % 128 + take, :], rows[r:r + take, :], [], [prow[g]])
                r += take
        for g in range(3):
            nr = min(128, NPC - g * 128)
            ps = PS()
            TR(ps[:, 0:nr], prow[g][0:nr, :], ident_f[0:nr, 0:nr], [prow[g], ident_f], [ps])
            CP(ptab[:, l, g * 128:g * 128 + nr], ps[:, 0:nr], [ps], [ptab])
    for l in range(L):
        if l == 0:
            MSET(dv(0, 0, 4), 0.0, [ptab], eng="dve")
        else:
            TT(dv(l, 0, 4), pv(l, "hg_lower", 0, 4), pv(0, "hg_lower", 0, 4), ALU.subtract, [ptab], [ptab])
            ACT(dv(l, 0, 4), dv(l, 0, 4), AF.Sigmoid, [ptab], [ptab])
        TSC(dv(l, 4, 4), dv(l, 0, 4), -1.0, 1.0, ALU.mult, ALU.add, [ptab], [ptab])
        ACT(dv(l, 8, 4), pv(l, "lru_lambda", 0, 4), AF.Exp, [ptab], [ptab], scale=-1.0)
        ACT(dv(l, 8, 4), dv(l, 8, 4), AF.Ln, [ptab], [ptab], bias=1.0)
        TSC(dv(l, 12, 4), dv(l, 8, 4), -16.0, None, ALU.mult, None, [ptab], [ptab])
        TSC(dv(l, 8, 4), dv(l, 8, 4), -8.0, None, ALU.mult, None, [ptab], [ptab])
        TSC(dv(l, 16, 14), pv(l, "rw_mu", 0, 14), -1.0, 1.0, ALU.mult, ALU.add, [ptab], [ptab])
    gfin = sb("gfin", [128, D], F32)
    DMA(gfin[:], norm_final_g.rearrange("(o d) -> o d", o=1).to_broadcast([128, D]), [], [gfin])
    rwup = sb("rwup", [128, L, 3, W], BF16)
    lrug = sb("lrug", [128, L, 2, 4, 128], BF16)
    MSET(lrug[:], 0.0, [lrug])
    for l in range(L):
        DMA(rwup[0:64, l, 0, :], rw_w_up[l], [], [rwup], eng="pool", sem="rs")
        DMA(rwup[64:128, l, 1, :], rw_a_up[l], [], [rwup], eng="pool", sem="rs")
        DMA(rwup[:, l, 2, :], rw_g_up[l], [], [rwup], eng="pool", sem="rs")
        for gi, wsrc in enumerate((lru_wa, lru_wx)):
            for n in range(8):
                j, hf = n // 2, n % 2
                DMA(lrug[hf * 64:hf * 64 + 64, l, gi, j, hf * 64:hf * 64 + 64], wsrc[l, n], [], [lrug], eng="pool", sem="rs")

    ctok = aalloc("ctok", [D], F32, parts=1 + NSQ)
    cT = sb("cT", [128, 8, 1 + NSQ], BF16)
    DMA(ctok[:, 0, :], cc, [], [ctok])
    ps = PS()
    for kc in range(8):
        TR(ps[:, kc * 17:(kc + 1) * 17], ctok[:, 0, kc * 128:(kc + 1) * 128], ident_f[0:17, 0:17], [ctok, ident_f], [ps])
    ACT(cT[:].rearrange("p a b -> p (a b)"), ps[:, 0:136], AF.Silu, [ps], [cT])
    modA = sb("modA", [128, L, 2, 8, 1 + NSQ], F32)
    ada_state = {l: 0 for l in range(L)}

    def ADA_PIECES(l, n):
        while n > 0 and ada_state[l] < 12:
            g = ada_state[l]
            ada_state[l] += 1
            n -= 1
            wt = WLOAD(kview(ada_w[l], g * 512, 512), 8, 512)
            ps = PS()
            for j in range(4):
                for kc in range(8):
                    MM(ps[:, j * 17:(j + 1) * 17], wt[:, kc, j * 128:(j + 1) * 128], cT[:, kc, :], kc == 0, kc == 7, [wt, cT], [ps])
            for j in range(4):
                fc = g * 4 + j
                ACT(mod[:, l, fc, :], ps[:, j * 17:(j + 1) * 17], AF.Identity, [ps, ptab], [mod], bias=pv(l, "ada_b", fc))
            if ada_state[l] == 12:
                for which, gname, sc0 in ((0, "norm_mix_g", 8), (1, "norm_mlp_g", 32)):
                    for kc in range(8):
                        TSC(modA[:, l, which, kc, :], mod[:, l, sc0 + kc, :], 1.0, pv(l, gname, kc), ALU.add, ALU.mult, [mod, ptab], [modA])

    ADA_PIECES(0, 12)
    areset()

    hgS = sb("hgS", [128, L, 4, 128], F32)
    rwST = sb("rwST", [128, L, 4, 64], F32)
    rwprev = sb("rwprev", [128, L, 14], F32)
    cfhist = sb("cfhist", [128, L, 4, 30], BF16)
    lruh = sb("lruh", [128, L, 4], F32)
    lruhist = sb("lruhist", [128, L, 4, 3], F32)
    for t_ in (hgS, rwST, rwprev, cfhist, lruh, lruhist):
        MSET(t_[:], 0.0, [t_], eng="dve")
    shcol = sb("shcol", [128, L, 14, 1 + NSQ], F32)
    lhcol = sb("lhcol", [128, L, 4, 1 + NSQ], F32)

    def slabs(NT):
        return [(0, NPT)] + ([(NPT, NSM)] if NT > NPT else [])

    def NORM(l, which, NT):
        sh0 = 0 if which == 0 else 24
        for (t0, n) in slabs(NT):
            sq = aalloc("sq", [8, n], BF16)
            ACT(sq[:], x[:, :, t0:t0 + n], AF.Square, [x], [sq])
            ps = PS()
            for kc in range(8):
                MM(ps[:, 0:n], ones_b[:], sq[:, kc, :], kc == 0, kc == 7, [ones_b, sq], [ps])
            rstd = aalloc("rstd", [n], F32)
            ACT(rstd[:, 0, :], ps[:, 0:n], AF.Ln, [ps], [rstd], scale=1.0 / D, bias=EPS)
            ACT(rstd[:, 0, :], rstd[:, 0, :], AF.Exp, [rstd], [rstd], scale=-0.5)
            if t0 == 0:
                for kc in range(8):
                    xk = aalloc("xn%d" % kc, [n], F32)
                    TT(xk[:, 0, :], x[:, kc, t0:t0 + n], rstd[:, 0, :], ALU.mult, [x, rstd], [xk])
                    ACT(h[:, kc, 0:n], xk[:, 0, :], AF.Identity, [xk, modA, mod], [h],
                        scale=modA[:, l, which, kc, 0:1], bias=mod[:, l, sh0 + kc, 0:1])
                continue
            xn = aalloc("xn", [8, n], F32)
            TT(xn[:], x[:, :, t0:t0 + n], rstd[:, 0:1, :].to_broadcast([128, 8, n]), ALU.mult, [x, rstd], [xn])
            for kc in range(8):
                if t0 == 0:
                    TSC(h[:, kc, 0:n], xn[:, kc, :], modA[:, l, which, kc, 0:1], mod[:, l, sh0 + kc, 0:1], ALU.mult, ALU.add,
                        [xn, modA, mod], [h])
                else:
                    v3 = xn[:, kc, :].rearrange("p (q t) -> p q t", t=TS)
                    TT(v3, v3, modA[:, l, which, kc, 1:1 + NSQ].unsqueeze(2).to_broadcast([128, NSQ, TS]), ALU.mult, [xn, modA], [xn])
                    TT(h[:, kc, t0:t0 + n].rearrange("p (q t) -> p q t", t=TS), v3,
                       mod[:, l, sh0 + kc, 1:1 + NSQ].unsqueeze(2).to_broadcast([128, NSQ, TS]), ALU.add, [xn, mod], [h])

    def RESID(l, g0, dc, psb, NT):
        for (t0, n), (ps, c0) in zip(slabs(NT), psb):
            if t0 == 0:
                STT(x[:, dc, 0:n], ps[:, c0:c0 + n], mod[:, l, g0 + dc, 0:1], x[:, dc, 0:n], ALU.mult, ALU.add, [ps, mod, x], [x])
            else:
                tmp = aalloc("rtmp", [n], F32)
                TT(tmp[:, 0, :].rearrange("p (q t) -> p q t", t=TS), ps[:, c0:c0 + n].rearrange("p (q t) -> p q t", t=TS),
                   mod[:, l, g0 + dc, 1:1 + NSQ].unsqueeze(2).to_broadcast([128, NSQ, TS]), ALU.mult, [ps, mod], [tmp])
                TT(x[:, dc, t0:t0 + n], x[:, dc, t0:t0 + n], tmp[:, 0, :], ALU.add, [x, tmp], [x])

    def PROJ(wt, j, NT, rhs_t, nk, ps, c0=0):
        for kc in range(nk):
            MM(ps[:, c0:c0 + NT], wt[:, kc, j * 128:(j + 1) * 128], rhs_t[:, kc, 0:NT], kc == 0, kc == nk - 1, [wt, rhs_t], [ps])

    def MERGE_MLP(l, NT):
        sl = slabs(NT)
        mg = aalloc("mg", [8, NT], F32)
        mgb = aalloc("mgb", [8, NT], BF16)
        for b in range(4):
            wb = WLOAD(w_branch[l, b].rearrange("(k p) c -> p k c", p=128), 4, D)
            for half in range(2):
                wg = WLOAD(kview(w_gate[l], b * D + half * 512, 512), 8, 512)
                for jj in range(4):
                    dc = half * 4 + jj
                    for (t0, n) in sl:
                        pb = PS()
                        for kc in range(4):
                            MM(pb[:, 0:n], wb[:, kc, dc * 128:(dc + 1) * 128], y[:, b * 4 + kc, t0:t0 + n], kc == 0, kc == 3, [wb, y], [pb])
                        pg = PS()
                        for kc in range(8):
                            MM(pg[:, 0:n], wg[:, kc, jj * 128:(jj + 1) * 128], h[:, kc, t0:t0 + n], kc == 0, kc == 7, [wg, h], [pg])
                        sg = aalloc("sg", [n], F32) if False else None
                        sgt = sgbuf
                        ACT(sgt[:, 0:n], pg[:, 0:n], AF.Sigmoid, [pg, ptab], [sgt], bias=pv(l, "b_gate", b * 8 + dc))
                        if b == 0:
                            TT(mg[:, dc, t0:t0 + n], sgt[:, 0:n], pb[:, 0:n], ALU.mult, [sgt, pb], [mg])
                        else:
                            TT(sgt[:, 0:n], sgt[:, 0:n], pb[:, 0:n], ALU.mult, [sgt, pb], [sgt])
                            if b < 3:
                                TT(mg[:, dc, t0:t0 + n], mg[:, dc, t0:t0 + n], sgt[:, 0:n], ALU.add, [mg, sgt], [mg])
                            else:
                                TT(mgb[:, dc, t0:t0 + n], mg[:, dc, t0:t0 + n], sgt[:, 0:n], ALU.add, [mg, sgt], [mgb])
        for half in range(2):
            wo = WLOAD(kview(w_out[l], half * 512, 512), 8, 512)
            for jj in range(4):
                dc = half * 4 + jj
                psb = []
                for (t0, n) in sl:
                    po = PS()
                    for kc in range(8):
                        MM(po[:, 0:n], wo[:, kc, jj * 128:(jj + 1) * 128], mgb[:, kc, t0:t0 + n], kc == 0, kc == 7, [wo, mgb], [po])
                    psb.append((po, 0))
                RESID(l, 16, dc, psb, NT)
        areset()
        NORM(l, 1, NT)
        areset()
        hid = aalloc("hid", [32, NT], BF16)
        for pi in range(8):
            w1 = WLOAD(kview(w_mlp1[l], pi * 512, 512), 8, 512)
            for jj in range(4):
                for (t0, n) in sl:
                    pp = PS()
                    for kc in range(8):
                        MM(pp[:, 0:n], w1[:, kc, jj * 128:(jj + 1) * 128], h[:, kc, t0:t0 + n], kc == 0, kc == 7, [w1, h], [pp])
                    ACT(sgbuf[:, 0:n], pp[:, 0:n], AF.Relu, [pp], [sgbuf])
                    TT(hid[:, pi * 4 + jj, t0:t0 + n], sgbuf[:, 0:n], sgbuf[:, 0:n], ALU.mult, [sgbuf], [hid])
        for cb in range(2):
            for kb in range(4):
                w2 = WLOAD(w_mlp2[l].rearrange("(k p) c -> p k c", p=128)[:, kb * 8:(kb + 1) * 8, cb * 512:(cb + 1) * 512], 8, 512)
                for jj in range(4):
                    for kc in range(8):
                        first = (kb == 0 and kc == 0)
                        last = (kb == 3 and kc == 7)
                        MM(psum[jj][:, 0:NPT], w2[:, kc, jj * 128:(jj + 1) * 128], hid[:, kb * 8 + kc, 0:NPT], first, last, [w2, hid], [psum[jj]])
                        if NT > NPT:
                            MM(psum[4 + jj][:, 0:NSM], w2[:, kc, jj * 128:(jj + 1) * 128], hid[:, kb * 8 + kc, NPT:NT], first, last,
                               [w2, hid], [psum[4 + jj]])
            for jj in range(4):
                RESID(l, 40, cb * 4 + jj, [(psum[jj], 0), (psum[4 + jj], 0)], NT)
        pctr[0] = 0
        areset()

    sgbuf = sb("sgbuf", [128, NPT], F32)

    xtmp = sb("xtmp", [128, NSM], F32)
    sgd2 = sb("sgd2", [128, NTMAX], BF16)

    def arelease(mark):
        ops = []
        for b in ar["bufs"]:
            if b.last_w is not None:
                ops.append(b.last_w)
            ops.extend(b.readers)
        best = {}
        for o in ops + ar["fence"]:
            k, v = Prog._key(o)
            if k not in best or Prog._key(best[k])[1] < v:
                best[k] = o
        ar["fence"] = list(best.values())
        ar["off"] = mark

    diag = [sb("diag%d" % i, [128, 128], BF16) for i in range(6)]
    dgc = [0]
    GC = 0.7978845608028654

    def chunks_of(NT, CC=CH):
        lst = [(c * CC, CC, "p", c) for c in range(NPT // CC)]
        if NT > NPT:
            lst += [(NPT + q * TS, TS, "s", q) for q in range(NSQ)]
        return lst

    def bview(ap2, n, C):
        return ap2.rearrange("p (c t) -> p c t", t=C)

    def TOK_OUT(src_t, src_ap, ncols, dst_rows, nm):
        ps = PS()
        for j in range(4):
            TR(ps[0:ncols, j * 128:(j + 1) * 128], src_ap(j), ident_f[:], [src_t[j] if isinstance(src_t, list) else src_t, ident_f], [ps])
        stg = aalloc("stg_" + nm, [W], F32)
        CP(stg[0:ncols, 0, :], ps[0:ncols, :], [ps], [stg], eng="act")
        for (r0, r1, dst) in dst_rows:
            OUT(dst, stg[r0:r1, 0, :], [stg], nm)

    def MIX_CF(ti, l, NT):
        last = ti == NTILE - 1
        sl = slabs(NT)
        wv = WLOAD(kview(w_in[l], 3840, 512), 8, 512)
        wg = WLOAD(kview(w_in[l], 4352, 512), 8, 512)
        u32 = aalloc("u32", [4, NT], F32)
        uxp = aalloc("uxp", [4, 30 + NPT], BF16)
        uxs = aalloc("uxs", [4, NSQ * 38], BF16) if last else None
        if last:
            for g in range(4):
                hst = aalloc("hst%d" % g, [W], F32, parts=120)
                DMA(hst[:, 0, :], s_cf[l, g * 4:(g + 1) * 4].rearrange("q r c -> (q r) c"), [], [hst])
                ps = PS()
                for j in range(4):
                    TR(ps[:, j * 120:(j + 1) * 120], hst[:, 0, j * 128:(j + 1) * 128], ident_f[0:120, 0:120], [hst, ident_f], [ps])
                for j in range(4):
                    CP(bview(uxs[:, j, :], NSQ, 38)[:, g * 4:(g + 1) * 4, 0:30], bview(ps[:, j * 120:(j + 1) * 120], 4, 30), [ps], [uxs],
                       eng=("act" if j % 2 else "dve"))
            OUT(o_cf[l, 1:1 + NSQ, 0:22, :], s_cf[l, :, 8:30, :], [], "cfcopy")
        for j in range(4):
            CP(uxp[:, j, 0:30], cfhist[:, l, j, :], [cfhist], [uxp])
            for (t0, n) in sl:
                p1 = PS(); p2 = PS()
                for kc in range(8):
                    MM(p1[:, 0:n], wv[:, kc, j * 128:(j + 1) * 128], h[:, kc, t0:t0 + n], kc == 0, kc == 7, [wv, h], [p1])
                for kc in range(8):
                    MM(p2[:, 0:n], wg[:, kc, j * 128:(j + 1) * 128], h[:, kc, t0:t0 + n], kc == 0, kc == 7, [wg, h], [p2])
                ACT(sgbuf[:, 0:n], p2[:, 0:n], AF.Sigmoid, [p2], [sgbuf])
                TT(u32[:, j, t0:t0 + n], sgbuf[:, 0:n], p1[:, 0:n], ALU.mult, [sgbuf, p1], [u32])
                if t0 == 0:
                    CP(uxp[:, j, 30:30 + NPT], u32[:, j, 0:NPT], [u32], [uxp], eng="act")
                else:
                    CP(bview(uxs[:, j, :], NSQ, 38)[:, :, 30:38], bview(u32[:, j, NPT:NT], NSQ, TS), [u32], [uxs], eng="act")
            CP(cfhist[:, l, j, :], uxp[:, j, NPT:NPT + 30], [uxp], [cfhist])
        yc = aalloc("yc", [4, NT], F32)
        ycb = aalloc("ycb", [4, NT], BF16)
        ycs = aalloc("ycs", [4, NT], BF16)
        for j in range(4):
            p1 = PS(); p2 = PS() if last else None
            for tap in range(31):
                dg = diag[dgc[0] % 6]; dgc[0] += 1
                ACT(dg[:], ident_b[:], AF.Identity, [ident_b, ptab], [dg], scale=pv(l, "cf_dw", tap * 4 + j))
                MM(p1[:, 0:NPT], dg[:], uxp[:, j, tap:tap + NPT], tap == 0, tap == 30, [dg, uxp], [p1])
                if last:
                    MM(p2[:, 0:NSM], dg[:], bview(uxs[:, j, :], NSQ, 38)[:, :, tap:tap + TS], tap == 0, tap == 30, [dg, uxs], [p2])
            for (t0, n), pp in zip(sl, (p1, p2)):
                ACT(yc[:, j, t0:t0 + n], pp[:, 0:n], AF.Identity, [pp, ptab], [yc], bias=pv(l, "cf_dw_b", j))
                CP(ycb[:, j, t0:t0 + n], yc[:, j, t0:t0 + n], [yc], [ycb])
                ACT(ycs[:, j, t0:t0 + n], yc[:, j, t0:t0 + n], AF.Square, [yc], [ycs])
        for (t0, n) in sl:
            pm = PS(); pq = PS()
            for j in range(4):
                MM(pm[:, 0:n], ones_b[:], ycb[:, j, t0:t0 + n], j == 0, j == 3, [ones_b, ycb], [pm])
            for j in range(4):
                MM(pq[:, 0:n], ones_b[:], ycs[:, j, t0:t0 + n], j == 0, j == 3, [ones_b, ycs], [pq])
            mean = aalloc("cfmean", [n], F32)
            var = aalloc("cfvar", [n], F32)
            ACT(mean[:, 0, :], pm[:, 0:n], AF.Copy, [pm], [mean], scale=1.0 / W)
            TT(var[:, 0, :], mean[:, 0, :], mean[:, 0, :], ALU.mult, [mean], [var])
            STT(var[:, 0, :], pq[:, 0:n], 1.0 / W, var[:, 0, :], ALU.mult, ALU.subtract, [pq, var], [var])
            ACT(var[:, 0, :], var[:, 0, :], AF.Ln, [var], [var], bias=1e-5)
            ACT(var[:, 0, :], var[:, 0, :], AF.Exp, [var], [var], scale=-0.5)
            for j in range(4):
                TT(yc[:, j, t0:t0 + n], yc[:, j, t0:t0 + n], mean[:, 0, :], ALU.subtract, [yc, mean], [yc])
                TT(yc[:, j, t0:t0 + n], yc[:, j, t0:t0 + n], var[:, 0, :], ALU.mult, [yc, var], [yc])
                ACT(y[:, 8 + j, t0:t0 + n], yc[:, j, t0:t0 + n], AF.Silu, [yc, ptab], [y], scale=pv(l, "cf_ln_g", j), bias=pv(l, "cf_ln_b", j))
        if last:
            TOK_OUT(u32, lambda j: u32[:, j, NPT - 30:NPT], 30, [(0, 30, o_cf[l, 0])], "cfp")
            TOK_OUT(u32, lambda j: u32[:, j, NPT:NT], NSM, [(q * TS, (q + 1) * TS, o_cf[l, 1 + q, 22:30, :]) for q in range(NSQ)], "cfs")

    def MIX_LRU(ti, l, NT):
        last = ti == NTILE - 1
        sl = slabs(NT)
        wx_ = WLOAD(kview(w_in[l], 4864, 512), 8, 512)
        wgl = WLOAD(kview(w_in[l], 5376, 512), 8, 512)
        xl32 = [aalloc("xl32_%d" % j, [NT], F32) for j in range(4)]
        exp_ = [aalloc("lext%d" % j, [3 + NPT], F32) for j in range(4)]
        exs = aalloc("lexs", [4, NSQ * 11], F32) if last else None
        xc = [aalloc("lxc%d" % j, [NT], F32) for j in range(4)]
        xcb = [aalloc("lxcb%d" % j, [NT], BF16) for j in range(4)]
        hs = [aalloc("lhs%d" % j, [NT], F32) for j in range(4)]
        tsets = [(aalloc("lta%d" % i, [NT], F32), aalloc("ltb%d" % i, [NT], F32), aalloc("ltc%d" % i, [NT], F32)) for i in range(1 if last else 2)]
        hs0 = aalloc("lhs0", [4, NSQ], F32) if last else None
        if last:
            hst = aalloc("lhst", [W], F32, parts=48)
            DMA(hst[:, 0, :], s_lc[l].rearrange("q r c -> (q r) c"), [], [hst])
            ps = PS()
            for j in range(4):
                TR(ps[:, j * 48:(j + 1) * 48], hst[:, 0, j * 128:(j + 1) * 128], ident_f[0:48, 0:48], [hst, ident_f], [ps])
            for j in range(4):
                CP(bview(exs[:, j, :], NSQ, 11)[:, :, 0:3], bview(ps[:, j * 48:(j + 1) * 48], NSQ, 3), [ps], [exs])
            hh = aalloc("lhh", [W], F32, parts=NSQ)
            DMA(hh[:, 0, :], s_lh[l], [], [hh])
            ps = PS()
            for j in range(4):
                TR(ps[:, j * 16:(j + 1) * 16], hh[:, 0, j * 128:(j + 1) * 128], ident_f[0:16, 0:16], [hh, ident_f], [ps])
            CP(hs0[:].rearrange("p a b -> p (a b)"), ps[:, 0:64], [ps], [hs0])
        def LJ(j, ta, tb, tcc):
            CP(exp_[j][:, 0, 0:3], lruhist[:, l, j, :], [lruhist], [exp_[j]])
            for (t0, n) in sl:
                p1 = PS()
                for kc in range(8):
                    MM(p1[:, 0:n], wx_[:, kc, j * 128:(j + 1) * 128], h[:, kc, t0:t0 + n], kc == 0, kc == 7, [wx_, h], [p1])
                CP(xl32[j][:, 0, t0:t0 + n], p1[:, 0:n], [p1], [xl32[j]], eng="act")
                if t0 == 0:
                    CP(exp_[j][:, 0, 3:3 + NPT], xl32[j][:, 0, 0:NPT], [xl32[j]], [exp_[j]])
                    src = lambda tap: exp_[j][:, 0, tap:tap + NPT]
                    dst = xc[j][:, 0, 0:NPT]
                    rd = exp_[j]
                else:
                    CP(bview(exs[:, j, :], NSQ, 11)[:, :, 3:11], bview(xl32[j][:, 0, NPT:NT], NSQ, TS), [xl32[j]], [exs])
                    src = lambda tap: bview(exs[:, j, :], NSQ, 11)[:, :, tap:tap + TS]
                    dst = bview(xc[j][:, 0, NPT:NT], NSQ, TS)
                    rd = exs
                TSC(dst, src(0), pv(l, "lru_conv_w", 0 * 4 + j), pv(l, "lru_conv_b", j), ALU.mult, ALU.add, [rd, ptab], [xc[j]])
                for tap in range(1, 4):
                    STT(dst, src(tap), pv(l, "lru_conv_w", tap * 4 + j), dst, ALU.mult, ALU.add, [rd, ptab, xc[j]], [xc[j]])
            CP(lruhist[:, l, j, :], exp_[j][:, 0, NPT:NPT + 3], [exp_[j]], [lruhist])
            yield
            CP(xcb[j][:, 0, 0:NT], xc[j][:, 0, 0:NT], [xc[j]], [xcb[j]], eng="act")
            for (t0, n) in sl:
                pa = PS(); px = PS(); pgl = PS()
                MM(pa[:, 0:n], lrug[:, l, 0, j, :], xcb[j][:, 0, t0:t0 + n], True, True, [lrug, xcb[j]], [pa])
                MM(px[:, 0:n], lrug[:, l, 1, j, :], xcb[j][:, 0, t0:t0 + n], True, True, [lrug, xcb[j]], [px])
                for kc in range(8):
                    MM(pgl[:, 0:n], wgl[:, kc, j * 128:(j + 1) * 128], h[:, kc, t0:t0 + n], kc == 0, kc == 7, [wgl, h], [pgl])
                A_ = ta[:, 0, t0:t0 + n]; B_ = tb[:, 0, t0:t0 + n]; C_ = tcc[:, 0, t0:t0 + n]
                yield
                ACT(C_, pa[:, 0:n], AF.Sigmoid, [pa, ptab], [tcc], bias=pv(l, "lru_ba", j))
                ACT(A_, C_, AF.Exp, [tcc, ptab], [ta], scale=dv(l, 8 + j))
                ACT(C_, C_, AF.Exp, [tcc, ptab], [tcc], scale=dv(l, 12 + j))
                ACT(C_, C_, AF.Sqrt, [tcc], [tcc], scale=-1.0, bias=1.0)
                ACT(B_, px[:, 0:n], AF.Sigmoid, [px, ptab], [tb], bias=pv(l, "lru_bx", j))
                yield
                TT(B_, B_, xc[j][:, 0, t0:t0 + n], ALU.mult, [tb, xc[j]], [tb])
                TT(B_, B_, C_, ALU.mult, [tb, tcc], [tb])
                if t0 == 0:
                    SCAN(hs[j][:, 0, 0:n], A_, B_, lruh[:, l, j:j + 1], [ta, tb, lruh], [hs[j]])
                    CP(lruh[:, l, j:j + 1], hs[j][:, 0, n - 1:n], [hs[j]], [lruh])
                else:
                    a3 = bview(A_, NSQ, TS); b3 = bview(B_, NSQ, TS)
                    TT(C_[:, 0:NSQ], a3[:, :, 0], hs0[:, j, :], ALU.mult, [ta, hs0], [tcc])
                    TT(b3[:, :, 0], b3[:, :, 0], C_[:, 0:NSQ], ALU.add, [tb, tcc], [tb])
                    MSET(a3[:, :, 0], 0.0, [ta], eng="dve")
                    SCAN(hs[j][:, 0, t0:t0 + n], A_, B_, 0.0, [ta, tb], [hs[j]])
                yield
                ACT(A_, pgl[:, 0:n], AF.Copy, [pgl], [ta])
                TT(B_, A_, A_, ALU.mult, [ta], [tb])
                TSC(B_, B_, 2.0 * GC * 0.044715, 2.0 * GC, ALU.mult, ALU.add, [tb], [tb])
                TT(B_, B_, A_, ALU.mult, [tb, ta], [tb])
                ACT(B_, B_, AF.Sigmoid, [tb], [tb])
                TT(B_, B_, A_, ALU.mult, [tb, ta], [tb])
                TT(y[:, 12 + j, t0:t0 + n], hs[j][:, 0, t0:t0 + n], B_, ALU.mult, [hs[j], tb], [y])
            if last:
                CP(lhcol[:, l, j, 0:1], hs[j][:, 0, NPT - 1:NPT], [hs[j]], [lhcol])
                CP(lhcol[:, l, j, 1:1 + NSQ], bview(hs[j][:, 0, NPT:NT], NSQ, TS)[:, :, TS - 1], [hs[j]], [lhcol])
        if len(tsets) == 2:
            for (ja, jb) in ((0, 1), (2, 3)):
                gens = [LJ(ja, *tsets[0]), LJ(jb, *tsets[1])]
                alive = [True, True]
                while any(alive):
                    for gi in range(2):
                        if alive[gi]:
                            try:
                                next(gens[gi])
                            except StopIteration:
                                alive[gi] = False
        else:
            for j in range(4):
                for _ in LJ(j, *tsets[0]):
                    pass
        if last:
            TOK_OUT(xl32, lambda j: xl32[j][:, 0, NPT - 3:NPT], 3, [(0, 3, o_lc[l, 0])], "lcp")
            TOK_OUT(xl32, lambda j: xl32[j][:, 0, NPT:NT], NSM, [(q * TS + 5, q * TS + 8, o_lc[l, 1 + q]) for q in range(NSQ)], "lcs")
    def MIX_HG(ti, l, NT):
        last = ti == NTILE - 1
        sl = slabs(NT)
        HC = 32
        chs = chunks_of(NT, HC)
        ncp = NPT // HC
        nch = len(chs)
        wq = WLOAD(kview(w_in[l], 0, 512), 8, 512)
        wf = WLOAD(kview(w_in[l], 512, 512), 8, 512)
        wi = WLOAD(kview(w_in[l], 1024, 512), 8, 512)
        wo_ = WLOAD(kview(w_in[l], 1536, 512), 8, 512)
        t1 = aalloc("hg_t1", [NT], F32); t2 = aalloc("hg_t2", [NT], F32); t3 = aalloc("hg_t3", [NT], F32)
        t4 = aalloc("hg_t4", [NT], F32)
        qbc = aalloc("hg_qbc", [NT], BF16); kbc = aalloc("hg_kbc", [NT], BF16); kdc = aalloc("hg_kdc", [NT], BF16)
        st_ = aalloc("hg_st", [4, nch], F32)
        vtok = aalloc("hg_vtok", [nch, 128], BF16, parts=HC)
        kdtok = aalloc("hg_kdtok", [nch, 128], BF16, parts=HC)
        scm = aalloc("hg_scm", [nch, HC], BF16, parts=HC)
        MSET(scm[:], 0.0, [scm], eng="dve")
        Sall = aalloc("hg_Sall", [17, 128], F32)
        Sbf = aalloc("hg_Sbf", [16, 128], BF16)
        pSsb = aalloc("hg_pSsb", [16, 128], F32)
        o32 = aalloc("hg_o32", [NT], F32)
        osq = aalloc("hg_osq", [NT], BF16)
        def emit_proj(hd):
            lst = []
            for (t0, n) in sl:
                p1 = PS(hold=True); p2 = PS(hold=True)
                for kc in range(8):
                    MM(p1[:, 0:n], wq[:, kc, hd * 128:(hd + 1) * 128], h[:, kc, t0:t0 + n], kc == 0, kc == 7, [wq, h], [p1])
                for kc in range(8):
                    MM(p2[:, 0:n], wf[:, kc, hd * 128:(hd + 1) * 128], h[:, kc, t0:t0 + n], kc == 0, kc == 7, [wf, h], [p2])
                lst.append((p1, p2, t0, n))
            return lst
        pend_proj = emit_proj(0)
        for hd in range(4):
            A_ = t1[:, 0, :]; B_ = t2[:, 0, :]; C_ = t3[:, 0, :]; D_ = t4[:, 0, :]
            for (p1, p2, t0, n) in pend_proj:
                ACT(A_[:, t0:t0 + n], p1[:, 0:n], AF.Silu, [p1], [t1])
                PREL(p1)
                ACT(B_[:, t0:t0 + n], p2[:, 0:n], AF.Sigmoid, [p2], [t2])
                PREL(p2)
            TSC(B_[:, 0:NT], B_[:, 0:NT], dv(l, 4 + hd), dv(l, hd), ALU.mult, ALU.add, [t2, ptab], [t2])
            ACT(C_[:, 0:NT], B_[:, 0:NT], AF.Ln, [t2], [t3])
            TSC(B_[:, 0:NT], B_[:, 0:NT], -1.0, 1.0, ALU.mult, ALU.add, [t2], [t2])
            SCAN(D_[:, 0:NT], cmask_h[:, 0:NT], C_[:, 0:NT], 0.0, [cmask_h, t3], [t4])
            bp = bview(D_[:, 0:NPT], ncp, HC)
            CP(st_[:, 0, 0:ncp], bp[:, :, HC // 2 - 1], [t4], [st_])
            CP(st_[:, 1, 0:ncp], bp[:, :, HC - 1], [t4], [st_])
            if last:
                bs_ = bview(D_[:, NPT:NT], NSQ, TS)
                CP(st_[:, 0, ncp:nch], bs_[:, :, TS // 2 - 1], [t4], [st_])
                CP(st_[:, 1, ncp:nch], bs_[:, :, TS - 1], [t4], [st_])
            TT(st_[:, 3, :], st_[:, 1, :], st_[:, 0, :], ALU.subtract, [st_], [st_])
            ACT(st_[:, 1:4, :], st_[:, 1:4, :], AF.Exp, [st_], [st_]) if False else None
            ACT(st_[:, 2, :], st_[:, 0, :], AF.Exp, [st_], [st_])
            ACT(st_[:, 1, :], st_[:, 1, :], AF.Exp, [st_], [st_])
            ACT(st_[:, 3, :], st_[:, 3, :], AF.Exp, [st_], [st_])
            TT(bp, bp, st_[:, 0, 0:ncp].unsqueeze(2).to_broadcast([128, ncp, HC]), ALU.subtract, [t4, st_], [t4])
            if last:
                TT(bs_, bs_, st_[:, 0, ncp:nch].unsqueeze(2).to_broadcast([128, NSQ, TS]), ALU.subtract, [t4, st_], [t4])
            ACT(C_[:, 0:NT], D_[:, 0:NT], AF.Exp, [t4], [t3])
            TT(qbc[:, 0, 0:NT], A_[:, 0:NT], C_[:, 0:NT], ALU.mult, [t1, t3], [qbc])
            ACT(C_[:, 0:NT], D_[:, 0:NT], AF.Exp, [t4], [t3], scale=-1.0)
            TT(kbc[:, 0, 0:NT], B_[:, 0:NT], C_[:, 0:NT], ALU.mult, [t2, t3], [kbc])
            TT(bview(kdc[:, 0, 0:NPT], ncp, HC), bview(kbc[:, 0, 0:NPT], ncp, HC),
               st_[:, 3, 0:ncp].unsqueeze(2).to_broadcast([128, ncp, HC]), ALU.mult, [kbc, st_], [kdc])
            if last:
                TT(bview(kdc[:, 0, NPT:NT], NSQ, TS), bview(kbc[:, 0, NPT:NT], NSQ, TS),
                   st_[:, 3, ncp:nch].unsqueeze(2).to_broadcast([128, NSQ, TS]), ALU.mult, [kbc, st_], [kdc])
            for g0 in range(0, nch, 4):
                grp = chs[g0:g0 + 4]
                pv_ = PS()
                for gi, (t0, C, kind, ci) in enumerate(grp):
                    for kc in range(8):
                        MM(pv_[0:C, gi * 128:(gi + 1) * 128], h[:, kc, t0:t0 + C], wi[:, kc, hd * 128:(hd + 1) * 128], kc == 0, kc == 7, [h, wi], [pv_])
                C = grp[0][1]
                CP(vtok[0:C, g0:g0 + len(grp), :], bview(pv_[0:C, 0:len(grp) * 128], len(grp), 128), [pv_], [vtok], eng="act")
            for g0 in range(0, nch, 8):
                grp = chs[g0:g0 + 8]
                pt = PS(); ptb = pt.ap[:, :].bitcast(BF16)
                psc = PS()
                for gi, (t0, C, kind, ci) in enumerate(grp):
                    TR(ptb[0:C, gi * 128:(gi + 1) * 128], kdc[:, 0, t0:t0 + C], ident_b[:], [kdc, ident_b], [pt])
                    MM(psc[0:C, gi * HC:gi * HC + C], kbc[:, 0, t0:t0 + C], qbc[:, 0, t0:t0 + C], True, True, [kbc, qbc], [psc])
                C = grp[0][1]
                ng = len(grp)
                CP(kdtok[0:C, g0:g0 + ng, :], bview(ptb[0:C, 0:ng * 128], ng, 128), [pt], [kdtok], eng="act")
                P.op("dve", lambda e, C=C, g0=g0, ng=ng, psc=psc: e.copy_predicated(
                    out=scm[0:C, g0:g0 + ng, 0:C], mask=mask_incl_i[0:C, 0:ng, 0:C],
                    data=bview(psc[0:C, 0:ng * HC], ng, HC)[:, :, 0:C]), bl([psc, mask_incl_i, scm]), bl([scm]))
            for g0 in range(0, ncp, 4):
                pb = PS()
                for gi in range(4):
                    k_ = g0 + gi
                    MM(pb[:, gi * 128:(gi + 1) * 128], kdtok[0:HC, k_, :], vtok[0:HC, k_, :], True, True, [kdtok, vtok], [pb])
                CP(pSsb[:, g0:g0 + 4, :], bview(pb[:, :], 4, 128), [pb], [pSsb], eng="act")
            if hd + 1 < 4:
                pend_proj = emit_proj(hd + 1)
            CP(Sall[:, 0, :], hgS[:, l, hd, :], [hgS], [Sall])
            for c in range(ncp):
                STT(Sall[:, c + 1, :], Sall[:, c, :], st_[:, 1, c:c + 1], pSsb[:, c, :], ALU.mult, ALU.add, [Sall, st_, pSsb], [Sall])
            CP(hgS[:, l, hd, :], Sall[:, ncp, :], [Sall], [hgS])
            for hf in range(2):
                cs = slice(hf * (ncp // 2), (hf + 1) * (ncp // 2))
                TT(Sbf[:, cs, :], Sall[:, cs, :], st_[:, 2, cs].unsqueeze(2).to_broadcast([128, ncp // 2, 128]), ALU.mult, [Sall, st_], [Sbf],
                   eng=("dve" if hf == 0 else "pool"))
            po_p = PS(hold=True)
            for k_ in range(ncp):
                t0 = k_ * HC
                MM(po_p[:, t0:t0 + HC], vtok[0:HC, k_, :], scm[0:HC, k_, 0:HC], True, False, [vtok, scm], [po_p])
                MM(po_p[:, t0:t0 + HC], Sbf[:, k_, :], qbc[:, 0, t0:t0 + HC], False, True, [Sbf, qbc], [po_p])
            po_s = None
            if last:
                DMA(Sall[:, 0:NSQ, :], s_hg[l, :, hd].rearrange("q k v -> k q v"), [], [Sall])
                TT(Sbf[:, 0:NSQ, :], Sall[:, 0:NSQ, :], st_[:, 2, ncp:nch].unsqueeze(2).to_broadcast([128, NSQ, 128]), ALU.mult, [Sall, st_], [Sbf])
                po_s = PS(hold=True)
                for q in range(NSQ):
                    k_ = ncp + q
                    t0 = NPT + q * TS
                    MM(po_s[:, q * TS:(q + 1) * TS], vtok[0:TS, k_, :], scm[0:TS, k_, 0:TS], True, False, [vtok, scm], [po_s])
                    MM(po_s[:, q * TS:(q + 1) * TS], Sbf[:, q, :], qbc[:, 0, t0:t0 + TS], False, True, [Sbf, qbc], [po_s])
                TT(Sall[:, 0:NSQ, :], Sall[:, 0:NSQ, :], st_[:, 1, ncp:nch].unsqueeze(2).to_broadcast([128, NSQ, 128]), ALU.mult, [Sall, st_], [Sall])
                for g0 in range(0, NSQ, 4):
                    pb = PS()
                    for gi in range(4):
                        k_ = ncp + g0 + gi
                        MM(pb[:, gi * 128:(gi + 1) * 128], kdtok[0:TS, k_, :], vtok[0:TS, k_, :], True, True, [kdtok, vtok], [pb])
                    TT(Sall[:, g0:g0 + 4, :], Sall[:, g0:g0 + 4, :], bview(pb[:, :], 4, 128), ALU.add, [Sall, pb], [Sall])
                OUT(o_hg[l, 1:1 + NSQ, hd].rearrange("q k v -> k q v"), Sall[:, 0:NSQ, :], [Sall], "hgs")
            if last:
                OUT(o_hg[l, 0, hd], hgS[:, l, hd, :], [hgS], "hgp")
            for (t0, n), pp in zip(sl, (po_p, po_s)):
                CP(o32[:, 0, t0:t0 + n], pp[:, 0:n], [pp], [o32], eng="act")
            PREL(po_p)
            if last:
                PREL(po_s)
            for (t0, n), pp in zip(sl, (po_p, po_s)):
                TT(osq[:, 0, t0:t0 + n], o32[:, 0, t0:t0 + n], o32[:, 0, t0:t0 + n], ALU.mult, [o32], [osq])
                pn = PS(); pg = PS()
                MM(pn[:, 0:n], ones_b[:], osq[:, 0, t0:t0 + n], True, True, [ones_b, osq], [pn])
                for kc in range(8):
                    MM(pg[:, 0:n], wo_[:, kc, hd * 128:(hd + 1) * 128], h[:, kc, t0:t0 + n], kc == 0, kc == 7, [wo_, h], [pg])
                ACT(A_[:, t0:t0 + n], pn[:, 0:n], AF.Ln, [pn], [t1], scale=1.0 / 128, bias=EPS)
                ACT(A_[:, t0:t0 + n], A_[:, t0:t0 + n], AF.Exp, [t1], [t1], scale=-0.5)
                ACT(B_[:, t0:t0 + n], pg[:, 0:n], AF.Silu, [pg], [t2])
                STT(A_[:, t0:t0 + n], o32[:, 0, t0:t0 + n], pv(l, "hg_norm_g", hd), A_[:, t0:t0 + n], ALU.mult, ALU.mult, [o32, ptab, t1], [t1])
                TT(y[:, hd, t0:t0 + n], A_[:, t0:t0 + n], B_[:, t0:t0 + n], ALU.mult, [t1, t2], [y])
    bones = sb("bones", [128, 128], BF16)
    bones2 = sb("bones2", [128, 2], BF16)
    MSET(bones[:], 0.0, [bones]); MSET(bones2[:], 0.0, [bones2])
    for par in range(2):
        MSET(bones[par * 64:(par + 1) * 64, par * 64:(par + 1) * 64], 1.0, [bones])
        MSET(bones2[par * 64:(par + 1) * 64, par:par + 1], 1.0, [bones2])
    mask_lower = sb("mask_lower", [64, 64], F32)
    MSET(mask_lower[:], 1.0, [mask_lower])
    P.op("pool", lambda e: e.affine_select(out=mask_lower[:], in_=mask_lower[:], pattern=[[-1, 64]], compare_op=ALU.is_ge,
                                           fill=0.0, base=-1, channel_multiplier=1), [mask_lower.b], [mask_lower.b])
    mask_incl_neg = sb("mask_incl_neg", [64, 64], F32)
    TSC(mask_incl_neg[:], mask_incl[:], -1.0, None, ALU.mult, None, [mask_incl], [mask_incl_neg])
    for l in range(L):
        TSC(dv(l, 30, 4), pv(l, "rw_k_a", 0, 4), -1.0, 1.0, ALU.mult, ALU.add, [ptab], [ptab])
    rwst_bd = sb("rwst_bd", [128, 4, 128], BF16)
    MSET(rwst_bd[:], 0.0, [rwst_bd])

    def MIX_RW(ti, l, NT):
        last = ti == NTILE - 1
        sl = slabs(NT)
        chs = chunks_of(NT)
        ncp = NPT // CH
        nch = len(chs)
        wts = [WLOAD(kview(w_in[l], 2048 + i * 512, 512), 8, 512) for i in range(3)]
        wl_ = WLOAD(kview(w_in[l], 3584, 256), 8, 256)
        At = aalloc("rw_At", [4, NT], BF16); Rt = aalloc("rw_Rt", [4, NT], BF16)
        Kt = aalloc("rw_Kt", [4, NT], BF16); Bt = aalloc("rw_Bt", [4, NT], BF16)
        rk = aalloc("rw_rk", [4, NT], BF16); vb = aalloc("rw_vb", [4, NT], BF16)
        sgd = aalloc("rw_sgd", [NT], BF16)
        ecl = aalloc("rw_ecl", [4, nch], F32)
        mark = ar["off"]
        r32 = aalloc("rw_r32", [4, NT], F32); k32 = aalloc("rw_k32", [4, NT], F32)
        twd = aalloc("rw_twd", [NT], BF16)
        mark2 = ar["off"]
        rexp = [aalloc("rw_rexp%d" % i, [1 + NPT], F32) for i in range(2)]
        rexs = [aalloc("rw_rexs%d" % i, [NSQ * (TS + 1)], F32) for i in range(2)] if last else None
        shs = aalloc("rw_shs", [14, NSQ], F32) if last else None
        if last:
            hh_ap = y.ap[0:NSQ, 4:10, :].rearrange("p a b -> p (a b)").bitcast(F32)[:, 0:RW_COLS]
            hh = T(hh_ap.rearrange("p (a b) -> p a b", a=1), y.b)
            DMA(hh[:, 0, :], s_sh[l], [], [hh])
            for g in range(2):
                ps = PS()
                for kc in range(7):
                    TR(ps[:, kc * 16:(kc + 1) * 16], hh[:, 0, (g * 7 + kc) * 128:(g * 7 + kc + 1) * 128], ident_f[0:16, 0:16], [hh, ident_f], [ps])
                CP(shs[:, g * 7:(g + 1) * 7, :].rearrange("p a b -> p (a b)"), ps[:, 0:112], [ps], [shs])
        for fc in range(14):
            wt = wts[fc // 4] if fc < 12 else wl_
            jj = fc % 4 if fc < 12 else fc - 12
            rp = rexp[fc % 2]
            rs_ = rexs[fc % 2] if last else None
            CP(rp[:, 0, 0:1], rwprev[:, l, fc:fc + 1], [rwprev], [rp])
            if last:
                CP(bview(rs_[:, 0, :], NSQ, TS + 1)[:, :, 0], shs[:, fc, :], [shs], [rs_])
            for (t0, n) in sl:
                pp = PS()
                for kc in range(8):
                    MM(pp[:, 0:n], wt[:, kc, jj * 128:(jj + 1) * 128], h[:, kc, t0:t0 + n], kc == 0, kc == 7, [wt, h], [pp])
                if t0 == 0:
                    CP(rp[:, 0, 1:1 + NPT], pp[:, 0:n], [pp], [rp], eng="act")
                else:
                    CP(bview(rs_[:, 0, :], NSQ, TS + 1)[:, :, 1:TS + 1], bview(pp[:, 0:n], NSQ, TS), [pp], [rs_], eng="act")
            CP(rwprev[:, l, fc:fc + 1], rp[:, 0, NPT:NPT + 1], [rp], [rwprev])
            if last:
                CP(shcol[:, l, fc, 0:1], rp[:, 0, NPT:NPT + 1], [rp], [shcol])
                CP(shcol[:, l, fc, 1:1 + NSQ], bview(rs_[:, 0, :], NSQ, TS + 1)[:, :, TS], [rs_], [shcol])
            if fc < 4:
                dst, dst_t = r32[:, fc, :], r32
            elif fc < 8:
                dst, dst_t = k32[:, fc - 4, :], k32
            elif fc < 12:
                dst, dst_t = vb[:, fc - 8, :], vb
            elif fc == 12:
                dst, dst_t = sgbuf[:, :], sgbuf
            else:
                dst, dst_t = sgbuf[:, :], sgbuf
            for (t0, n) in sl:
                if t0 == 0:
                    raw = rp[:, 0, 1:1 + NPT]; prv = rp[:, 0, 0:NPT]; rd = rp
                    d_ = dst[:, 0:NPT] if fc < 12 else sgbuf[:, 0:NPT]
                    tm = sgbuf[:, 0:NPT] if fc < 12 and fc >= 8 else d_
                else:
                    raw = bview(rs_[:, 0, :], NSQ, TS + 1)[:, :, 1:TS + 1]; prv = bview(rs_[:, 0, :], NSQ, TS + 1)[:, :, 0:TS]; rd = rs_
                    d_ = bview(dst[:, NPT:NT], NSQ, TS) if fc < 12 else bview(xtmp[:, 0:NSM], NSQ, TS)
                    tm = bview(xtmp[:, 0:NSM], NSQ, TS) if fc < 12 and fc >= 8 else d_
                tmt = sgbuf if t0 == 0 else xtmp
                wr = [dst_t] if fc < 8 else [tmt]
                TSC(tm, raw, dv(l, 16 + fc), None, ALU.mult, None, [rd, ptab], wr)
                STT(tm, prv, pv(l, "rw_mu", fc), tm, ALU.mult, ALU.add, [rd, ptab] + wr, wr)
                if 8 <= fc < 12:
                    CP(d_, tm, wr, [vb], eng="act")
                elif fc == 12:
                    src = tm
                    if t0 == 0:
                        ACT(twd[0:64, 0, 0:NPT], sgbuf[0:64, 0:NPT], AF.Tanh, [sgbuf], [twd])
                        CP(twd[64:128, 0, 0:NPT], sgbuf[64:128, 0:NPT], [sgbuf], [twd], eng="act")
                    else:
                        ACT(twd[0:64, 0, NPT:NT], xtmp[0:64, 0:NSM], AF.Tanh, [xtmp], [twd])
                        CP(twd[64:128, 0, NPT:NT], xtmp[64:128, 0:NSM], [xtmp], [twd], eng="act")
                elif fc == 13:
                    if t0 == 0:
                        ACT(sgd[:, 0, 0:NPT], sgbuf[:, 0:NPT], AF.Sigmoid, [sgbuf], [sgd])
                    else:
                        ACT(sgd[:, 0, NPT:NT], xtmp[:, 0:NSM], AF.Sigmoid, [xtmp], [sgd])
        if stage < 2.2:
            return
        arelease(mark2)
        ta = aalloc("rw_ta", [NT], F32); tb = aalloc("rw_tb", [NT], F32); tcc = aalloc("rw_tc", [NT], F32)
        td = aalloc("rw_td", [NT], F32); te = aalloc("rw_te", [NT], F32)
        for j in range(4):
            A_ = ta[:, 0, :]; B_ = tb[:, 0, :]; C_ = tcc[:, 0, :]; D_ = td[:, 0, :]; E_ = te[:, 0, :]
            for (t0, n) in sl:
                pw = PS(); pa = PS()
                MM(pw[:, 0:n], rwup[0:64, l, 0, j * 128:(j + 1) * 128], twd[0:64, 0, t0:t0 + n], True, True, [rwup, twd], [pw])
                MM(pa[:, 0:n], rwup[64:128, l, 1, j * 128:(j + 1) * 128], twd[64:128, 0, t0:t0 + n], True, True, [rwup, twd], [pa])
                ACT(A_[:, t0:t0 + n], pw[:, 0:n], AF.Sigmoid, [pw, ptab], [ta], bias=pv(l, "rw_w0", j))
                ACT(B_[:, t0:t0 + n], pa[:, 0:n], AF.Sigmoid, [pa, ptab], [tb], bias=pv(l, "rw_a0", j))
            TSC(A_[:, 0:NT], A_[:, 0:NT], -0.606531, None, ALU.mult, None, [ta], [ta])
            SCAN(C_[:, 0:NT], cmask[:, 0:NT], A_[:, 0:NT], 0.0, [cmask, ta], [tcc])
            CP(ecl[:, j, 0:ncp], bview(C_[:, 0:NPT], ncp, CH)[:, :, CH - 1], [tcc], [ecl])
            if last:
                CP(ecl[:, j, ncp:nch], bview(C_[:, NPT:NT], NSQ, TS)[:, :, TS - 1], [tcc], [ecl])
            ACT(ecl[:, j, :], ecl[:, j, :], AF.Exp, [ecl], [ecl])
            TT(A_[:, 0:NT], C_[:, 0:NT], A_[:, 0:NT], ALU.subtract, [tcc, ta], [ta])
            ACT(A_[:, 0:NT], A_[:, 0:NT], AF.Exp, [ta], [ta])
            TSC(D_[:, 0:NT], k32[:, j, 0:NT], pv(l, "rw_k_k", j), None, ALU.mult, None, [k32, ptab], [td])
            TT(sgd2[:, 0:NT], D_[:, 0:NT], D_[:, 0:NT], ALU.mult, [td], [sgd2])
            for (t0, n) in sl:
                pn = PS()
                MM(pn[:, 0:n], bones[:], sgd2[:, t0:t0 + n], True, True, [bones, sgd2], [pn])
                TSC(E_[:, t0:t0 + n], pn[:, 0:n], 6e-20, None, ALU.max, None, [pn], [te])
            ACT(E_[:, 0:NT], E_[:, 0:NT], AF.Ln, [te], [te])
            ACT(E_[:, 0:NT], E_[:, 0:NT], AF.Exp, [te], [te], scale=-0.5)
            TT(D_[:, 0:NT], D_[:, 0:NT], E_[:, 0:NT], ALU.mult, [td, te], [td])
            TT(At[:, j, 0:NT], D_[:, 0:NT], A_[:, 0:NT], ALU.mult, [td, ta], [At])
            ACT(A_[:, 0:NT], C_[:, 0:NT], AF.Exp, [tcc], [ta], scale=-1.0)
            TT(D_[:, 0:NT], D_[:, 0:NT], B_[:, 0:NT], ALU.mult, [td, tb], [td])
            TT(Bt[:, j, 0:NT], D_[:, 0:NT], A_[:, 0:NT], ALU.mult, [td, ta], [Bt])
            TSC(B_[:, 0:NT], B_[:, 0:NT], pv(l, "rw_k_a", j), dv(l, 30 + j), ALU.mult, ALU.add, [tb, ptab], [tb])
            TT(B_[:, 0:NT], B_[:, 0:NT], k32[:, j, 0:NT], ALU.mult, [tb, k32], [tb])
            TT(Kt[:, j, 0:NT], B_[:, 0:NT], A_[:, 0:NT], ALU.mult, [tb, ta], [Kt])
            TT(B_[:, 0:NT], B_[:, 0:NT], r32[:, j, 0:NT], ALU.mult, [tb, r32], [tb])
            TSC(rk[:, j, 0:NT], B_[:, 0:NT], pv(l, "rw_r_k", j), None, ALU.mult, None, [tb, ptab], [rk])
            ACT(C_[:, 0:NT], C_[:, 0:NT], AF.Exp, [tcc], [tcc])
            TT(Rt[:, j, 0:NT], r32[:, j, 0:NT], C_[:, 0:NT], ALU.mult, [r32, tcc], [Rt])
        arelease(mark)
        if stage < 2.4:
            return
        yraw = aalloc("rw_yraw", [4, NT], BF16)
        mark3 = ar["off"]
        gM = aalloc("rw_gM", [8, CH], BF16, parts=CH); gMT = aalloc("rw_gMT", [8, CH], BF16, parts=CH)
        gQ = aalloc("rw_gQ", [8, CH], BF16, parts=CH); gN = aalloc("rw_gN", [8, CH], BF16, parts=CH); gP = aalloc("rw_gP", [8, CH], BF16, parts=CH)
        Tb = [aalloc("rw_T%d" % i, [8, CH], BF16, parts=CH) for i in range(2)]
        Pb = [aalloc("rw_P%d" % i, [8, CH], BF16, parts=CH) for i in range(2)]
        PTb = [aalloc("rw_PT%d" % i, [8, CH], BF16, parts=CH) for i in range(2)]
        vtok = aalloc("rw_vtok", [W], BF16, parts=CH); ktok = aalloc("rw_ktok", [W], BF16, parts=CH); btok = aalloc("rw_btok", [W], BF16, parts=CH)
        gN2 = aalloc("rw_gN2", [8, CH], BF16, parts=CH); gQ2 = aalloc("rw_gQ2", [8, CH], BF16, parts=CH); gP2 = aalloc("rw_gP2", [8, CH], BF16, parts=CH)
        Tb2 = [aalloc("rw_T2%d" % i, [8, CH], BF16, parts=CH) for i in range(2)]
        vtok2 = aalloc("rw_vtok2", [W], BF16, parts=CH); ktok2 = aalloc("rw_ktok2", [W], BF16, parts=CH); btok2 = aalloc("rw_btok2", [W], BF16, parts=CH)
        gNs, gQs, gPs = [gN, gN2], [gQ, gQ2], [gP, gP2]
        vtoks, ktoks, btoks = [vtok, vtok2], [ktok, ktok2], [btok, btok2]
        Tbs = [Tb, Tb2]
        gt = aalloc("rw_gt", [W], BF16, parts=CH); ut = aalloc("rw_ut", [W], BF16, parts=CH)
        yA = aalloc("rw_yA", [W], F32, parts=CH); yB = aalloc("rw_yB", [W], F32, parts=CH)
        yst = aalloc("rw_yst", [4, 8], F32, parts=CH)
        yob = aalloc("rw_yob", [W], BF16, parts=CH)
        sts = aalloc("rw_sts", [4, 64], F32) if last else None
        stl = aalloc("rw_stl", [8, 64], F32, parts=64) if last else None
        sto = aalloc("rw_sto", [W], F32, parts=64) if last else None
        stmp = aalloc("rw_stmp", [4, 64], F32)

        def state_out(ST_t, ST_ap, dst):
            ps = PS()
            for j in range(4):
                TR(ps[0:64, j * 128:(j + 1) * 128], ST_ap[:, j, :], ident_f[:], [ST_t, ident_f], [ps])
            CP(sto[:, 0, :], ps[0:64, :], [ps], [sto], eng="act")
            OUT(dst.rearrange("h v k -> v h k"), sto[:, 0, :].rearrange("v (h k) -> v h k", h=8), [sto], "rwst")

        def SI(k_, t0, C, par, hook):
            nlev = {64: 5, 8: 2}[C]
            gN, gQ, gP = gNs[par], gQs[par], gPs[par]
            vtok, ktok, btok = vtoks[par], ktoks[par], btoks[par]
            Tb = Tbs[par]
            pM = PS(); pMT = PS(); pQ = PS(hold=True); pN = PS(hold=True); pP = PS(hold=True)
            for hh_ in range(8):
                j, par = hh_ // 2, hh_ % 2
                rows = slice(par * 64, par * 64 + 64)
                o_ = slice(hh_ * CH, hh_ * CH + C)
                a_ = At[rows, j, t0:t0 + C]; r_ = Rt[rows, j, t0:t0 + C]; k__ = Kt[rows, j, t0:t0 + C]; b_ = Bt[rows, j, t0:t0 + C]
                MM(pM[0:C, o_], b_, a_, True, True, [Bt, At], [pM])
                MM(pMT[0:C, o_], a_, b_, True, True, [Bt, At], [pMT])
                MM(pQ[0:C, o_], b_, r_, True, True, [Bt, Rt], [pQ])
                MM(pN[0:C, o_], k__, a_, True, True, [Kt, At], [pN])
                MM(pP[0:C, o_], k__, r_, True, True, [Kt, Rt], [pP])

            for src_, dst_, neg in ((vb, vtok, False), (Kt, ktok, False), (Bt, btok, True)):
                pt = PS(); ptb = pt.ap[:, :].bitcast(BF16)
                for j in range(4):
                    TR(ptb[0:C, j * 128:(j + 1) * 128], src_[:, j, t0:t0 + C], ident_b[:], [src_, ident_b], [pt])
                if neg:
                    ACT(dst_[0:C, 0, :], ptb[0:C, 0:W], AF.Copy, [pt], [dst_], scale=-1.0)
                else:
                    CP(dst_[0:C, 0, :], ptb[0:C, 0:W], [pt], [dst_], eng="act")
            def g3(t_):
                return t_[0:C, :, 0:C]

            def p3(p_):
                return bview(p_[0:C, :], 8, CH)[:, :, 0:C]

            def mk(m_):
                return m_[0:C, 0:C].unsqueeze(1).to_broadcast([C, 8, C])
            TT(g3(gM), p3(pM), mk(mask_strict), ALU.mult, [pM, mask_strict], [gM])
            TT(g3(gMT), p3(pMT), mk(mask_lower), ALU.mult, [pMT, mask_lower], [gMT])
            if hook is not None and C == TS:
                hook()
            late = [(gN, pN, mask_strict), (gQ, pQ, mask_incl_neg), (gP, pP, mask_incl)]

            def late_evac():
                if late:
                    g_, p_, m_ = late.pop(0)
                    TT(g3(g_), p3(p_), mk(m_), ALU.mult, [p_, m_], [g_])
                    PREL(p_)
            Tc = Tb[0]
            Pc, PTc = gM, gMT

            def emit_PP(lev, Pc, PTc):
                lastlev = lev == nlev
                pp1 = PS() if not lastlev else None
                pp2 = PS()
                for hh_ in range(8):
                    o_ = slice(hh_ * CH, hh_ * CH + C)
                    if not lastlev:
                        MM(pp1[0:C, o_], PTc[0:C, hh_, 0:C], Pc[0:C, hh_, 0:C], True, True, [PTc, Pc], [pp1])
                    MM(pp2[0:C, o_], Pc[0:C, hh_, 0:C], PTc[0:C, hh_, 0:C], True, True, [PTc, Pc], [pp2])
                return pp1, pp2
            pend = emit_PP(1, Pc, PTc)
            TT(g3(Tc), mk(ident_f), g3(gM), ALU.subtract, [ident_f, gM], [Tc])
            if hook is not None and C == TS:
                hook()
            tpend = None
            for lev in range(1, nlev + 1):
                Pn, PTn, Tn = Pb[lev % 2], PTb[lev % 2], Tb[lev % 2]
                lastlev = lev == nlev
                pp1, pp2 = pend
                if not lastlev:
                    CP(g3(Pn), p3(pp1), [pp1], [Pn], eng="act")
                CP(g3(PTn), p3(pp2), [pp2], [PTn], eng="dve")
                if tpend is not None:
                    pp3_, Told_, Tnew_ = tpend
                    TT(g3(Tnew_), p3(pp3_), g3(Told_), ALU.add, [pp3_, Told_], [Tnew_])
                    PREL(pp3_)
                    Tc = Tnew_
                    tpend = None
                late_evac()
                if not lastlev:
                    pend = emit_PP(lev + 1, Pn, PTn)
                pp3 = PS(hold=True)
                for hh_ in range(8):
                    o_ = slice(hh_ * CH, hh_ * CH + C)
                    MM(pp3[0:C, o_], PTn[0:C, hh_, 0:C], Tc[0:C, hh_, 0:C], True, True, [PTn, Tc], [pp3])
                tpend = (pp3, Tc, Tn)
                Pc, PTc = Pn, PTn
                if hook is not None and not lastlev:
                    hook()
                if lastlev:
                    pp3_, Told_, Tnew_ = tpend
                    TT(g3(Tnew_), p3(pp3_), g3(Told_), ALU.add, [pp3_, Told_], [Tnew_])
                    Tc = Tnew_
                    tpend = None
                    PREL(pp3_)
            while late:
                late_evac()
            return Tc

        def SD(k_, t0, C, kind, ci, par, Tc):
            gN, gQ, gP = gNs[par], gQs[par], gPs[par]
            vtok, ktok, btok = vtoks[par], ktoks[par], btoks[par]
            nlev = {64: 5, 8: 2}[C]
            if kind == "p":
                ST_t, ST_ap = rwST, rwST[:, l, :, :]
            else:
                ST_t, ST_ap = sts, sts[:, :, :]
                DMA(stl[:, :, :], s_rw[l, ci].rearrange("h v k -> v h k"), [], [stl])
                ps = PS()
                for j in range(4):
                    TR(ps[:, j * 64:(j + 1) * 64], stl[:, 2 * j:2 * j + 2, :].rearrange("v h k -> v (h k)"), ident_f[0:64, 0:64], [stl, ident_f], [ps])
                CP(sts[:, :, :].rearrange("p a b -> p (a b)"), ps[:, 0:256], [ps], [sts])
            for par in range(2):
                rows = slice(par * 64, par * 64 + 64)
                CP(rwst_bd[rows, :, par * 64:par * 64 + 64], ST_ap[rows], [ST_t], [rwst_bd], eng="act")
            pG = PS(hold=True)
            for j in range(4):
                MM(pG[0:C, j * 128:(j + 1) * 128], At[:, j, t0:t0 + C], rwst_bd[:, j, :], True, False, [At, rwst_bd], [pG])
                for par in range(2):
                    hh_ = 2 * j + par
                    vs = slice(hh_ * 64, hh_ * 64 + 64)
                    MM(pG[0:C, vs], gN[0:C, hh_, 0:C], vtok[0:C, 0, vs], False, par == 1, [gN, vtok], [pG])
            yield
            CP(gt[0:C, 0, :], pG[0:C, :], [pG], [gt], eng="act")
            PREL(pG)
            pU = PS(hold=True)
            for hh_ in range(8):
                vs = slice(hh_ * 64, hh_ * 64 + 64)
                MM(pU[0:C, vs], Tc[0:C, hh_, 0:C], gt[0:C, 0, vs], True, True, [Tc, gt], [pU])
            yield
            CP(ut[0:C, 0, :], pU[0:C, :], [pU], [ut], eng="act")
            PREL(pU)
            pY = PS(hold=True)
            for j in range(4):
                MM(pY[0:C, j * 128:(j + 1) * 128], Rt[:, j, t0:t0 + C], rwst_bd[:, j, :], True, False, [Rt, rwst_bd], [pY])
                for par in range(2):
                    hh_ = 2 * j + par
                    vs = slice(hh_ * 64, hh_ * 64 + 64)
                    MM(pY[0:C, vs], gP[0:C, hh_, 0:C], vtok[0:C, 0, vs], False, False, [gP, vtok], [pY])
                    MM(pY[0:C, vs], gQ[0:C, hh_, 0:C], ut[0:C, 0, vs], False, par == 1, [gQ, ut], [pY])
            pS = PS(hold=True)
            for j in range(4):
                js = slice(j * 128, (j + 1) * 128)
                MM(pS[:, js], ktok[0:C, 0, js], vtok[0:C, 0, js], True, False, [ktok, vtok], [pS])
                MM(pS[:, js], btok[0:C, 0, js], ut[0:C, 0, js], False, True, [btok, ut], [pS])
            yield
            for par in range(2):
                rows = slice(par * 64, par * 64 + 64)
                TT(stmp[rows, :, :], ST_ap[rows], bview(pS[rows, :], 4, 128)[:, :, par * 64:par * 64 + 64], ALU.add, [ST_t, pS], [stmp])
            PREL(pS)
            TT(ST_ap, stmp[:, :, :], ecl[:, :, k_:k_ + 1].to_broadcast([128, 4, 64]), ALU.mult, [stmp, ecl], [ST_t])
            if kind == "s":
                state_out(ST_t, ST_ap, o_rw[l, 1 + ci])
            if ti == 0 and l == 0 and L > 1:
                ADA_PIECES(1, 2)
            CP(yob[0:C, 0, :], pY[0:C, :], [pY], [yob], eng="act")
            PREL(pY)
            yield
            pt = PS(); ptb = pt.ap[:, :].bitcast(BF16)
            for j in range(4):
                TR(ptb[:, j * 64:j * 64 + C], yob[0:C, 0, j * 128:(j + 1) * 128], ident_b[0:C, 0:C], [yob, ident_b], [pt])
            CP(yraw[:, :, t0:t0 + C], bview(ptb[:, 0:256], 4, 64)[:, :, 0:C], [pt], [yraw], eng="act")

        seq = list(enumerate(chs))
        Tc_cur = SI(seq[0][0], seq[0][1][0], seq[0][1][1], 0, None)
        for i, (k_, (t0, C, kind, ci)) in enumerate(seq):
            sdg = SD(k_, t0, C, kind, ci, i % 2, Tc_cur)
            if i + 1 < len(seq):
                k2, (t02, C2, _, _) = seq[i + 1]
                Tc_cur = SI(k2, t02, C2, (i + 1) % 2, lambda: next(sdg, None))
            for _ in sdg:
                pass
        if last:
            state_out(rwST, rwST[:, l, :, :], o_rw[l, 0])
        arelease(mark3)
        psets = [[aalloc("rw_p%d%d" % (i, k), [NT], F32) for k in range(4)] + [aalloc("rw_psq%d" % i, [NT], BF16)] for i in range(2)]
        for j in range(4):
            yr = T(yraw.ap, Buf("yr%d" % j, ar["fence"]))
            yr.b.last_w = yraw.b.last_w
            ar["bufs"].append(yr.b)
            tA_, tB_, tC_, tD_, tsq = psets[j % 2]
            for (t0, n) in sl:
                tsl = slice(t0, t0 + n)
                ACT(tsq[:, 0, tsl], yraw[:, j, tsl], AF.Square, [yr], [tsq])
                pm = PS(); pq = PS(); pb = PS(); pg = PS()
                MM(pm[:, 0:n], bones[:], yraw[:, j, tsl], True, True, [bones, yr], [pm])
                MM(pq[:, 0:n], bones[:], tsq[:, 0, tsl], True, True, [bones, tsq], [pq])
                MM(pb[:, 0:n], bones[:], rk[:, j, tsl], True, True, [bones, rk], [pb])
                MM(pg[:, 0:n], rwup[:, l, 2, j * 128:(j + 1) * 128], sgd[:, 0, tsl], True, True, [rwup, sgd], [pg])
                A_ = tA_[:, 0, tsl]; B_ = tB_[:, 0, tsl]; C_ = tC_[:, 0, tsl]; D_ = tD_[:, 0, tsl]
                ACT(A_, pm[:, 0:n], AF.Copy, [pm], [tA_], scale=1.0 / 64)
                TT(B_, A_, A_, ALU.mult, [tA_], [tB_])
                STT(B_, pq[:, 0:n], 1.0 / 64, B_, ALU.mult, ALU.subtract, [pq, tB_], [tB_])
                ACT(B_, B_, AF.Ln, [tB_], [tB_], bias=64e-5)
                ACT(B_, B_, AF.Exp, [tB_], [tB_], scale=-0.5)
                TT(C_, yraw[:, j, tsl], A_, ALU.subtract, [yr, tA_], [tC_])
                TT(C_, C_, B_, ALU.mult, [tC_, tB_], [tC_])
                ACT(C_, C_, AF.Identity, [tC_, ptab], [tC_], scale=pv(l, "rw_ln_g", j), bias=pv(l, "rw_ln_b", j))
                TT(D_, pb[:, 0:n], vb[:, j, tsl], ALU.mult, [pb, vb], [tD_])
                TT(C_, C_, D_, ALU.add, [tC_, tD_], [tC_])
                TT(y[:, 4 + j, tsl], C_, pg[:, 0:n], ALU.mult, [tC_, pg], [y])

    def MIXERS(ti, l, NT):
        if stage >= 2:
            MIX_HG(ti, l, NT); areset()
        if stage > 2:
            MIX_RW(ti, l, NT); areset()
        if stage >= 4:
            MIX_CF(ti, l, NT); areset()
        if stage >= 5:
            MIX_LRU(ti, l, NT)

    def FINAL_STATES():
        for l in range(nlayer):
            stg = aalloc("fs_sh%d" % l, [RW_COLS], F32, parts=1 + NSQ)
            for g in range(4):
                ps = PS()
                nk = 4 if g < 3 else 2
                for kk_ in range(nk):
                    kc = g * 4 + kk_
                    TR(ps[0:17, kk_ * 128:(kk_ + 1) * 128], shcol[:, l, kc, :], ident_f[:], [shcol, ident_f], [ps])
                CP(stg[:, 0, g * 512:g * 512 + nk * 128], ps[0:17, 0:nk * 128], [ps], [stg])
            OUT(o_sh[l], stg[:, 0, :], [stg], "sh")
            stg2 = aalloc("fs_lh%d" % l, [W], F32, parts=1 + NSQ)
            ps = PS()
            for j in range(4):
                TR(ps[0:17, j * 128:(j + 1) * 128], lhcol[:, l, j, :], ident_f[:], [lhcol, ident_f], [ps])
            CP(stg2[:, 0, :], ps[0:17, :], [ps], [stg2])
            OUT(o_lh[l], stg2[:, 0, :], [stg2], "lh")

    for ti in range(NTILE):
        NT = NPT + (NSM if ti == NTILE - 1 else 0)
        nblk = NT // 128
        for blk in range(nblk):
            xt = aalloc("xtok%d" % (blk % 2), [D], F32) if blk < 2 else xt_bufs[blk % 2]
            if blk < 2:
                if blk == 0:
                    xt_bufs = []
                xt_bufs.append(xt)
            src = xp[ti * NPT + blk * 128: ti * NPT + (blk + 1) * 128, :] if blk < 4 else xs[:, :]
            DMA(xt[:, 0, :], src, [], [xt])
            for half in range(2):
                ps = PS()
                for j in range(4):
                    kc = half * 4 + j
                    TR(ps[:, j * 128:(j + 1) * 128], xt[:, 0, kc * 128:(kc + 1) * 128], ident_f[:], [xt, ident_f], [ps])
                CP(x[:, half * 4:half * 4 + 4, blk * 128:(blk + 1) * 128], ps[:, :].rearrange("p (j t) -> p j t", j=4), [ps], [x],
                   eng=("act" if half else "dve"))
        areset()
        for l in range(nlayer):
            ADA_PIECES(l, 12)
            if stage >= 1:
                NORM(l, 0, NT)
            areset()
            MIXERS(ti, l, NT)
            areset()
            if stage >= 6:
                MERGE_MLP(l, NT)
        for blk in range(nblk):
            xo = aalloc("xo%d" % (blk % 2), [D], F32) if blk < 2 else xo_bufs[blk % 2]
            if blk < 2:
                if blk == 0:
                    xo_bufs = []
                    fstat = aalloc("fstat", [8], F32)
                    junk = aalloc("junk", [D], F32)
                xo_bufs.append(xo)
            for half in range(2):
                ps = PS()
                for j in range(4):
                    kc = half * 4 + j
                    TR(ps[:, j * 128:(j + 1) * 128], x[:, kc, blk * 128:(blk + 1) * 128], ident_f[:], [x, ident_f], [ps])
                CP(xo[:, 0, half * 512:(half + 1) * 512], ps[:, :], [ps], [xo], eng=("act" if half else "dve"))
            P.op("act", lambda e, xo=xo, junk=junk, fstat=fstat: e.activation(out=junk[:, 0, :], in_=xo[:, 0, :], func=AF.Square,
                                                                              accum_out=fstat[:, 0, 0:1]), bl([xo]), bl([junk, fstat]))
            ACT(fstat[:, 0, 1:2], fstat[:, 0, 0:1], AF.Sqrt, [fstat], [fstat], scale=1.0 / D, bias=EPS)
            RCP(fstat[:, 0, 2:3], fstat[:, 0, 1:2], [fstat], [fstat])
            STT(xo[:, 0, :], xo[:, 0, :], fstat[:, 0, 2:3], gfin[:], ALU.mult, ALU.mult, [xo, fstat, gfin], [xo])
            dst = o_yp[ti * NPT + blk * 128: ti * NPT + (blk + 1) * 128, :] if blk < 4 else o_ys[:, :]
            OUT(dst, xo[:, 0, :], [xo], "y")
        areset()

    if stage >= 6:
        FINAL_STATES()
    print('arena high-water', ar.get('hw'), 'of', ARENA_E, 'sbuf left', nc.sbuf_bytes_remaining)
    P.emit(out_bufs)
    return nc, dbg_outs


_CACHE = {}


def kernel(**inp):
    f = lambda a: np.ascontiguousarray(np.asarray(a, dtype=np.float32))
    if "nc" not in _CACHE:
        _CACHE["nc"] = build()
    nc, _ = _CACHE["nc"]
    shared = {k: f(inp[k]) for k in ("ada_w", "ada_b", "norm_mix_g", "norm_mlp_g", "norm_final_g", "w_in", "hg_lower", "hg_norm_g",
                                     "rw_mu", "rw_w0", "rw_w_up", "rw_a0", "rw_a_up", "rw_g_up", "rw_k_k", "rw_k_a", "rw_r_k",
                                     "rw_ln_g", "rw_ln_b", "cf_dw", "cf_dw_b", "cf_ln_g", "cf_ln_b", "lru_conv_w", "lru_conv_b",
                                     "lru_wa", "lru_ba", "lru_wx", "lru_bx", "lru_lambda", "w_branch", "w_gate", "b_gate", "w_out",
                                     "w_mlp1", "w_mlp2")}
    xp = f(inp["x_prompt"]); xs = f(inp["x_sample"])
    in_maps = []
    for c in range(NCORE):
        sq = slice(c * NSQ, (c + 1) * NSQ)
        m = dict(shared)
        m["xp"] = xp[c]
        m["xs"] = np.ascontiguousarray(xs[sq].reshape(NSM, D))
        m["s_hg"] = f(inp["state_hgrn"][:, sq]); m["s_rw"] = f(inp["state_rwkv"][:, sq])
        m["s_sh"] = f(inp["state_rwkv_shift"][:, sq]); m["s_cf"] = f(inp["state_conv"][:, sq])
        m["s_lh"] = f(inp["state_lru_h"][:, sq]); m["s_lc"] = f(inp["state_lru_conv"][:, sq])
        m["cc"] = np.ascontiguousarray(np.concatenate([f(inp["c_prompt"])[c:c + 1], f(inp["c_sample"])[sq]], axis=0))
        in_maps.append(m)
    res = run_bass_kernel_spmd(nc, in_maps, core_ids=list(range(NCORE)))
    R = res.results
    _CACHE["last"] = R
    y_p = np.stack([R[c]["o_yp"] for c in range(NCORE)], axis=0)
    y_s = np.concatenate([R[c]["o_ys"].reshape(NSQ, TS, D) for c in range(NCORE)], axis=0)
    outs = [y_p, y_s]
    for nm in ("o_hg", "o_rw", "o_sh", "o_cf", "o_lh", "o_lc"):
        outs.append(np.stack([R[c][nm][:, 0] for c in range(NCORE)], axis=1))
    for nm in ("o_hg", "o_rw", "o_sh", "o_cf", "o_lh", "o_lc"):
        outs.append(np.concatenate([R[c][nm][:, 1:] for c in range(NCORE)], axis=1))
    return tuple(np.ascontiguousarray(o.astype(np.float32)) for o in outs)
```

```python
import contextlib
import numpy as np
import concourse.bass as bass
import concourse.mybir as mybir
from concourse.bass_utils import run_bass_kernel_spmd

F32 = mybir.dt.float32
BF16 = mybir.dt.bfloat16
AF = mybir.ActivationFunctionType
ALU = mybir.AluOpType
AX = mybir.AxisListType

ENGS = ("pe", "act", "dve", "pool", "sp")
D = 1024
W = 512
NCORE = 8
SEQ = 2048
NTILE = 4
NPT = 512
NSQ = 16
TS = 8
NSM = NSQ * TS
NTMAX = NPT + NSM
L = 2
IN_COLS = 5888
RW_COLS = 1792
CH = 64
EPS = 1e-6
DEBUG_SITES = False
SITES = {}


class Buf:
    __slots__ = ("name", "last_w", "readers")

    def __init__(self, name, readers=None):
        self.name = name
        self.last_w = None
        self.readers = list(readers) if readers else []


class Op:
    __slots__ = ("eng", "idx", "fn", "waits", "inc", "semval", "dma_sem", "dma_val", "clock", "site")

    def __init__(self, eng, idx, fn):
        self.eng = eng
        self.idx = idx
        self.fn = fn
        self.waits = []
        self.inc = False
        self.semval = 0
        self.dma_sem = None
        self.dma_val = 0
        self.clock = None


class Prog:
    def __init__(self, nc):
        self.nc = nc
        self.ops = {e: [] for e in ENGS}
        self.clock = {e: {} for e in ENGS}
        self.dma_sems = {}
        self.last_dma = {}

    @staticmethod
    def _key(prod):
        if prod.dma_sem is not None:
            return ("dma", prod.dma_sem), prod.dma_val
        return prod.eng, prod.idx

    def _record(self, eng, fn, reads, writes, dma_sem=None):
        lst = self.ops[eng]
        op = Op(eng, len(lst), fn)
        if DEBUG_SITES:
            import sys as _s
            f_ = _s._getframe(3)
            op.site = (f_.f_lineno, f_.f_back.f_lineno if f_.f_back else 0)
        cands = []
        for b in reads:
            if b.last_w is not None:
                cands.append(b.last_w)
        for b in writes:
            if b.last_w is not None:
                cands.append(b.last_w)
            cands.extend(b.readers)
        ck = self.clock[eng]
        best = {}
        if dma_sem is not None:
            prev = self.last_dma.get(dma_sem)
            if prev is not None and ck.get(("dma", dma_sem), -1) < prev.dma_val:
                best[("dma", dma_sem)] = prev
        for p in cands:
            if p.dma_sem is None and p.eng == "pe" and eng == "pe":
                continue
            k, v = self._key(p)
            if ck.get(k, -1) >= v:
                continue
            if k not in best or self._key(best[k])[1] < v:
                best[k] = p
        for k, p in best.items():
            ck[k] = self._key(p)[1]
            op.waits.append(p)
            if p.dma_sem is None:
                p.inc = True
            if p.clock is not None:
                for kk, vv in p.clock.items():
                    if ck.get(kk, -1) < vv:
                        ck[kk] = vv
        if dma_sem is not None:
            cnt = self.dma_sems.setdefault(dma_sem, [0])
            cnt[0] += 16
            op.dma_sem = dma_sem
            op.dma_val = cnt[0]
            self.last_dma[dma_sem] = op
        for b in reads:
            b.readers.append(op)
        for b in writes:
            b.last_w = op
            b.readers = []
        op.clock = dict(ck)
        lst.append(op)
        return op

    def op(self, eng, fn, reads=(), writes=()):
        return self._record(eng, fn, reads, writes)

    def dma(self, eng, fn, sem, reads=(), writes=()):
        return self._record(eng, fn, reads, writes, dma_sem=sem)

    def emit(self, final_bufs=()):
        nc = self.nc
        self._record("sp", None, list(final_bufs), [])
        with contextlib.ExitStack() as st:
            sems = {e: st.enter_context(nc.semaphore("s_" + e)) for e in ENGS}
            dsem = {n: st.enter_context(nc.semaphore("d_" + str(n))) for n in self.dma_sems}
            for e in ENGS:
                c = 0
                for op in self.ops[e]:
                    if op.dma_sem is None and op.inc:
                        c += 1
                        op.semval = c
            block = st.enter_context(nc.Block())

            def run(eng_name):
                def body(eng):
                    for op in self.ops[eng_name]:
                        for p in op.waits:
                            if p.dma_sem is not None:
                                eng.wait_ge(dsem[p.dma_sem], p.dma_val)
                            else:
                                eng.wait_ge(sems[p.eng], p.semval)
                        if op.fn is None:
                            continue
                        ins = op.fn(eng)
                        if DEBUG_SITES:
                            try:
                                SITES[ins.ins.name] = op.site
                            except Exception:
                                pass
                        if op.dma_sem is not None:
                            ins.then_inc(dsem[op.dma_sem], 16)
                        elif op.inc:
                            ins.then_inc(sems[eng_name], 1)
                return body

            block.tensor(run("pe"))
            block.scalar(run("act"))
            block.vector(run("dve"))
            block.gpsimd(run("pool"))
            block.sync(run("sp"))


class T:
    __slots__ = ("ap", "b")

    def __init__(self, ap, b):
        self.ap = ap
        self.b = b

    def __getitem__(self, k):
        return self.ap[k]


def build(debug=None, ntile=NTILE, nlayer=L, stage=9):
    NTILE = ntile
    SEQ = NTILE * NPT
    nc = bass.Bass("TRN2", target_bir_lowering=False)
    P = Prog(nc)
    st = contextlib.ExitStack()
    dbg_outs = {}

    def din(name, shape):
        return nc.dram_tensor(name, list(shape), F32, kind="ExternalInput").ap()

    def dout(name, shape):
        return nc.dram_tensor(name, list(shape), F32, kind="ExternalOutput").ap()

    xp = din("xp", [SEQ, D]); xs = din("xs", [NSM, D])
    s_hg = din("s_hg", [L, NSQ, 4, 128, 128]); s_rw = din("s_rw", [L, NSQ, 8, 64, 64])
    s_sh = din("s_sh", [L, NSQ, RW_COLS]); s_cf = din("s_cf", [L, NSQ, 30, W])
    s_lh = din("s_lh", [L, NSQ, W]); s_lc = din("s_lc", [L, NSQ, 3, W])
    cc = din("cc", [1 + NSQ, D])
    ada_w = din("ada_w", [L, D, 6 * D]); ada_b = din("ada_b", [L, 6 * D])
    norm_mix_g = din("norm_mix_g", [L, D]); norm_mlp_g = din("norm_mlp_g", [L, D]); norm_final_g = din("norm_final_g", [D])
    w_in = din("w_in", [L, D, IN_COLS])
    hg_lower = din("hg_lower", [L, W]); hg_norm_g = din("hg_norm_g", [L, W])
    rw_mu = din("rw_mu", [L, RW_COLS]); rw_w0 = din("rw_w0", [L, W]); rw_w_up = din("rw_w_up", [L, 64, W])
    rw_a0 = din("rw_a0", [L, W]); rw_a_up = din("rw_a_up", [L, 64, W]); rw_g_up = din("rw_g_up", [L, 128, W])
    rw_k_k = din("rw_k_k", [L, W]); rw_k_a = din("rw_k_a", [L, W]); rw_r_k = din("rw_r_k", [L, W])
    rw_ln_g = din("rw_ln_g", [L, W]); rw_ln_b = din("rw_ln_b", [L, W])
    cf_dw = din("cf_dw", [L, 31, W]); cf_dw_b = din("cf_dw_b", [L, W]); cf_ln_g = din("cf_ln_g", [L, W]); cf_ln_b = din("cf_ln_b", [L, W])
    lru_conv_w = din("lru_conv_w", [L, 4, W]); lru_conv_b = din("lru_conv_b", [L, W])
    lru_wa = din("lru_wa", [L, 8, 64, 64]); lru_ba = din("lru_ba", [L, W]); lru_wx = din("lru_wx", [L, 8, 64, 64]); lru_bx = din("lru_bx", [L, W])
    lru_lambda = din("lru_lambda", [L, W])
    w_branch = din("w_branch", [L, 4, W, D]); w_gate = din("w_gate", [L, D, 4 * D]); b_gate = din("b_gate", [L, 4 * D])
    w_out = din("w_out", [L, D, D]); w_mlp1 = din("w_mlp1", [L, D, 4 * D]); w_mlp2 = din("w_mlp2", [L, 4 * D, D])

    o_yp = dout("o_yp", [SEQ, D]); o_ys = dout("o_ys", [NSM, D])
    o_hg = dout("o_hg", [L, 1 + NSQ, 4, 128, 128]); o_rw = dout("o_rw", [L, 1 + NSQ, 8, 64, 64])
    o_sh = dout("o_sh", [L, 1 + NSQ, RW_COLS]); o_cf = dout("o_cf", [L, 1 + NSQ, 30, W])
    o_lh = dout("o_lh", [L, 1 + NSQ, W]); o_lc = dout("o_lc", [L, 1 + NSQ, 3, W])
    out_bufs = []

    def sb(name, shape, dt):
        return T(st.enter_context(nc.sbuf_tensor(name, list(shape), dt)), Buf(name))

    ARENA_E = 38 * 1024
    arena = st.enter_context(nc.sbuf_tensor("arena", [128, ARENA_E], BF16))
    ar = {"off": 0, "bufs": [], "fence": []}

    def aalloc(name, shape, dt, parts=128):
        n = 1
        for s_ in shape:
            n *= s_
        ne = n * (2 if dt == F32 else 1)
        ne = (ne + 15) // 16 * 16
        off = ar["off"]
        assert off + ne <= ARENA_E, (name, off, ne)
        ar["off"] = off + ne
        ar["hw"] = max(ar.get("hw", 0), off + ne)
        v = arena[0:parts, off:off + ne]
        if dt == F32:
            v = v.bitcast(F32)
        v = v[:, 0:n]
        if len(shape) == 1:
            v = v.rearrange("p (a b) -> p a b", a=1)
        elif len(shape) == 2:
            v = v.rearrange("p (a b) -> p a b", a=shape[0])
        elif len(shape) == 3:
            v = v.rearrange("p (a b c) -> p a b c", a=shape[0], b=shape[1])
        b = Buf(name, ar["fence"])
        ar["bufs"].append(b)
        return T(v, b)

    def areset():
        ops = []
        for b in ar["bufs"]:
            if b.last_w is not None:
                ops.append(b.last_w)
            ops.extend(b.readers)
        best = {}
        for o in ops + ar["fence"]:
            k, v = Prog._key(o)
            if k not in best or Prog._key(best[k])[1] < v:
                best[k] = o
        ar["fence"] = list(best.values())
        ar["bufs"] = []
        ar["off"] = 0

    psum = []
    for i in range(8):
        psum.append(T(st.enter_context(nc.psum_tensor("ps%d" % i, [128, 512], F32)), Buf("ps%d" % i)))
    pctr = [0]

    pheld = set()

    def PS(hold=False):
        assert len(pheld) < 8, 'all PSUM banks held'
        while (pctr[0] % 8) in pheld:
            pctr[0] += 1
        i = pctr[0] % 8
        pctr[0] += 1
        if hold:
            pheld.add(i)
        return psum[i]

    def PREL(t):
        pheld.discard(psum.index(t))

    def bl(ts_):
        return [t.b if isinstance(t, T) else t for t in ts_]

    def MM(out, lhsT, rhs, start, stop, R, Wr):
        P.op("pe", lambda e: e.matmul(out, lhsT=lhsT, rhs=rhs, start=start, stop=stop), bl(R), bl(Wr))

    def TR(out, in_, ident, R, Wr):
        P.op("pe", lambda e: e.transpose(out=out, in_=in_, identity=ident), bl(R), bl(Wr))

    def ACT(out, in_, func, R, Wr, scale=1.0, bias=0.0, eng="act"):
        P.op(eng, lambda e: e.activation(out=out, in_=in_, func=func, bias=bias, scale=scale), bl(R), bl(Wr))

    def TT(out, a, b, op, R, Wr, eng="dve"):
        P.op(eng, lambda e: e.tensor_tensor(out=out, in0=a, in1=b, op=op), bl(R), bl(Wr))

    def TSC(out, a, s1, s2, op0, op1, R, Wr, eng="dve"):
        if op1 is None:
            P.op(eng, lambda e: e.tensor_scalar(out=out, in0=a, scalar1=s1, scalar2=None, op0=op0), bl(R), bl(Wr))
        else:
            P.op(eng, lambda e: e.tensor_scalar(out=out, in0=a, scalar1=s1, scalar2=s2, op0=op0, op1=op1), bl(R), bl(Wr))

    def STT(out, a, s, b, op0, op1, R, Wr):
        P.op("dve", lambda e: e.scalar_tensor_tensor(out=out, in0=a, scalar=s, in1=b, op0=op0, op1=op1), bl(R), bl(Wr))

    def CP(out, in_, R, Wr, eng="dve"):
        if eng == "act":
            P.op("act", lambda e: e.copy(out=out, in_=in_), bl(R), bl(Wr))
        else:
            P.op(eng, lambda e: e.tensor_copy(out=out, in_=in_), bl(R), bl(Wr))

    def MSET(ap, val, Wr, eng="pool"):
        P.op(eng, lambda e: e.memset(ap, val), [], bl(Wr))

    def SCAN(out, d0, d1, init, R, Wr):
        P.op("dve", lambda e: e.tensor_tensor_scan(out=out, data0=d0, data1=d1, initial=init, op0=ALU.mult, op1=ALU.add), bl(R), bl(Wr))

    def RED(out, in_, R, Wr):
        P.op("dve", lambda e: e.tensor_reduce(out=out, in_=in_, axis=AX.X, op=ALU.add), bl(R), bl(Wr))

    def RCP(out, in_, R, Wr):
        P.op("dve", lambda e: e.reciprocal(out=out, in_=in_), bl(R), bl(Wr))

    dctr = [0]

    def DMA(out, in_, R, Wr, eng="sp", sem=None):
        if sem is None:
            sem = "g%d" % (dctr[0] % 12)
            dctr[0] += 1
        P.dma(eng, lambda e: e.dma_start(out=out, in_=in_), sem, bl(R), bl(Wr))

    def OUT(out, in_, R, name):
        b = Buf("out_" + name)
        out_bufs.append(b)
        sem = "o%d" % (dctr[0] % 8)
        dctr[0] += 1
        P.dma("sp", lambda e: e.dma_start(out=out, in_=in_), sem, bl(R), [b])

    def DBG(name, t, ap, shape):
        if debug is None or name not in debug:
            return
        d = dout("dbg_" + name, shape)
        dbg_outs[name] = shape
        OUT(d, ap, [t], "dbg_" + name)

    ident_f = sb("ident_f", [128, 128], F32)
    ident_b = sb("ident_b", [128, 128], BF16)
    ones_b = sb("ones_b", [128, 128], BF16)
    MSET(ident_f[:], 0.0, [ident_f])
    P.op("pool", lambda e: e.affine_select(out=ident_f[:], in_=ident_f[:], pattern=[[-1, 128]], compare_op=ALU.not_equal,
                                           fill=1.0, base=0, channel_multiplier=1), [ident_f.b], [ident_f.b])
    CP(ident_b[:], ident_f[:], [ident_f], [ident_b], eng="pool")
    MSET(ones_b[:], 1.0, [ones_b])
    mask_incl = sb("mask_incl", [64, 64], F32)
    mask_strict = sb("mask_strict", [64, 64], F32)
    for mt, base in ((mask_incl, 0), (mask_strict, -1)):
        MSET(mt[:], 1.0, [mt])
        P.op("pool", lambda e, mt=mt, base=base: e.affine_select(out=mt[:], in_=mt[:], pattern=[[1, 64]], compare_op=ALU.is_ge,
                                                                 fill=0.0, base=base, channel_multiplier=-1), [mt.b], [mt.b])
    cmask = sb("cmask", [128, NTMAX], BF16)
    MSET(cmask[:], 1.0, [cmask])
    MSET(cmask[:, 0:NPT].rearrange("p (c t) -> p c t", t=CH)[:, :, 0:1], 0.0, [cmask])
    MSET(cmask[:, NPT:NTMAX].rearrange("p (c t) -> p c t", t=TS)[:, :, 0:1], 0.0, [cmask])

    cmask_h = sb("cmask_h", [128, NTMAX], BF16)
    MSET(cmask_h[:], 1.0, [cmask_h])
    MSET(cmask_h[:, 0:NPT].rearrange("p (c t) -> p c t", t=32)[:, :, 0:1], 0.0, [cmask_h])
    MSET(cmask_h[:, NPT:NTMAX].rearrange("p (c t) -> p c t", t=TS)[:, :, 0:1], 0.0, [cmask_h])
    mask_incl_i = sb("mask_incl_i", [32, 8, 32], mybir.dt.uint8)
    CP(mask_incl_i[:], mask_incl[0:32, 0:32].unsqueeze(1).to_broadcast([32, 8, 32]), [mask_incl], [mask_incl_i], eng="pool")
    x = sb("x", [128, 8, NTMAX], F32)
    h = sb("h", [128, 8, NTMAX], BF16)
    y = sb("y", [128, 16, NTMAX], BF16)
    NSLOT = 4
    wslots = [sb("wslot%d" % i, [128, 4096], BF16) for i in range(NSLOT)]
    wctr = [0]
    mod = sb("mod", [128, L, 48, 1 + NSQ], F32)
    PCOLS = {}
    pc = [0]

    def pcol(name, n):
        PCOLS[name] = (pc[0], n)
        pc[0] += n

    for nm, n in (("ada_b", 48), ("norm_mix_g", 8), ("norm_mlp_g", 8), ("hg_lower", 4), ("hg_norm_g", 4), ("rw_mu", 14),
                  ("rw_w0", 4), ("rw_a0", 4), ("rw_k_k", 4), ("rw_k_a", 4), ("rw_r_k", 4), ("cf_dw", 124), ("cf_dw_b", 4),
                  ("cf_ln_g", 4), ("cf_ln_b", 4), ("lru_conv_w", 16), ("lru_conv_b", 4), ("lru_ba", 4), ("lru_bx", 4),
                  ("lru_lambda", 4), ("b_gate", 32), ("rw_ln_g", 4), ("rw_ln_b", 4)):
        pcol(nm, n)
    NPC = pc[0]
    ptab = sb("ptab", [128, L, NPC + 40], F32)
    DER = NPC

    def pv(l, name, j=0, n=1):
        o, _ = PCOLS[name]
        return ptab[:, l, o + j:o + j + n]

    def dv(l, j, n=1):
        return ptab[:, l, DER + j:DER + j + n]

    def WLOAD(src3, kc, cols):
        sl = wslots[wctr[0] % NSLOT]
        wctr[0] += 1
        v = sl.ap[:, 0:kc * cols].rearrange("p (k c) -> p k c", k=kc)
        P.dma("pool", lambda e: e.dma_start(out=v, in_=src3), "w%d" % ((wctr[0] - 1) % NSLOT), [], [sl.b])
        return T(v, sl.b)

    def kview(w2d, c0, cols):
        return w2d.rearrange("(k p) c -> p k c", p=128)[:, :, c0:c0 + cols]

    prow = [sb("prow%d" % i, [128, 128], F32) for i in range(3)]
    for l in range(L):
        plist = [("ada_b", ada_b[l]), ("norm_mix_g", norm_mix_g[l]), ("norm_mlp_g", norm_mlp_g[l]), ("hg_lower", hg_lower[l]),
                 ("hg_norm_g", hg_norm_g[l]), ("rw_mu", rw_mu[l]), ("rw_w0", rw_w0[l]), ("rw_a0", rw_a0[l]), ("rw_k_k", rw_k_k[l]),
                 ("rw_k_a", rw_k_a[l]), ("rw_r_k", rw_r_k[l]), ("cf_dw", cf_dw[l].rearrange("j c -> (j c)")), ("cf_dw_b", cf_dw_b[l]),
                 ("cf_ln_g", cf_ln_g[l]), ("cf_ln_b", cf_ln_b[l]), ("lru_conv_w", lru_conv_w[l].rearrange("j c -> (j c)")),
                 ("lru_conv_b", lru_conv_b[l]), ("lru_ba", lru_ba[l]), ("lru_bx", lru_bx[l]), ("lru_lambda", lru_lambda[l]),
                 ("b_gate", b_gate[l]), ("rw_ln_g", rw_ln_g[l]), ("rw_ln_b", rw_ln_b[l])]
        for nm, src in plist:
            o, n = PCOLS[nm]
            rows = src.rearrange("(r p) -> r p", p=128)
            r = 0
            while r < n:
                g = (o + r) // 128
                take = min(n - r, 128 - (o + r) % 128)
                DMA(prow[g][(o + r) % 128:(o + r) % 128 + take, :], rows[r:r + take, :], [], [prow[g]])
                r += take
        for g in range(3):
            nr = min(128, NPC - g * 128)
            ps = PS()
            TR(ps[:, 0:nr], prow[g][0:nr, :], ident_f[0:nr, 0:nr], [prow[g], ident_f], [ps])
            CP(ptab[:, l, g * 128:g * 128 + nr], ps[:, 0:nr], [ps], [ptab])
    for l in range(L):
        if l == 0:
            MSET(dv(0, 0, 4), 0.0, [ptab], eng="dve")
        else:
            TT(dv(l, 0, 4), pv(l, "hg_lower", 0, 4), pv(0, "hg_lower", 0, 4), ALU.subtract, [ptab], [ptab])
            ACT(dv(l, 0, 4), dv(l, 0, 4), AF.Sigmoid, [ptab], [ptab])
        TSC(dv(l, 4, 4), dv(l, 0, 4), -1.0, 1.0, ALU.mult, ALU.add, [ptab], [ptab])
        ACT(dv(l, 8, 4), pv(l, "lru_lambda", 0, 4), AF.Exp, [ptab], [ptab], scale=-1.0)
        ACT(dv(l, 8, 4), dv(l, 8, 4), AF.Ln, [ptab], [ptab], bias=1.0)
        TSC(dv(l, 12, 4), dv(l, 8, 4), -16.0, None, ALU.mult, None, [ptab], [ptab])
        TSC(dv(l, 8, 4), dv(l, 8, 4), -8.0, None, ALU.mult, None, [ptab], [ptab])
        TSC(dv(l, 16, 14), pv(l, "rw_mu", 0, 14), -1.0, 1.0, ALU.mult, ALU.add, [ptab], [ptab])
    gfin = sb("gfin", [128, D], F32)
    DMA(gfin[:], norm_final_g.rearrange("(o d) -> o d", o=1).to_broadcast([128, D]), [], [gfin])
    rwup = sb("rwup", [128, L, 3, W], BF16)
    lrug = sb("lrug", [128, L, 2, 4, 128], BF16)
    MSET(lrug[:], 0.0, [lrug])
    for l in range(L):
        DMA(rwup[0:64, l, 0, :], rw_w_up[l], [], [rwup], eng="pool", sem="rs")
        DMA(rwup[64:128, l, 1, :], rw_a_up[l], [], [rwup], eng="pool", sem="rs")
        DMA(rwup[:, l, 2, :], rw_g_up[l], [], [rwup], eng="pool", sem="rs")
        for gi, wsrc in enumerate((lru_wa, lru_wx)):
            for n in range(8):
                j, hf = n // 2, n % 2
                DMA(lrug[hf * 64:hf * 64 + 64, l, gi, j, hf * 64:hf * 64 + 64], wsrc[l, n], [], [lrug], eng="pool", sem="rs")

    ctok = aalloc("ctok", [D], F32, parts=1 + NSQ)
    cT = sb("cT", [128, 8, 1 + NSQ], BF16)
    DMA(ctok[:, 0, :], cc, [], [ctok])
    ps = PS()
    for kc in range(8):
        TR(ps[:, kc * 17:(kc + 1) * 17], ctok[:, 0, kc * 128:(kc + 1) * 128], ident_f[0:17, 0:17], [ctok, ident_f], [ps])
    ACT(cT[:].rearrange("p a b -> p (a b)"), ps[:, 0:136], AF.Silu, [ps], [cT])
    modA = sb("modA", [128, L, 2, 8, 1 + NSQ], F32)
    ada_state = {l: 0 for l in range(L)}

    def ADA_PIECES(l, n):
        while n > 0 and ada_state[l] < 12:
            g = ada_state[l]
            ada_state[l] += 1
            n -= 1
            wt = WLOAD(kview(ada_w[l], g * 512, 512), 8, 512)
            ps = PS()
            for j in range(4):
                for kc in range(8):
                    MM(ps[:, j * 17:(j + 1) * 17], wt[:, kc, j * 128:(j + 1) * 128], cT[:, kc, :], kc == 0, kc == 7, [wt, cT], [ps])
            for j in range(4):
                fc = g * 4 + j
                ACT(mod[:, l, fc, :], ps[:, j * 17:(j + 1) * 17], AF.Identity, [ps, ptab], [mod], bias=pv(l, "ada_b", fc))
            if ada_state[l] == 12:
                for which, gname, sc0 in ((0, "norm_mix_g", 8), (1, "norm_mlp_g", 32)):
                    for kc in range(8):
                        TSC(modA[:, l, which, kc, :], mod[:, l, sc0 + kc, :], 1.0, pv(l, gname, kc), ALU.add, ALU.mult, [mod, ptab], [modA])

    ADA_PIECES(0, 12)
    areset()

    hgS = sb("hgS", [128, L, 4, 128], F32)
    rwST = sb("rwST", [128, L, 4, 64], F32)
    rwprev = sb("rwprev", [128, L, 14], F32)
    cfhist = sb("cfhist", [128, L, 4, 30], BF16)
    lruh = sb("lruh", [128, L, 4], F32)
    lruhist = sb("lruhist", [128, L, 4, 3], F32)
    for t_ in (hgS, rwST, rwprev, cfhist, lruh, lruhist):
        MSET(t_[:], 0.0, [t_], eng="dve")
    shcol = sb("shcol", [128, L, 14, 1 + NSQ], F32)
    lhcol = sb("lhcol", [128, L, 4, 1 + NSQ], F32)

    def slabs(NT):
        return [(0, NPT)] + ([(NPT, NSM)] if NT > NPT else [])

    def NORM(l, which, NT):
        sh0 = 0 if which == 0 else 24
        for (t0, n) in slabs(NT):
            sq = aalloc("sq", [8, n], BF16)
            ACT(sq[:], x[:, :, t0:t0 + n], AF.Square, [x], [sq])
            ps = PS()
            for kc in range(8):
                MM(ps[:, 0:n], ones_b[:], sq[:, kc, :], kc == 0, kc == 7, [ones_b, sq], [ps])
            rstd = aalloc("rstd", [n], F32)
            ACT(rstd[:, 0, :], ps[:, 0:n], AF.Ln, [ps], [rstd], scale=1.0 / D, bias=EPS)
            ACT(rstd[:, 0, :], rstd[:, 0, :], AF.Exp, [rstd], [rstd], scale=-0.5)
            if t0 == 0:
                for kc in range(8):
                    xk = aalloc("xn%d" % kc, [n], F32)
                    TT(xk[:, 0, :], x[:, kc, t0:t0 + n], rstd[:, 0, :], ALU.mult, [x, rstd], [xk])
                    ACT(h[:, kc, 0:n], xk[:, 0, :], AF.Identity, [xk, modA, mod], [h],
                        scale=modA[:, l, which, kc, 0:1], bias=mod[:, l, sh0 + kc, 0:1])
                continue
            xn = aalloc("xn", [8, n], F32)
            TT(xn[:], x[:, :, t0:t0 + n], rstd[:, 0:1, :].to_broadcast([128, 8, n]), ALU.mult, [x, rstd], [xn])
            for kc in range(8):
                if t0 == 0:
                    TSC(h[:, kc, 0:n], xn[:, kc, :], modA[:, l, which, kc, 0:1], mod[:, l, sh0 + kc, 0:1], ALU.mult, ALU.add,
                        [xn, modA, mod], [h])
                else:
                    v3 = xn[:, kc, :].rearrange("p (q t) -> p q t", t=TS)
                    TT(v3, v3, modA[:, l, which, kc, 1:1 + NSQ].unsqueeze(2).to_broadcast([128, NSQ, TS]), ALU.mult, [xn, modA], [xn])
                    TT(h[:, kc, t0:t0 + n].rearrange("p (q t) -> p q t", t=TS), v3,
                       mod[:, l, sh0 + kc, 1:1 + NSQ].unsqueeze(2).to_broadcast([128, NSQ, TS]), ALU.add, [xn, mod], [h])

    def RESID(l, g0, dc, psb, NT):
        for (t0, n), (ps, c0) in zip(slabs(NT), psb):
            if t0 == 0:
                STT(x[:, dc, 0:n], ps[:, c0:c0 + n], mod[:, l, g0 + dc, 0:1], x[:, dc, 0:n], ALU.mult, ALU.add, [ps, mod, x], [x])
            else:
                tmp = aalloc("rtmp", [n], F32)
                TT(tmp[:, 0, :].rearrange("p (q t) -> p q t", t=TS), ps[:, c0:c0 + n].rearrange("p (q t) -> p q t", t=TS),
                   mod[:, l, g0 + dc, 1:1 + NSQ].unsqueeze(2).to_broadcast([128, NSQ, TS]), ALU.mult, [ps, mod], [tmp])
                TT(x[:, dc, t0:t0 + n], x[:, dc, t0:t0 + n], tmp[:, 0, :], ALU.add, [x, tmp], [x])

    def PROJ(wt, j, NT, rhs_t, nk, ps, c0=0):
        for kc in range(nk):
            MM(ps[:, c0:c0 + NT], wt[:, kc, j * 128:(j + 1) * 128], rhs_t[:, kc, 0:NT], kc == 0, kc == nk - 1, [wt, rhs_t], [ps])

    def MERGE_MLP(l, NT):
        sl = slabs(NT)
        mg = aalloc("mg", [8, NT], F32)
        mgb = aalloc("mgb", [8, NT], BF16)
        for b in range(4):
            wb = WLOAD(w_branch[l, b].rearrange("(k p) c -> p k c", p=128), 4, D)
            for half in range(2):
                wg = WLOAD(kview(w_gate[l], b * D + half * 512, 512), 8, 512)
                for jj in range(4):
                    dc = half * 4 + jj
                    for (t0, n) in sl:
                        pb = PS()
                        for kc in range(4):
                            MM(pb[:, 0:n], wb[:, kc, dc * 128:(dc + 1) * 128], y[:, b * 4 + kc, t0:t0 + n], kc == 0, kc == 3, [wb, y], [pb])
                        pg = PS()
                        for kc in range(8):
                            MM(pg[:, 0:n], wg[:, kc, jj * 128:(jj + 1) * 128], h[:, kc, t0:t0 + n], kc == 0, kc == 7, [wg, h], [pg])
                        sg = aalloc("sg", [n], F32) if False else None
                        sgt = sgbuf
                        ACT(sgt[:, 0:n], pg[:, 0:n], AF.Sigmoid, [pg, ptab], [sgt], bias=pv(l, "b_gate", b * 8 + dc))
                        if b == 0:
                            TT(mg[:, dc, t0:t0 + n], sgt[:, 0:n], pb[:, 0:n], ALU.mult, [sgt, pb], [mg])
                        else:
                            TT(sgt[:, 0:n], sgt[:, 0:n], pb[:, 0:n], ALU.mult, [sgt, pb], [sgt])
                            if b < 3:
                                TT(mg[:, dc, t0:t0 + n], mg[:, dc, t0:t0 + n], sgt[:, 0:n], ALU.add, [mg, sgt], [mg])
                            else:
                                TT(mgb[:, dc, t0:t0 + n], mg[:, dc, t0:t0 + n], sgt[:, 0:n], ALU.add, [mg, sgt], [mgb])
        for half in range(2):
            wo = WLOAD(kview(w_out[l], half * 512, 512), 8, 512)
            for jj in range(4):
                dc = half * 4 + jj
                psb = []
                for (t0, n) in sl:
                    po = PS()
                    for kc in range(8):
                        MM(po[:, 0:n], wo[:, kc, jj * 128:(jj + 1) * 128], mgb[:, kc, t0:t0 + n], kc == 0, kc == 7, [wo, mgb], [po])
                    psb.append((po, 0))
                RESID(l, 16, dc, psb, NT)
        areset()
        NORM(l, 1, NT)
        areset()
        hid = aalloc("hid", [32, NT], BF16)
        for pi in range(8):
            w1 = WLOAD(kview(w_mlp1[l], pi * 512, 512), 8, 512)
            for jj in range(4):
                for (t0, n) in sl:
                    pp = PS()
                    for kc in range(8):
                        MM(pp[:, 0:n], w1[:, kc, jj * 128:(jj + 1) * 128], h[:, kc, t0:t0 + n], kc == 0, kc == 7, [w1, h], [pp])
                    ACT(sgbuf[:, 0:n], pp[:, 0:n], AF.Relu, [pp], [sgbuf])
                    TT(hid[:, pi * 4 + jj, t0:t0 + n], sgbuf[:, 0:n], sgbuf[:, 0:n], ALU.mult, [sgbuf], [hid])
        for cb in range(2):
            for kb in range(4):
                w2 = WLOAD(w_mlp2[l].rearrange("(k p) c -> p k c", p=128)[:, kb * 8:(kb + 1) * 8, cb * 512:(cb + 1) * 512], 8, 512)
                for jj in range(4):
                    for kc in range(8):
                        first = (kb == 0 and kc == 0)
                        last = (kb == 3 and kc == 7)
                        MM(psum[jj][:, 0:NPT], w2[:, kc, jj * 128:(jj + 1) * 128], hid[:, kb * 8 + kc, 0:NPT], first, last, [w2, hid], [psum[jj]])
                        if NT > NPT:
                            MM(psum[4 + jj][:, 0:NSM], w2[:, kc, jj * 128:(jj + 1) * 128], hid[:, kb * 8 + kc, NPT:NT], first, last,
                               [w2, hid], [psum[4 + jj]])
            for jj in range(4):
                RESID(l, 40, cb * 4 + jj, [(psum[jj], 0), (psum[4 + jj], 0)], NT)
        pctr[0] = 0
        areset()

    sgbuf = sb("sgbuf", [128, NPT], F32)

    xtmp = sb("xtmp", [128, NSM], F32)
    sgd2 = sb("sgd2", [128, NTMAX], BF16)

    def arelease(mark):
        ops = []
        for b in ar["bufs"]:
            if b.last_w is not None:
                ops.append(b.last_w)
            ops.extend(b.readers)
        best = {}
        for o in ops + ar["fence"]:
            k, v = Prog._key(o)
            if k not in best or Prog._key(best[k])[1] < v:
                best[k] = o
        ar["fence"] = list(best.values())
        ar["off"] = mark

    diag = [sb("diag%d" % i, [128, 128], BF16) for i in range(6)]
    dgc = [0]
    GC = 0.7978845608028654

    def chunks_of(NT, CC=CH):
        lst = [(c * CC, CC, "p", c) for c in range(NPT // CC)]
        if NT > NPT:
            lst += [(NPT + q * TS, TS, "s", q) for q in range(NSQ)]
        return lst

    def bview(ap2, n, C):
        return ap2.rearrange("p (c t) -> p c t", t=C)

    def TOK_OUT(src_t, src_ap, ncols, dst_rows, nm):
        ps = PS()
        for j in range(4):
            TR(ps[0:ncols, j * 128:(j + 1) * 128], src_ap(j), ident_f[:], [src_t[j] if isinstance(src_t, list) else src_t, ident_f], [ps])
        stg = aalloc("stg_" + nm, [W], F32)
        CP(stg[0:ncols, 0, :], ps[0:ncols, :], [ps], [stg], eng="act")
        for (r0, r1, dst) in dst_rows:
            OUT(dst, stg[r0:r1, 0, :], [stg], nm)

    def MIX_CF(ti, l, NT):
        last = ti == NTILE - 1
        sl = slabs(NT)
        wv = WLOAD(kview(w_in[l], 3840, 512), 8, 512)
        wg = WLOAD(kview(w_in[l], 4352, 512), 8, 512)
        u32 = aalloc("u32", [4, NT], F32)
        uxp = aalloc("uxp", [4, 30 + NPT], BF16)
        uxs = aalloc("uxs", [4, NSQ * 38], BF16) if last else None
        if last:
            for g in range(4):
                hst = aalloc("hst%d" % g, [W], F32, parts=120)
                DMA(hst[:, 0, :], s_cf[l, g * 4:(g + 1) * 4].rearrange("q r c -> (q r) c"), [], [hst])
                ps = PS()
                for j in range(4):
                    TR(ps[:, j * 120:(j + 1) * 120], hst[:, 0, j * 128:(j + 1) * 128], ident_f[0:120, 0:120], [hst, ident_f], [ps])
                for j in range(4):
                    CP(bview(uxs[:, j, :], NSQ, 38)[:, g * 4:(g + 1) * 4, 0:30], bview(ps[:, j * 120:(j + 1) * 120], 4, 30), [ps], [uxs],
                       eng=("act" if j % 2 else "dve"))
            OUT(o_cf[l, 1:1 + NSQ, 0:22, :], s_cf[l, :, 8:30, :], [], "cfcopy")
        for j in range(4):
            CP(uxp[:, j, 0:30], cfhist[:, l, j, :], [cfhist], [uxp])
            for (t0, n) in sl:
                p1 = PS(); p2 = PS()
                for kc in range(8):
                    MM(p1[:, 0:n], wv[:, kc, j * 128:(j + 1) * 128], h[:, kc, t0:t0 + n], kc == 0, kc == 7, [wv, h], [p1])
                for kc in range(8):
                    MM(p2[:, 0:n], wg[:, kc, j * 128:(j + 1) * 128], h[:, kc, t0:t0 + n], kc == 0, kc == 7, [wg, h], [p2])
                ACT(sgbuf[:, 0:n], p2[:, 0:n], AF.Sigmoid, [p2], [sgbuf])
                TT(u32[:, j, t0:t0 + n], sgbuf[:, 0:n], p1[:, 0:n], ALU.mult, [sgbuf, p1], [u32])
                if t0 == 0:
                    CP(uxp[:, j, 30:30 + NPT], u32[:, j, 0:NPT], [u32], [uxp], eng="act")
                else:
                    CP(bview(uxs[:, j, :], NSQ, 38)[:, :, 30:38], bview(u32[:, j, NPT:NT], NSQ, TS), [u32], [uxs], eng="act")
            CP(cfhist[:, l, j, :], uxp[:, j, NPT:NPT + 30], [uxp], [cfhist])
        yc = aalloc("yc", [4, NT], F32)
        ycb = aalloc("ycb", [4, NT], BF16)
        ycs = aalloc("ycs", [4, NT], BF16)
        for j in range(4):
            p1 = PS(); p2 = PS() if last else None
            for tap in range(31):
                dg = diag[dgc[0] % 6]; dgc[0] += 1
                ACT(dg[:], ident_b[:], AF.Identity, [ident_b, ptab], [dg], scale=pv(l, "cf_dw", tap * 4 + j))
                MM(p1[:, 0:NPT], dg[:], uxp[:, j, tap:tap + NPT], tap == 0, tap == 30, [dg, uxp], [p1])
                if last:
                    MM(p2[:, 0:NSM], dg[:], bview(uxs[:, j, :], NSQ, 38)[:, :, tap:tap + TS], tap == 0, tap == 30, [dg, uxs], [p2])
            for (t0, n), pp in zip(sl, (p1, p2)):
                ACT(yc[:, j, t0:t0 + n], pp[:, 0:n], AF.Identity, [pp, ptab], [yc], bias=pv(l, "cf_dw_b", j))
                CP(ycb[:, j, t0:t0 + n], yc[:, j, t0:t0 + n], [yc], [ycb])
                ACT(ycs[:, j, t0:t0 + n], yc[:, j, t0:t0 + n], AF.Square, [yc], [ycs])
        for (t0, n) in sl:
            pm = PS(); pq = PS()
            for j in range(4):
                MM(pm[:, 0:n], ones_b[:], ycb[:, j, t0:t0 + n], j == 0, j == 3, [ones_b, ycb], [pm])
            for j in range(4):
                MM(pq[:, 0:n], ones_b[:], ycs[:, j, t0:t0 + n], j == 0, j == 3, [ones_b, ycs], [pq])
            mean = aalloc("cfmean", [n], F32)
            var = aalloc("cfvar", [n], F32)
            ACT(mean[:, 0, :], pm[:, 0:n], AF.Copy, [pm], [mean], scale=1.0 / W)
            TT(var[:, 0, :], mean[:, 0, :], mean[:, 0, :], ALU.mult, [mean], [var])
            STT(var[:, 0, :], pq[:, 0:n], 1.0 / W, var[:, 0, :], ALU.mult, ALU.subtract, [pq, var], [var])
            ACT(var[:, 0, :], var[:, 0, :], AF.Ln, [var], [var], bias=1e-5)
            ACT(var[:, 0, :], var[:, 0, :], AF.Exp, [var], [var], scale=-0.5)
            for j in range(4):
                TT(yc[:, j, t0:t0 + n], yc[:, j, t0:t0 + n], mean[:, 0, :], ALU.subtract, [yc, mean], [yc])
                TT(yc[:, j, t0:t0 + n], yc[:, j, t0:t0 + n], var[:, 0, :], ALU.mult, [yc, var], [yc])
                ACT(y[:, 8 + j, t0:t0 + n], yc[:, j, t0:t0 + n], AF.Silu, [yc, ptab], [y], scale=pv(l, "cf_ln_g", j), bias=pv(l, "cf_ln_b", j))
        if last:
            TOK_OUT(u32, lambda j: u32[:, j, NPT - 30:NPT], 30, [(0, 30, o_cf[l, 0])], "cfp")
            TOK_OUT(u32, lambda j: u32[:, j, NPT:NT], NSM, [(q * TS, (q + 1) * TS, o_cf[l, 1 + q, 22:30, :]) for q in range(NSQ)], "cfs")

    def MIX_LRU(ti, l, NT):
        last = ti == NTILE - 1
        sl = slabs(NT)
        wx_ = WLOAD(kview(w_in[l], 4864, 512), 8, 512)
        wgl = WLOAD(kview(w_in[l], 5376, 512), 8, 512)
        xl32 = [aalloc("xl32_%d" % j, [NT], F32) for j in range(4)]
        exp_ = [aalloc("lext%d" % j, [3 + NPT], F32) for j in range(4)]
        exs = aalloc("lexs", [4, NSQ * 11], F32) if last else None
        xc = [aalloc("lxc%d" % j, [NT], F32) for j in range(4)]
        xcb = [aalloc("lxcb%d" % j, [NT], BF16) for j in range(4)]
        hs = [aalloc("lhs%d" % j, [NT], F32) for j in range(4)]
        tsets = [(aalloc("lta%d" % i, [NT], F32), aalloc("ltb%d" % i, [NT], F32), aalloc("ltc%d" % i, [NT], F32)) for i in range(1 if last else 2)]
        hs0 = aalloc("lhs0", [4, NSQ], F32) if last else None
        if last:
            hst = aalloc("lhst", [W], F32, parts=48)
            DMA(hst[:, 0, :], s_lc[l].rearrange("q r c -> (q r) c"), [], [hst])
            ps = PS()
            for j in range(4):
                TR(ps[:, j * 48:(j + 1) * 48], hst[:, 0, j * 128:(j + 1) * 128], ident_f[0:48, 0:48], [hst, ident_f], [ps])
            for j in range(4):
                CP(bview(exs[:, j, :], NSQ, 11)[:, :, 0:3], bview(ps[:, j * 48:(j + 1) * 48], NSQ, 3), [ps], [exs])
            hh = aalloc("lhh", [W], F32, parts=NSQ)
            DMA(hh[:, 0, :], s_lh[l], [], [hh])
            ps = PS()
            for j in range(4):
                TR(ps[:, j * 16:(j + 1) * 16], hh[:, 0, j * 128:(j + 1) * 128], ident_f[0:16, 0:16], [hh, ident_f], [ps])
            CP(hs0[:].rearrange("p a b -> p (a b)"), ps[:, 0:64], [ps], [hs0])
        def LJ(j, ta, tb, tcc):
            CP(exp_[j][:, 0, 0:3], lruhist[:, l, j, :], [lruhist], [exp_[j]])
            for (t0, n) in sl:
                p1 = PS()
                for kc in range(8):
                    MM(p1[:, 0:n], wx_[:, kc, j * 128:(j + 1) * 128], h[:, kc, t0:t0 + n], kc == 0, kc == 7, [wx_, h], [p1])
                CP(xl32[j][:, 0, t0:t0 + n], p1[:, 0:n], [p1], [xl32[j]], eng="act")
                if t0 == 0:
                    CP(exp_[j][:, 0, 3:3 + NPT], xl32[j][:, 0, 0:NPT], [xl32[j]], [exp_[j]])
                    src = lambda tap: exp_[j][:, 0, tap:tap + NPT]
                    dst = xc[j][:, 0, 0:NPT]
                    rd = exp_[j]
                else:
                    CP(bview(exs[:, j, :], NSQ, 11)[:, :, 3:11], bview(xl32[j][:, 0, NPT:NT], NSQ, TS), [xl32[j]], [exs])
                    src = lambda tap: bview(exs[:, j, :], NSQ, 11)[:, :, tap:tap + TS]
                    dst = bview(xc[j][:, 0, NPT:NT], NSQ, TS)
                    rd = exs
                TSC(dst, src(0), pv(l, "lru_conv_w", 0 * 4 + j), pv(l, "lru_conv_b", j), ALU.mult, ALU.add, [rd, ptab], [xc[j]])
                for tap in range(1, 4):
                    STT(dst, src(tap), pv(l, "lru_conv_w", tap * 4 + j), dst, ALU.mult, ALU.add, [rd, ptab, xc[j]], [xc[j]])
            CP(lruhist[:, l, j, :], exp_[j][:, 0, NPT:NPT + 3], [exp_[j]], [lruhist])
            yield
            CP(xcb[j][:, 0, 0:NT], xc[j][:, 0, 0:NT], [xc[j]], [xcb[j]], eng="act")
            for (t0, n) in sl:
                pa = PS(); px = PS(); pgl = PS()
                MM(pa[:, 0:n], lrug[:, l, 0, j, :], xcb[j][:, 0, t0:t0 + n], True, True, [lrug, xcb[j]], [pa])
                MM(px[:, 0:n], lrug[:, l, 1, j, :], xcb[j][:, 0, t0:t0 + n], True, True, [lrug, xcb[j]], [px])
                for kc in range(8):
                    MM(pgl[:, 0:n], wgl[:, kc, j * 128:(j + 1) * 128], h[:, kc, t0:t0 + n], kc == 0, kc == 7, [wgl, h], [pgl])
                A_ = ta[:, 0, t0:t0 + n]; B_ = tb[:, 0, t0:t0 + n]; C_ = tcc[:, 0, t0:t0 + n]
                yield
                ACT(C_, pa[:, 0:n], AF.Sigmoid, [pa, ptab], [tcc], bias=pv(l, "lru_ba", j))
                ACT(A_, C_, AF.Exp, [tcc, ptab], [ta], scale=dv(l, 8 + j))
                ACT(C_, C_, AF.Exp, [tcc, ptab], [tcc], scale=dv(l, 12 + j))
                ACT(C_, C_, AF.Sqrt, [tcc], [tcc], scale=-1.0, bias=1.0)
                ACT(B_, px[:, 0:n], AF.Sigmoid, [px, ptab], [tb], bias=pv(l, "lru_bx", j))
                yield
                TT(B_, B_, xc[j][:, 0, t0:t0 + n], ALU.mult, [tb, xc[j]], [tb])
                TT(B_, B_, C_, ALU.mult, [tb, tcc], [tb])
                if t0 == 0:
                    SCAN(hs[j][:, 0, 0:n], A_, B_, lruh[:, l, j:j + 1], [ta, tb, lruh], [hs[j]])
                    CP(lruh[:, l, j:j + 1], hs[j][:, 0, n - 1:n], [hs[j]], [lruh])
                else:
                    a3 = bview(A_, NSQ, TS); b3 = bview(B_, NSQ, TS)
                    TT(C_[:, 0:NSQ], a3[:, :, 0], hs0[:, j, :], ALU.mult, [ta, hs0], [tcc])
                    TT(b3[:, :, 0], b3[:, :, 0], C_[:, 0:NSQ], ALU.add, [tb, tcc], [tb])
                    MSET(a3[:, :, 0], 0.0, [ta], eng="dve")
                    SCAN(hs[j][:, 0, t0:t0 + n], A_, B_, 0.0, [ta, tb], [hs[j]])
                yield
                ACT(A_, pgl[:, 0:n], AF.Copy, [pgl], [ta])
                TT(B_, A_, A_, ALU.mult, [ta], [tb])
                TSC(B_, B_, 2.0 * GC * 0.044715, 2.0 * GC, ALU.mult, ALU.add, [tb], [tb])
                TT(B_, B_, A_, ALU.mult, [tb, ta], [tb])
                ACT(B_, B_, AF.Sigmoid, [tb], [tb])
                TT(B_, B_, A_, ALU.mult, [tb, ta], [tb])
                TT(y[:, 12 + j, t0:t0 + n], hs[j][:, 0, t0:t0 + n], B_, ALU.mult, [hs[j], tb], [y])
            if last:
                CP(lhcol[:, l, j, 0:1], hs[j][:, 0, NPT - 1:NPT], [hs[j]], [lhcol])
                CP(lhcol[:, l, j, 1:1 + NSQ], bview(hs[j][:, 0, NPT:NT], NSQ, TS)[:, :, TS - 1], [hs[j]], [lhcol])
        if len(tsets) == 2:
            for (ja, jb) in ((0, 1), (2, 3)):
                gens = [LJ(ja, *tsets[0]), LJ(jb, *tsets[1])]
                alive = [True, True]
                while any(alive):
                    for gi in range(2):
                        if alive[gi]:
                            try:
                                next(gens[gi])
                            except StopIteration:
                                alive[gi] = False
        else:
            for j in range(4):
                for _ in LJ(j, *tsets[0]):
                    pass
        if last:
            TOK_OUT(xl32, lambda j: xl32[j][:, 0, NPT - 3:NPT], 3, [(0, 3, o_lc[l, 0])], "lcp")
            TOK_OUT(xl32, lambda j: xl32[j][:, 0, NPT:NT], NSM, [(q * TS + 5, q * TS + 8, o_lc[l, 1 + q]) for q in range(NSQ)], "lcs")
    def MIX_HG(ti, l, NT):
        last = ti == NTILE - 1
        sl = slabs(NT)
        HC = 32
        chs = chunks_of(NT, HC)
        ncp = NPT // HC
        nch = len(chs)
        wq = WLOAD(kview(w_in[l], 0, 512), 8, 512)
        wf = WLOAD(kview(w_in[l], 512, 512), 8, 512)
        wi = WLOAD(kview(w_in[l], 1024, 512), 8, 512)
        wo_ = WLOAD(kview(w_in[l], 1536, 512), 8, 512)
        t1 = aalloc("hg_t1", [NT], F32); t2 = aalloc("hg_t2", [NT], F32); t3 = aalloc("hg_t3", [NT], F32)
        t4 = aalloc("hg_t4", [NT], F32)
        qbc = aalloc("hg_qbc", [NT], BF16); kbc = aalloc("hg_kbc", [NT], BF16); kdc = aalloc("hg_kdc", [NT], BF16)
        st_ = aalloc("hg_st", [4, nch], F32)
        vtok = aalloc("hg_vtok", [nch, 128], BF16, parts=HC)
        kdtok = aalloc("hg_kdtok", [nch, 128], BF16, parts=HC)
        scm = aalloc("hg_scm", [nch, HC], BF16, parts=HC)
        MSET(scm[:], 0.0, [scm], eng="dve")
        Sall = aalloc("hg_Sall", [17, 128], F32)
        Sbf = aalloc("hg_Sbf", [16, 128], BF16)
        pSsb = aalloc("hg_pSsb", [16, 128], F32)
        o32 = aalloc("hg_o32", [NT], F32)
        osq = aalloc("hg_osq", [NT], BF16)
        def emit_proj(hd):
            lst = []
            for (t0, n) in sl:
                p1 = PS(hold=True); p2 = PS(hold=True)
                for kc in range(8):
                    MM(p1[:, 0:n], wq[:, kc, hd * 128:(hd + 1) * 128], h[:, kc, t0:t0 + n], kc == 0, kc == 7, [wq, h], [p1])
                for kc in range(8):
                    MM(p2[:, 0:n], wf[:, kc, hd * 128:(hd + 1) * 128], h[:, kc, t0:t0 + n], kc == 0, kc == 7, [wf, h], [p2])
                lst.append((p1, p2, t0, n))
            return lst
        pend_proj = emit_proj(0)
        for hd in range(4):
            A_ = t1[:, 0, :]; B_ = t2[:, 0, :]; C_ = t3[:, 0, :]; D_ = t4[:, 0, :]
            for (p1, p2, t0, n) in pend_proj:
                ACT(A_[:, t0:t0 + n], p1[:, 0:n], AF.Silu, [p1], [t1])
                PREL(p1)
                ACT(B_[:, t0:t0 + n], p2[:, 0:n], AF.Sigmoid, [p2], [t2])
                PREL(p2)
            TSC(B_[:, 0:NT], B_[:, 0:NT], dv(l, 4 + hd), dv(l, hd), ALU.mult, ALU.add, [t2, ptab], [t2])
            ACT(C_[:, 0:NT], B_[:, 0:NT], AF.Ln, [t2], [t3])
            TSC(B_[:, 0:NT], B_[:, 0:NT], -1.0, 1.0, ALU.mult, ALU.add, [t2], [t2])
            SCAN(D_[:, 0:NT], cmask_h[:, 0:NT], C_[:, 0:NT], 0.0, [cmask_h, t3], [t4])
            bp = bview(D_[:, 0:NPT], ncp, HC)
            CP(st_[:, 0, 0:ncp], bp[:, :, HC // 2 - 1], [t4], [st_])
            CP(st_[:, 1, 0:ncp], bp[:, :, HC - 1], [t4], [st_])
            if last:
                bs_ = bview(D_[:, NPT:NT], NSQ, TS)
                CP(st_[:, 0, ncp:nch], bs_[:, :, TS // 2 - 1], [t4], [st_])
                CP(st_[:, 1, ncp:nch], bs_[:, :, TS - 1], [t4], [st_])
            TT(st_[:, 3, :], st_[:, 1, :], st_[:, 0, :], ALU.subtract, [st_], [st_])
            ACT(st_[:, 1:4, :], st_[:, 1:4, :], AF.Exp, [st_], [st_]) if False else None
            ACT(st_[:, 2, :], st_[:, 0, :], AF.Exp, [st_], [st_])
            ACT(st_[:, 1, :], st_[:, 1, :], AF.Exp, [st_], [st_])
            ACT(st_[:, 3, :], st_[:, 3, :], AF.Exp, [st_], [st_])
            TT(bp, bp, st_[:, 0, 0:ncp].unsqueeze(2).to_broadcast([128, ncp, HC]), ALU.subtract, [t4, st_], [t4])
            if last:
                TT(bs_, bs_, st_[:, 0, ncp:nch].unsqueeze(2).to_broadcast([128, NSQ, TS]), ALU.subtract, [t4, st_], [t4])
            ACT(C_[:, 0:NT], D_[:, 0:NT], AF.Exp, [t4], [t3])
            TT(qbc[:, 0, 0:NT], A_[:, 0:NT], C_[:, 0:NT], ALU.mult, [t1, t3], [qbc])
            ACT(C_[:, 0:NT], D_[:, 0:NT], AF.Exp, [t4], [t3], scale=-1.0)
            TT(kbc[:, 0, 0:NT], B_[:, 0:NT], C_[:, 0:NT], ALU.mult, [t2, t3], [kbc])
            TT(bview(kdc[:, 0, 0:NPT], ncp, HC), bview(kbc[:, 0, 0:NPT], ncp, HC),
               st_[:, 3, 0:ncp].unsqueeze(2).to_broadcast([128, ncp, HC]), ALU.mult, [kbc, st_], [kdc])
            if last:
                TT(bview(kdc[:, 0, NPT:NT], NSQ, TS), bview(kbc[:, 0, NPT:NT], NSQ, TS),
                   st_[:, 3, ncp:nch].unsqueeze(2).to_broadcast([128, NSQ, TS]), ALU.mult, [kbc, st_], [kdc])
            for g0 in range(0, nch, 4):
                grp = chs[g0:g0 + 4]
                pv_ = PS()
                for gi, (t0, C, kind, ci) in enumerate(grp):
                    for kc in range(8):
                        MM(pv_[0:C, gi * 128:(gi + 1) * 128], h[:, kc, t0:t0 + C], wi[:, kc, hd * 128:(hd + 1) * 128], kc == 0, kc == 7, [h, wi], [pv_])
                C = grp[0][1]
                CP(vtok[0:C, g0:g0 + len(grp), :], bview(pv_[0:C, 0:len(grp) * 128], len(grp), 128), [pv_], [vtok], eng="act")
            for g0 in range(0, nch, 8):
                grp = chs[g0:g0 + 8]
                pt = PS(); ptb = pt.ap[:, :].bitcast(BF16)
                psc = PS()
                for gi, (t0, C, kind, ci) in enumerate(grp):
                    TR(ptb[0:C, gi * 128:(gi + 1) * 128], kdc[:, 0, t0:t0 + C], ident_b[:], [kdc, ident_b], [pt])
                    MM(psc[0:C, gi * HC:gi * HC + C], kbc[:, 0, t0:t0 + C], qbc[:, 0, t0:t0 + C], True, True, [kbc, qbc], [psc])
                C = grp[0][1]
                ng = len(grp)
                CP(kdtok[0:C, g0:g0 + ng, :], bview(ptb[0:C, 0:ng * 128], ng, 128), [pt], [kdtok], eng="act")
                P.op("dve", lambda e, C=C, g0=g0, ng=ng, psc=psc: e.copy_predicated(
                    out=scm[0:C, g0:g0 + ng, 0:C], mask=mask_incl_i[0:C, 0:ng, 0:C],
                    data=bview(psc[0:C, 0:ng * HC], ng, HC)[:, :, 0:C]), bl([psc, mask_incl_i, scm]), bl([scm]))
            for g0 in range(0, ncp, 4):
                pb = PS()
                for gi in range(4):
                    k_ = g0 + gi
                    MM(pb[:, gi * 128:(gi + 1) * 128], kdtok[0:HC, k_, :], vtok[0:HC, k_, :], True, True, [kdtok, vtok], [pb])
                CP(pSsb[:, g0:g0 + 4, :], bview(pb[:, :], 4, 128), [pb], [pSsb], eng="act")
            pg_pre = None
            if not last:
                pg_pre = PS(hold=True)
                for kc in range(8):
                    MM(pg_pre[:, 0:NPT], wo_[:, kc, hd * 128:(hd + 1) * 128], h[:, kc, 0:NPT], kc == 0, kc == 7, [wo_, h], [pg_pre])
            if hd + 1 < 4:
                pend_proj = emit_proj(hd + 1)
            CP(Sall[:, 0, :], hgS[:, l, hd, :], [hgS], [Sall])
            for c in range(ncp):
                STT(Sall[:, c + 1, :], Sall[:, c, :], st_[:, 1, c:c + 1], pSsb[:, c, :], ALU.mult, ALU.add, [Sall, st_, pSsb], [Sall])
            CP(hgS[:, l, hd, :], Sall[:, ncp, :], [Sall], [hgS])
            for hf in range(2):
                cs = slice(hf * (ncp // 2), (hf + 1) * (ncp // 2))
                TT(Sbf[:, cs, :], Sall[:, cs, :], st_[:, 2, cs].unsqueeze(2).to_broadcast([128, ncp // 2, 128]), ALU.mult, [Sall, st_], [Sbf],
                   eng=("dve" if hf == 0 else "pool"))
            po_p = PS(hold=True)
            for k_ in range(ncp):
                t0 = k_ * HC
                MM(po_p[:, t0:t0 + HC], vtok[0:HC, k_, :], scm[0:HC, k_, 0:HC], True, False, [vtok, scm], [po_p])
                MM(po_p[:, t0:t0 + HC], Sbf[:, k_, :], qbc[:, 0, t0:t0 + HC], False, True, [Sbf, qbc], [po_p])
            po_s = None
            if last:
                DMA(Sall[:, 0:NSQ, :], s_hg[l, :, hd].rearrange("q k v -> k q v"), [], [Sall])
                TT(Sbf[:, 0:NSQ, :], Sall[:, 0:NSQ, :], st_[:, 2, ncp:nch].unsqueeze(2).to_broadcast([128, NSQ, 128]), ALU.mult, [Sall, st_], [Sbf])
                po_s = PS(hold=True)
                for q in range(NSQ):
                    k_ = ncp + q
                    t0 = NPT + q * TS
                    MM(po_s[:, q * TS:(q + 1) * TS], vtok[0:TS, k_, :], scm[0:TS, k_, 0:TS], True, False, [vtok, scm], [po_s])
                    MM(po_s[:, q * TS:(q + 1) * TS], Sbf[:, q, :], qbc[:, 0, t0:t0 + TS], False, True, [Sbf, qbc], [po_s])
                TT(Sall[:, 0:NSQ, :], Sall[:, 0:NSQ, :], st_[:, 1, ncp:nch].unsqueeze(2).to_broadcast([128, NSQ, 128]), ALU.mult, [Sall, st_], [Sall])
                for g0 in range(0, NSQ, 4):
                    pb = PS()
                    for gi in range(4):
                        k_ = ncp + g0 + gi
                        MM(pb[:, gi * 128:(gi + 1) * 128], kdtok[0:TS, k_, :], vtok[0:TS, k_, :], True, True, [kdtok, vtok], [pb])
                    TT(Sall[:, g0:g0 + 4, :], Sall[:, g0:g0 + 4, :], bview(pb[:, :], 4, 128), ALU.add, [Sall, pb], [Sall])
                OUT(o_hg[l, 1:1 + NSQ, hd].rearrange("q k v -> k q v"), Sall[:, 0:NSQ, :], [Sall], "hgs")
            if last:
                OUT(o_hg[l, 0, hd], hgS[:, l, hd, :], [hgS], "hgp")
            for (t0, n), pp in zip(sl, (po_p, po_s)):
                CP(o32[:, 0, t0:t0 + n], pp[:, 0:n], [pp], [o32], eng="act")
            PREL(po_p)
            if last:
                PREL(po_s)
            for (t0, n), pp in zip(sl, (po_p, po_s)):
                TT(osq[:, 0, t0:t0 + n], o32[:, 0, t0:t0 + n], o32[:, 0, t0:t0 + n], ALU.mult, [o32], [osq])
                pn = PS()
                MM(pn[:, 0:n], ones_b[:], osq[:, 0, t0:t0 + n], True, True, [ones_b, osq], [pn])
                if pg_pre is not None and t0 == 0:
                    pg = pg_pre
                else:
                    pg = PS()
                    for kc in range(8):
                        MM(pg[:, 0:n], wo_[:, kc, hd * 128:(hd + 1) * 128], h[:, kc, t0:t0 + n], kc == 0, kc == 7, [wo_, h], [pg])
                ACT(A_[:, t0:t0 + n], pn[:, 0:n], AF.Ln, [pn], [t1], scale=1.0 / 128, bias=EPS)
                ACT(A_[:, t0:t0 + n], A_[:, t0:t0 + n], AF.Exp, [t1], [t1], scale=-0.5)
                ACT(B_[:, t0:t0 + n], pg[:, 0:n], AF.Silu, [pg], [t2])
                if pg is pg_pre:
                    PREL(pg)
                STT(A_[:, t0:t0 + n], o32[:, 0, t0:t0 + n], pv(l, "hg_norm_g", hd), A_[:, t0:t0 + n], ALU.mult, ALU.mult, [o32, ptab, t1], [t1])
                TT(y[:, hd, t0:t0 + n], A_[:, t0:t0 + n], B_[:, t0:t0 + n], ALU.mult, [t1, t2], [y])
    bones = sb("bones", [128, 128], BF16)
    bones2 = sb("bones2", [128, 2], BF16)
    MSET(bones[:], 0.0, [bones]); MSET(bones2[:], 0.0, [bones2])
    for par in range(2):
        MSET(bones[par * 64:(par + 1) * 64, par * 64:(par + 1) * 64], 1.0, [bones])
        MSET(bones2[par * 64:(par + 1) * 64, par:par + 1], 1.0, [bones2])
    mask_lower = sb("mask_lower", [64, 64], F32)
    MSET(mask_lower[:], 1.0, [mask_lower])
    P.op("pool", lambda e: e.affine_select(out=mask_lower[:], in_=mask_lower[:], pattern=[[-1, 64]], compare_op=ALU.is_ge,
                                           fill=0.0, base=-1, channel_multiplier=1), [mask_lower.b], [mask_lower.b])
    mask_incl_neg = sb("mask_incl_neg", [64, 64], F32)
    TSC(mask_incl_neg[:], mask_incl[:], -1.0, None, ALU.mult, None, [mask_incl], [mask_incl_neg])
    for l in range(L):
        TSC(dv(l, 30, 4), pv(l, "rw_k_a", 0, 4), -1.0, 1.0, ALU.mult, ALU.add, [ptab], [ptab])
    rwst_bd = sb("rwst_bd", [128, 4, 128], BF16)
    MSET(rwst_bd[:], 0.0, [rwst_bd])

    def MIX_RW(ti, l, NT):
        last = ti == NTILE - 1
        sl = slabs(NT)
        chs = chunks_of(NT)
        ncp = NPT // CH
        nch = len(chs)
        wts = [WLOAD(kview(w_in[l], 2048 + i * 512, 512), 8, 512) for i in range(3)]
        wl_ = WLOAD(kview(w_in[l], 3584, 256), 8, 256)
        At = aalloc("rw_At", [4, NT], BF16); Rt = aalloc("rw_Rt", [4, NT], BF16)
        Kt = aalloc("rw_Kt", [4, NT], BF16); Bt = aalloc("rw_Bt", [4, NT], BF16)
        rk = aalloc("rw_rk", [4, NT], BF16); vb = aalloc("rw_vb", [4, NT], BF16)
        sgd = aalloc("rw_sgd", [NT], BF16)
        ecl = aalloc("rw_ecl", [4, nch], F32)
        mark = ar["off"]
        r32 = aalloc("rw_r32", [4, NT], F32); k32 = aalloc("rw_k32", [4, NT], F32)
        twd = aalloc("rw_twd", [NT], BF16)
        mark2 = ar["off"]
        rexp = [aalloc("rw_rexp%d" % i, [1 + NPT], F32) for i in range(2)]
        rexs = [aalloc("rw_rexs%d" % i, [NSQ * (TS + 1)], F32) for i in range(2)] if last else None
        shs = aalloc("rw_shs", [14, NSQ], F32) if last else None
        if last:
            hh_ap = y.ap[0:NSQ, 4:10, :].rearrange("p a b -> p (a b)").bitcast(F32)[:, 0:RW_COLS]
            hh = T(hh_ap.rearrange("p (a b) -> p a b", a=1), y.b)
            DMA(hh[:, 0, :], s_sh[l], [], [hh])
            for g in range(2):
                ps = PS()
                for kc in range(7):
                    TR(ps[:, kc * 16:(kc + 1) * 16], hh[:, 0, (g * 7 + kc) * 128:(g * 7 + kc + 1) * 128], ident_f[0:16, 0:16], [hh, ident_f], [ps])
                CP(shs[:, g * 7:(g + 1) * 7, :].rearrange("p a b -> p (a b)"), ps[:, 0:112], [ps], [shs])
        for fc in range(14):
            wt = wts[fc // 4] if fc < 12 else wl_
            jj = fc % 4 if fc < 12 else fc - 12
            rp = rexp[fc % 2]
            rs_ = rexs[fc % 2] if last else None
            CP(rp[:, 0, 0:1], rwprev[:, l, fc:fc + 1], [rwprev], [rp])
            if last:
                CP(bview(rs_[:, 0, :], NSQ, TS + 1)[:, :, 0], shs[:, fc, :], [shs], [rs_])
            for (t0, n) in sl:
                pp = PS()
                for kc in range(8):
                    MM(pp[:, 0:n], wt[:, kc, jj * 128:(jj + 1) * 128], h[:, kc, t0:t0 + n], kc == 0, kc == 7, [wt, h], [pp])
                if t0 == 0:
                    CP(rp[:, 0, 1:1 + NPT], pp[:, 0:n], [pp], [rp], eng="act")
                else:
                    CP(bview(rs_[:, 0, :], NSQ, TS + 1)[:, :, 1:TS + 1], bview(pp[:, 0:n], NSQ, TS), [pp], [rs_], eng="act")
            CP(rwprev[:, l, fc:fc + 1], rp[:, 0, NPT:NPT + 1], [rp], [rwprev])
            if last:
                CP(shcol[:, l, fc, 0:1], rp[:, 0, NPT:NPT + 1], [rp], [shcol])
                CP(shcol[:, l, fc, 1:1 + NSQ], bview(rs_[:, 0, :], NSQ, TS + 1)[:, :, TS], [rs_], [shcol])
            if fc < 4:
                dst, dst_t = r32[:, fc, :], r32
            elif fc < 8:
                dst, dst_t = k32[:, fc - 4, :], k32
            elif fc < 12:
                dst, dst_t = vb[:, fc - 8, :], vb
            elif fc == 12:
                dst, dst_t = sgbuf[:, :], sgbuf
            else:
                dst, dst_t = sgbuf[:, :], sgbuf
            for (t0, n) in sl:
                if t0 == 0:
                    raw = rp[:, 0, 1:1 + NPT]; prv = rp[:, 0, 0:NPT]; rd = rp
                    d_ = dst[:, 0:NPT] if fc < 12 else sgbuf[:, 0:NPT]
                    tm = sgbuf[:, 0:NPT] if fc < 12 and fc >= 8 else d_
                else:
                    raw = bview(rs_[:, 0, :], NSQ, TS + 1)[:, :, 1:TS + 1]; prv = bview(rs_[:, 0, :], NSQ, TS + 1)[:, :, 0:TS]; rd = rs_
                    d_ = bview(dst[:, NPT:NT], NSQ, TS) if fc < 12 else bview(xtmp[:, 0:NSM], NSQ, TS)
                    tm = bview(xtmp[:, 0:NSM], NSQ, TS) if fc < 12 and fc >= 8 else d_
                tmt = sgbuf if t0 == 0 else xtmp
                wr = [dst_t] if fc < 8 else [tmt]
                TSC(tm, raw, dv(l, 16 + fc), None, ALU.mult, None, [rd, ptab], wr)
                STT(tm, prv, pv(l, "rw_mu", fc), tm, ALU.mult, ALU.add, [rd, ptab] + wr, wr)
                if 8 <= fc < 12:
                    CP(d_, tm, wr, [vb], eng="act")
                elif fc == 12:
                    src = tm
                    if t0 == 0:
                        ACT(twd[0:64, 0, 0:NPT], sgbuf[0:64, 0:NPT], AF.Tanh, [sgbuf], [twd])
                        CP(twd[64:128, 0, 0:NPT], sgbuf[64:128, 0:NPT], [sgbuf], [twd], eng="act")
                    else:
                        ACT(twd[0:64, 0, NPT:NT], xtmp[0:64, 0:NSM], AF.Tanh, [xtmp], [twd])
                        CP(twd[64:128, 0, NPT:NT], xtmp[64:128, 0:NSM], [xtmp], [twd], eng="act")
                elif fc == 13:
                    if t0 == 0:
                        ACT(sgd[:, 0, 0:NPT], sgbuf[:, 0:NPT], AF.Sigmoid, [sgbuf], [sgd])
                    else:
                        ACT(sgd[:, 0, NPT:NT], xtmp[:, 0:NSM], AF.Sigmoid, [xtmp], [sgd])
        if stage < 2.2:
            return
        arelease(mark2)
        ta = aalloc("rw_ta", [NT], F32); tb = aalloc("rw_tb", [NT], F32); tcc = aalloc("rw_tc", [NT], F32)
        td = aalloc("rw_td", [NT], F32); te = aalloc("rw_te", [NT], F32)
        for j in range(4):
            A_ = ta[:, 0, :]; B_ = tb[:, 0, :]; C_ = tcc[:, 0, :]; D_ = td[:, 0, :]; E_ = te[:, 0, :]
            for (t0, n) in sl:
                pw = PS(); pa = PS()
                MM(pw[:, 0:n], rwup[0:64, l, 0, j * 128:(j + 1) * 128], twd[0:64, 0, t0:t0 + n], True, True, [rwup, twd], [pw])
                MM(pa[:, 0:n], rwup[64:128, l, 1, j * 128:(j + 1) * 128], twd[64:128, 0, t0:t0 + n], True, True, [rwup, twd], [pa])
                ACT(A_[:, t0:t0 + n], pw[:, 0:n], AF.Sigmoid, [pw, ptab], [ta], bias=pv(l, "rw_w0", j))
                ACT(B_[:, t0:t0 + n], pa[:, 0:n], AF.Sigmoid, [pa, ptab], [tb], bias=pv(l, "rw_a0", j))
            TSC(A_[:, 0:NT], A_[:, 0:NT], -0.606531, None, ALU.mult, None, [ta], [ta])
            SCAN(C_[:, 0:NT], cmask[:, 0:NT], A_[:, 0:NT], 0.0, [cmask, ta], [tcc])
            CP(ecl[:, j, 0:ncp], bview(C_[:, 0:NPT], ncp, CH)[:, :, CH - 1], [tcc], [ecl])
            if last:
                CP(ecl[:, j, ncp:nch], bview(C_[:, NPT:NT], NSQ, TS)[:, :, TS - 1], [tcc], [ecl])
            ACT(ecl[:, j, :], ecl[:, j, :], AF.Exp, [ecl], [ecl])
            TT(A_[:, 0:NT], C_[:, 0:NT], A_[:, 0:NT], ALU.subtract, [tcc, ta], [ta])
            ACT(A_[:, 0:NT], A_[:, 0:NT], AF.Exp, [ta], [ta])
            TSC(D_[:, 0:NT], k32[:, j, 0:NT], pv(l, "rw_k_k", j), None, ALU.mult, None, [k32, ptab], [td])
            TT(sgd2[:, 0:NT], D_[:, 0:NT], D_[:, 0:NT], ALU.mult, [td], [sgd2])
            for (t0, n) in sl:
                pn = PS()
                MM(pn[:, 0:n], bones[:], sgd2[:, t0:t0 + n], True, True, [bones, sgd2], [pn])
                TSC(E_[:, t0:t0 + n], pn[:, 0:n], 6e-20, None, ALU.max, None, [pn], [te])
            ACT(E_[:, 0:NT], E_[:, 0:NT], AF.Ln, [te], [te])
            ACT(E_[:, 0:NT], E_[:, 0:NT], AF.Exp, [te], [te], scale=-0.5)
            TT(D_[:, 0:NT], D_[:, 0:NT], E_[:, 0:NT], ALU.mult, [td, te], [td])
            TT(At[:, j, 0:NT], D_[:, 0:NT], A_[:, 0:NT], ALU.mult, [td, ta], [At])
            ACT(A_[:, 0:NT], C_[:, 0:NT], AF.Exp, [tcc], [ta], scale=-1.0)
            TT(D_[:, 0:NT], D_[:, 0:NT], B_[:, 0:NT], ALU.mult, [td, tb], [td])
            TT(Bt[:, j, 0:NT], D_[:, 0:NT], A_[:, 0:NT], ALU.mult, [td, ta], [Bt])
            TSC(B_[:, 0:NT], B_[:, 0:NT], pv(l, "rw_k_a", j), dv(l, 30 + j), ALU.mult, ALU.add, [tb, ptab], [tb])
            TT(B_[:, 0:NT], B_[:, 0:NT], k32[:, j, 0:NT], ALU.mult, [tb, k32], [tb])
            TT(Kt[:, j, 0:NT], B_[:, 0:NT], A_[:, 0:NT], ALU.mult, [tb, ta], [Kt])
            TT(B_[:, 0:NT], B_[:, 0:NT], r32[:, j, 0:NT], ALU.mult, [tb, r32], [tb])
            TSC(rk[:, j, 0:NT], B_[:, 0:NT], pv(l, "rw_r_k", j), None, ALU.mult, None, [tb, ptab], [rk])
            ACT(C_[:, 0:NT], C_[:, 0:NT], AF.Exp, [tcc], [tcc])
            TT(Rt[:, j, 0:NT], r32[:, j, 0:NT], C_[:, 0:NT], ALU.mult, [r32, tcc], [Rt])
        arelease(mark)
        if stage < 2.4:
            return
        yraw = aalloc("rw_yraw", [4, NT], BF16)
        mark3 = ar["off"]
        gM = aalloc("rw_gM", [8, CH], BF16, parts=CH); gMT = aalloc("rw_gMT", [8, CH], BF16, parts=CH)
        gQ = aalloc("rw_gQ", [8, CH], BF16, parts=CH); gN = aalloc("rw_gN", [8, CH], BF16, parts=CH); gP = aalloc("rw_gP", [8, CH], BF16, parts=CH)
        Tb = [aalloc("rw_T%d" % i, [8, CH], BF16, parts=CH) for i in range(2)]
        Pb = [aalloc("rw_P%d" % i, [8, CH], BF16, parts=CH) for i in range(2)]
        PTb = [aalloc("rw_PT%d" % i, [8, CH], BF16, parts=CH) for i in range(2)]
        vtok = aalloc("rw_vtok", [W], BF16, parts=CH); ktok = aalloc("rw_ktok", [W], BF16, parts=CH); btok = aalloc("rw_btok", [W], BF16, parts=CH)
        gN2 = aalloc("rw_gN2", [8, CH], BF16, parts=CH); gQ2 = aalloc("rw_gQ2", [8, CH], BF16, parts=CH); gP2 = aalloc("rw_gP2", [8, CH], BF16, parts=CH)
        Tb2 = [aalloc("rw_T2%d" % i, [8, CH], BF16, parts=CH) for i in range(2)]
        vtok2 = aalloc("rw_vtok2", [W], BF16, parts=CH); ktok2 = aalloc("rw_ktok2", [W], BF16, parts=CH); btok2 = aalloc("rw_btok2", [W], BF16, parts=CH)
        gNs, gQs, gPs = [gN, gN2], [gQ, gQ2], [gP, gP2]
        vtoks, ktoks, btoks = [vtok, vtok2], [ktok, ktok2], [btok, btok2]
        Tbs = [Tb, Tb2]
        gt = aalloc("rw_gt", [W], BF16, parts=CH); ut = aalloc("rw_ut", [W], BF16, parts=CH)
        yA = aalloc("rw_yA", [W], F32, parts=CH); yB = aalloc("rw_yB", [W], F32, parts=CH)
        yst = aalloc("rw_yst", [4, 8], F32, parts=CH)
        yob = aalloc("rw_yob", [W], BF16, parts=CH)
        sts = aalloc("rw_sts", [4, 64], F32) if last else None
        stl = aalloc("rw_stl", [8, 64], F32, parts=64) if last else None
        sto = aalloc("rw_sto", [W], F32, parts=64) if last else None
        stmp = aalloc("rw_stmp", [4, 64], F32)

        def state_out(ST_t, ST_ap, dst):
            ps = PS()
            for j in range(4):
                TR(ps[0:64, j * 128:(j + 1) * 128], ST_ap[:, j, :], ident_f[:], [ST_t, ident_f], [ps])
            CP(sto[:, 0, :], ps[0:64, :], [ps], [sto], eng="act")
            OUT(dst.rearrange("h v k -> v h k"), sto[:, 0, :].rearrange("v (h k) -> v h k", h=8), [sto], "rwst")

        def SI(k_, t0, C, par, hook):
            nlev = {64: 5, 8: 2}[C]
            gN, gQ, gP = gNs[par], gQs[par], gPs[par]
            vtok, ktok, btok = vtoks[par], ktoks[par], btoks[par]
            Tb = Tbs[par]
            pM = PS(); pMT = PS(); pQ = PS(hold=True); pN = PS(hold=True); pP = PS(hold=True)
            for hh_ in range(8):
                j, par = hh_ // 2, hh_ % 2
                rows = slice(par * 64, par * 64 + 64)
                o_ = slice(hh_ * CH, hh_ * CH + C)
                a_ = At[rows, j, t0:t0 + C]; r_ = Rt[rows, j, t0:t0 + C]; k__ = Kt[rows, j, t0:t0 + C]; b_ = Bt[rows, j, t0:t0 + C]
                MM(pM[0:C, o_], b_, a_, True, True, [Bt, At], [pM])
                MM(pMT[0:C, o_], a_, b_, True, True, [Bt, At], [pMT])
                MM(pQ[0:C, o_], b_, r_, True, True, [Bt, Rt], [pQ])
                MM(pN[0:C, o_], k__, a_, True, True, [Kt, At], [pN])
                MM(pP[0:C, o_], k__, r_, True, True, [Kt, Rt], [pP])

            for src_, dst_, neg in ((vb, vtok, False), (Kt, ktok, False), (Bt, btok, True)):
                pt = PS(); ptb = pt.ap[:, :].bitcast(BF16)
                for j in range(4):
                    TR(ptb[0:C, j * 128:(j + 1) * 128], src_[:, j, t0:t0 + C], ident_b[:], [src_, ident_b], [pt])
                if neg:
                    ACT(dst_[0:C, 0, :], ptb[0:C, 0:W], AF.Copy, [pt], [dst_], scale=-1.0)
                else:
                    CP(dst_[0:C, 0, :], ptb[0:C, 0:W], [pt], [dst_], eng="act")
            def g3(t_):
                return t_[0:C, :, 0:C]

            def p3(p_):
                return bview(p_[0:C, :], 8, CH)[:, :, 0:C]

            def mk(m_):
                return m_[0:C, 0:C].unsqueeze(1).to_broadcast([C, 8, C])
            TT(g3(gM), p3(pM), mk(mask_strict), ALU.mult, [pM, mask_strict], [gM])
            TT(g3(gMT), p3(pMT), mk(mask_lower), ALU.mult, [pMT, mask_lower], [gMT])
            if hook is not None and C == TS:
                hook()
            late = [(gN, pN, mask_strict), (gQ, pQ, mask_incl_neg), (gP, pP, mask_incl)]

            def late_evac():
                if late:
                    g_, p_, m_ = late.pop(0)
                    TT(g3(g_), p3(p_), mk(m_), ALU.mult, [p_, m_], [g_])
                    PREL(p_)
            Tc = Tb[0]
            Pc, PTc = gM, gMT

            def emit_PP(lev, Pc, PTc):
                lastlev = lev == nlev
                pp1 = PS() if not lastlev else None
                pp2 = PS()
                for hh_ in range(8):
                    o_ = slice(hh_ * CH, hh_ * CH + C)
                    if not lastlev:
                        MM(pp1[0:C, o_], PTc[0:C, hh_, 0:C], Pc[0:C, hh_, 0:C], True, True, [PTc, Pc], [pp1])
                    MM(pp2[0:C, o_], Pc[0:C, hh_, 0:C], PTc[0:C, hh_, 0:C], True, True, [PTc, Pc], [pp2])
                return pp1, pp2
            pend = emit_PP(1, Pc, PTc)
            TT(g3(Tc), mk(ident_f), g3(gM), ALU.subtract, [ident_f, gM], [Tc])
            if hook is not None and C == TS:
                hook()
            tpend = None
            for lev in range(1, nlev + 1):
                Pn, PTn, Tn = Pb[lev % 2], PTb[lev % 2], Tb[lev % 2]
                lastlev = lev == nlev
                pp1, pp2 = pend
                if not lastlev:
                    CP(g3(Pn), p3(pp1), [pp1], [Pn], eng="act")
                CP(g3(PTn), p3(pp2), [pp2], [PTn], eng="dve")
                if tpend is not None:
                    pp3_, Told_, Tnew_ = tpend
                    TT(g3(Tnew_), p3(pp3_), g3(Told_), ALU.add, [pp3_, Told_], [Tnew_])
                    PREL(pp3_)
                    Tc = Tnew_
                    tpend = None
                late_evac()
                if not lastlev:
                    pend = emit_PP(lev + 1, Pn, PTn)
                pp3 = PS(hold=True)
                for hh_ in range(8):
                    o_ = slice(hh_ * CH, hh_ * CH + C)
                    MM(pp3[0:C, o_], PTn[0:C, hh_, 0:C], Tc[0:C, hh_, 0:C], True, True, [PTn, Tc], [pp3])
                tpend = (pp3, Tc, Tn)
                Pc, PTc = Pn, PTn
                if hook is not None and not lastlev:
                    hook()
                if lastlev:
                    pp3_, Told_, Tnew_ = tpend
                    TT(g3(Tnew_), p3(pp3_), g3(Told_), ALU.add, [pp3_, Told_], [Tnew_])
                    Tc = Tnew_
                    tpend = None
                    PREL(pp3_)
            while late:
                late_evac()
            return Tc

        def SD(k_, t0, C, kind, ci, par, Tc):
            gN, gQ, gP = gNs[par], gQs[par], gPs[par]
            vtok, ktok, btok = vtoks[par], ktoks[par], btoks[par]
            nlev = {64: 5, 8: 2}[C]
            if kind == "p":
                ST_t, ST_ap = rwST, rwST[:, l, :, :]
            else:
                ST_t, ST_ap = sts, sts[:, :, :]
                DMA(stl[:, :, :], s_rw[l, ci].rearrange("h v k -> v h k"), [], [stl])
                ps = PS()
                for j in range(4):
                    TR(ps[:, j * 64:(j + 1) * 64], stl[:, 2 * j:2 * j + 2, :].rearrange("v h k -> v (h k)"), ident_f[0:64, 0:64], [stl, ident_f], [ps])
                CP(sts[:, :, :].rearrange("p a b -> p (a b)"), ps[:, 0:256], [ps], [sts])
            for par in range(2):
                rows = slice(par * 64, par * 64 + 64)
                CP(rwst_bd[rows, :, par * 64:par * 64 + 64], ST_ap[rows], [ST_t], [rwst_bd], eng="act")
            pG = PS(hold=True)
            for j in range(4):
                MM(pG[0:C, j * 128:(j + 1) * 128], At[:, j, t0:t0 + C], rwst_bd[:, j, :], True, False, [At, rwst_bd], [pG])
                for par in range(2):
                    hh_ = 2 * j + par
                    vs = slice(hh_ * 64, hh_ * 64 + 64)
                    MM(pG[0:C, vs], gN[0:C, hh_, 0:C], vtok[0:C, 0, vs], False, par == 1, [gN, vtok], [pG])
            yield
            CP(gt[0:C, 0, :], pG[0:C, :], [pG], [gt], eng="act")
            PREL(pG)
            pU = PS(hold=True)
            for hh_ in range(8):
                vs = slice(hh_ * 64, hh_ * 64 + 64)
                MM(pU[0:C, vs], Tc[0:C, hh_, 0:C], gt[0:C, 0, vs], True, True, [Tc, gt], [pU])
            yield
            CP(ut[0:C, 0, :], pU[0:C, :], [pU], [ut], eng="act")
            PREL(pU)
            pY = PS(hold=True)
            for j in range(4):
                MM(pY[0:C, j * 128:(j + 1) * 128], Rt[:, j, t0:t0 + C], rwst_bd[:, j, :], True, False, [Rt, rwst_bd], [pY])
                for par in range(2):
                    hh_ = 2 * j + par
                    vs = slice(hh_ * 64, hh_ * 64 + 64)
                    MM(pY[0:C, vs], gP[0:C, hh_, 0:C], vtok[0:C, 0, vs], False, False, [gP, vtok], [pY])
                    MM(pY[0:C, vs], gQ[0:C, hh_, 0:C], ut[0:C, 0, vs], False, par == 1, [gQ, ut], [pY])
            pS = PS(hold=True)
            for j in range(4):
                js = slice(j * 128, (j + 1) * 128)
                MM(pS[:, js], ktok[0:C, 0, js], vtok[0:C, 0, js], True, False, [ktok, vtok], [pS])
                MM(pS[:, js], btok[0:C, 0, js], ut[0:C, 0, js], False, True, [btok, ut], [pS])
            yield
            for par in range(2):
                rows = slice(par * 64, par * 64 + 64)
                TT(stmp[rows, :, :], ST_ap[rows], bview(pS[rows, :], 4, 128)[:, :, par * 64:par * 64 + 64], ALU.add, [ST_t, pS], [stmp])
            PREL(pS)
            TT(ST_ap, stmp[:, :, :], ecl[:, :, k_:k_ + 1].to_broadcast([128, 4, 64]), ALU.mult, [stmp, ecl], [ST_t])
            if kind == "s":
                state_out(ST_t, ST_ap, o_rw[l, 1 + ci])
            if ti == 0 and l == 0 and L > 1:
                ADA_PIECES(1, 2)
            CP(yob[0:C, 0, :], pY[0:C, :], [pY], [yob], eng="act")
            PREL(pY)
            yield
            pt = PS(); ptb = pt.ap[:, :].bitcast(BF16)
            for j in range(4):
                TR(ptb[:, j * 64:j * 64 + C], yob[0:C, 0, j * 128:(j + 1) * 128], ident_b[0:C, 0:C], [yob, ident_b], [pt])
            CP(yraw[:, :, t0:t0 + C], bview(ptb[:, 0:256], 4, 64)[:, :, 0:C], [pt], [yraw], eng="act")

        seq = list(enumerate(chs))
        Tc_cur = SI(seq[0][0], seq[0][1][0], seq[0][1][1], 0, None)
        for i, (k_, (t0, C, kind, ci)) in enumerate(seq):
            sdg = SD(k_, t0, C, kind, ci, i % 2, Tc_cur)
            if i + 1 < len(seq):
                k2, (t02, C2, _, _) = seq[i + 1]
                Tc_cur = SI(k2, t02, C2, (i + 1) % 2, lambda: next(sdg, None))
            for _ in sdg:
                pass
        if last:
            state_out(rwST, rwST[:, l, :, :], o_rw[l, 0])
        arelease(mark3)
        psets = [[aalloc("rw_p%d%d" % (i, k), [NT], F32) for k in range(4)] + [aalloc("rw_psq%d" % i, [NT], BF16)] for i in range(2)]
        for j in range(4):
            yr = T(yraw.ap, Buf("yr%d" % j, ar["fence"]))
            yr.b.last_w = yraw.b.last_w
            ar["bufs"].append(yr.b)
            tA_, tB_, tC_, tD_, tsq = psets[j % 2]
            for (t0, n) in sl:
                tsl = slice(t0, t0 + n)
                ACT(tsq[:, 0, tsl], yraw[:, j, tsl], AF.Square, [yr], [tsq])
                pm = PS(); pq = PS(); pb = PS(); pg = PS()
                MM(pm[:, 0:n], bones[:], yraw[:, j, tsl], True, True, [bones, yr], [pm])
                MM(pq[:, 0:n], bones[:], tsq[:, 0, tsl], True, True, [bones, tsq], [pq])
                MM(pb[:, 0:n], bones[:], rk[:, j, tsl], True, True, [bones, rk], [pb])
                MM(pg[:, 0:n], rwup[:, l, 2, j * 128:(j + 1) * 128], sgd[:, 0, tsl], True, True, [rwup, sgd], [pg])
                A_ = tA_[:, 0, tsl]; B_ = tB_[:, 0, tsl]; C_ = tC_[:, 0, tsl]; D_ = tD_[:, 0, tsl]
                ACT(A_, pm[:, 0:n], AF.Copy, [pm], [tA_], scale=1.0 / 64)
                TT(B_, A_, A_, ALU.mult, [tA_], [tB_])
                STT(B_, pq[:, 0:n], 1.0 / 64, B_, ALU.mult, ALU.subtract, [pq, tB_], [tB_])
                ACT(B_, B_, AF.Ln, [tB_], [tB_], bias=64e-5)
                ACT(B_, B_, AF.Exp, [tB_], [tB_], scale=-0.5)
                TT(C_, yraw[:, j, tsl], A_, ALU.subtract, [yr, tA_], [tC_])
                TT(C_, C_, B_, ALU.mult, [tC_, tB_], [tC_])
                ACT(C_, C_, AF.Identity, [tC_, ptab], [tC_], scale=pv(l, "rw_ln_g", j), bias=pv(l, "rw_ln_b", j))
                TT(D_, pb[:, 0:n], vb[:, j, tsl], ALU.mult, [pb, vb], [tD_])
                TT(C_, C_, D_, ALU.add, [tC_, tD_], [tC_])
                TT(y[:, 4 + j, tsl], C_, pg[:, 0:n], ALU.mult, [tC_, pg], [y])

    def MIXERS(ti, l, NT):
        if stage >= 2:
            MIX_HG(ti, l, NT); areset()
        if stage > 2:
            MIX_RW(ti, l, NT); areset()
        if stage >= 4:
            MIX_CF(ti, l, NT); areset()
        if stage >= 5:
            MIX_LRU(ti, l, NT)

    def FINAL_STATES():
        for l in range(nlayer):
            stg = aalloc("fs_sh%d" % l, [RW_COLS], F32, parts=1 + NSQ)
            for g in range(4):
                ps = PS()
                nk = 4 if g < 3 else 2
                for kk_ in range(nk):
                    kc = g * 4 + kk_
                    TR(ps[0:17, kk_ * 128:(kk_ + 1) * 128], shcol[:, l, kc, :], ident_f[:], [shcol, ident_f], [ps])
                CP(stg[:, 0, g * 512:g * 512 + nk * 128], ps[0:17, 0:nk * 128], [ps], [stg])
            OUT(o_sh[l], stg[:, 0, :], [stg], "sh")
            stg2 = aalloc("fs_lh%d" % l, [W], F32, parts=1 + NSQ)
            ps = PS()
            for j in range(4):
                TR(ps[0:17, j * 128:(j + 1) * 128], lhcol[:, l, j, :], ident_f[:], [lhcol, ident_f], [ps])
            CP(stg2[:, 0, :], ps[0:17, :], [ps], [stg2])
            OUT(o_lh[l], stg2[:, 0, :], [stg2], "lh")

    for ti in range(NTILE):
        NT = NPT + (NSM if ti == NTILE - 1 else 0)
        nblk = NT // 128
        for blk in range(nblk):
            xt = aalloc("xtok%d" % (blk % 2), [D], F32) if blk < 2 else xt_bufs[blk % 2]
            if blk < 2:
                if blk == 0:
                    xt_bufs = []
                xt_bufs.append(xt)
            src = xp[ti * NPT + blk * 128: ti * NPT + (blk + 1) * 128, :] if blk < 4 else xs[:, :]
            DMA(xt[:, 0, :], src, [], [xt])
            for half in range(2):
                ps = PS()
                for j in range(4):
                    kc = half * 4 + j
                    TR(ps[:, j * 128:(j + 1) * 128], xt[:, 0, kc * 128:(kc + 1) * 128], ident_f[:], [xt, ident_f], [ps])
                CP(x[:, half * 4:half * 4 + 4, blk * 128:(blk + 1) * 128], ps[:, :].rearrange("p (j t) -> p j t", j=4), [ps], [x],
                   eng=("act" if half else "dve"))
        areset()
        for l in range(nlayer):
            ADA_PIECES(l, 12)
            if stage >= 1:
                NORM(l, 0, NT)
            areset()
            MIXERS(ti, l, NT)
            areset()
            if stage >= 6:
                MERGE_MLP(l, NT)
        for blk in range(nblk):
            xo = aalloc("xo%d" % (blk % 2), [D], F32) if blk < 2 else xo_bufs[blk % 2]
            if blk < 2:
                if blk == 0:
                    xo_bufs = []
                    fstat = aalloc("fstat", [8], F32)
                    junk = aalloc("junk", [D], F32)
                xo_bufs.append(xo)
            for half in range(2):
                ps = PS()
                for j in range(4):
                    kc = half * 4 + j
                    TR(ps[:, j * 128:(j + 1) * 128], x[:, kc, blk * 128:(blk + 1) * 128], ident_f[:], [x, ident_f], [ps])
                CP(xo[:, 0, half * 512:(half + 1) * 512], ps[:, :], [ps], [xo], eng=("act" if half else "dve"))
            P.op("act", lambda e, xo=xo, junk=junk, fstat=fstat: e.activation(out=junk[:, 0, :], in_=xo[:, 0, :], func=AF.Square,
                                                                              accum_out=fstat[:, 0, 0:1]), bl([xo]), bl([junk, fstat]))
            ACT(fstat[:, 0, 1:2], fstat[:, 0, 0:1], AF.Sqrt, [fstat], [fstat], scale=1.0 / D, bias=EPS)
            RCP(fstat[:, 0, 2:3], fstat[:, 0, 1:2], [fstat], [fstat])
            STT(xo[:, 0, :], xo[:, 0, :], fstat[:, 0, 2:3], gfin[:], ALU.mult, ALU.mult, [xo, fstat, gfin], [xo])
            dst = o_yp[ti * NPT + blk * 128: ti * NPT + (blk + 1) * 128, :] if blk < 4 else o_ys[:, :]
            OUT(dst, xo[:, 0, :], [xo], "y")
        areset()

    if stage >= 6:
        FINAL_STATES()
    print('arena high-water', ar.get('hw'), 'of', ARENA_E, 'sbuf left', nc.sbuf_bytes_remaining)
    P.emit(out_bufs)
    return nc, dbg_outs


_CACHE = {}


def kernel(**inp):
    f = lambda a: np.ascontiguousarray(np.asarray(a, dtype=np.float32))
    if "nc" not in _CACHE:
        _CACHE["nc"] = build()
    nc, _ = _CACHE["nc"]
    shared = {k: f(inp[k]) for k in ("ada_w", "ada_b", "norm_mix_g", "norm_mlp_g", "norm_final_g", "w_in", "hg_lower", "hg_norm_g",
                                     "rw_mu", "rw_w0", "rw_w_up", "rw_a0", "rw_a_up", "rw_g_up", "rw_k_k", "rw_k_a", "rw_r_k",
                                     "rw_ln_g", "rw_ln_b", "cf_dw", "cf_dw_b", "cf_ln_g", "cf_ln_b", "lru_conv_w", "lru_conv_b",
                                     "lru_wa", "lru_ba", "lru_wx", "lru_bx", "lru_lambda", "w_branch", "w_gate", "b_gate", "w_out",
                                     "w_mlp1", "w_mlp2")}
    xp = f(inp["x_prompt"]); xs = f(inp["x_sample"])
    in_maps = []
    for c in range(NCORE):
        sq = slice(c * NSQ, (c + 1) * NSQ)
        m = dict(shared)
        m["xp"] = xp[c]
        m["xs"] = np.ascontiguousarray(xs[sq].reshape(NSM, D))
        m["s_hg"] = f(inp["state_hgrn"][:, sq]); m["s_rw"] = f(inp["state_rwkv"][:, sq])
        m["s_sh"] = f(inp["state_rwkv_shift"][:, sq]); m["s_cf"] = f(inp["state_conv"][:, sq])
        m["s_lh"] = f(inp["state_lru_h"][:, sq]); m["s_lc"] = f(inp["state_lru_conv"][:, sq])
        m["cc"] = np.ascontiguousarray(np.concatenate([f(inp["c_prompt"])[c:c + 1], f(inp["c_sample"])[sq]], axis=0))
        in_maps.append(m)
    res = run_bass_kernel_spmd(nc, in_maps, core_ids=list(range(NCORE)))
    R = res.results
    _CACHE["last"] = R
    y_p = np.stack([R[c]["o_yp"] for c in range(NCORE)], axis=0)
    y_s = np.concatenate([R[c]["o_ys"].reshape(NSQ, TS, D) for c in range(NCORE)], axis=0)
    outs = [y_p, y_s]
    for nm in ("o_hg", "o_rw", "o_sh", "o_cf", "o_lh", "o_lc"):
        outs.append(np.stack([R[c][nm][:, 0] for c in range(NCORE)], axis=1))
    for nm in ("o_hg", "o_rw", "o_sh", "o_cf", "o_lh", "o_lc"):
        outs.append(np.concatenate([R[c][nm][:, 1:] for c in range(NCORE)], axis=1))
    return tuple(np.ascontiguousarray(o.astype(np.float32)) for o in outs)
```

```python
import contextlib
import numpy as np
import concourse.bass as bass
import concourse.mybir as mybir
from concourse.bass_utils import run_bass_kernel_spmd

F32 = mybir.dt.float32
BF16 = mybir.dt.bfloat16
AF = mybir.ActivationFunctionType
ALU = mybir.AluOpType
AX = mybir.AxisListType

ENGS = ("pe", "act", "dve", "pool", "sp")
D = 1024
W = 512
NCORE = 8
SEQ = 2048
NTILE = 4
NPT = 512
NSQ = 16
TS = 8
NSM = NSQ * TS
NTMAX = NPT + NSM
L = 2
IN_COLS = 5888
RW_COLS = 1792
CH = 64
EPS = 1e-6
DEBUG_SITES = False
SITES = {}


class Buf:
    __slots__ = ("name", "last_w", "readers")

    def __init__(self, name, readers=None):
        self.name = name
        self.last_w = None
        self.readers = list(readers) if readers else []


class Op:
    __slots__ = ("eng", "idx", "fn", "waits", "inc", "semval", "dma_sem", "dma_val", "clock", "site")

    def __init__(self, eng, idx, fn):
        self.eng = eng
        self.idx = idx
        self.fn = fn
        self.waits = []
        self.inc = False
        self.semval = 0
        self.dma_sem = None
        self.dma_val = 0
        self.clock = None


class Prog:
    def __init__(self, nc):
        self.nc = nc
        self.ops = {e: [] for e in ENGS}
        self.clock = {e: {} for e in ENGS}
        self.dma_sems = {}
        self.last_dma = {}

    @staticmethod
    def _key(prod):
        if prod.dma_sem is not None:
            return ("dma", prod.dma_sem), prod.dma_val
        return prod.eng, prod.idx

    def _record(self, eng, fn, reads, writes, dma_sem=None):
        lst = self.ops[eng]
        op = Op(eng, len(lst), fn)
        if DEBUG_SITES:
            import sys as _s
            f_ = _s._getframe(3)
            op.site = (f_.f_lineno, f_.f_back.f_lineno if f_.f_back else 0)
        cands = []
        for b in reads:
            if b.last_w is not None:
                cands.append(b.last_w)
        for b in writes:
            if b.last_w is not None:
                cands.append(b.last_w)
            cands.extend(b.readers)
        ck = self.clock[eng]
        best = {}
        if dma_sem is not None:
            prev = self.last_dma.get(dma_sem)
            if prev is not None and ck.get(("dma", dma_sem), -1) < prev.dma_val:
                best[("dma", dma_sem)] = prev
        for p in cands:
            if p.dma_sem is None and p.eng == "pe" and eng == "pe":
                continue
            k, v = self._key(p)
            if ck.get(k, -1) >= v:
                continue
            if k not in best or self._key(best[k])[1] < v:
                best[k] = p
        for k, p in best.items():
            ck[k] = self._key(p)[1]
            op.waits.append(p)
            if p.dma_sem is None:
                p.inc = True
            if p.clock is not None:
                for kk, vv in p.clock.items():
                    if ck.get(kk, -1) < vv:
                        ck[kk] = vv
        if dma_sem is not None:
            cnt = self.dma_sems.setdefault(dma_sem, [0])
            cnt[0] += 16
            op.dma_sem = dma_sem
            op.dma_val = cnt[0]
            self.last_dma[dma_sem] = op
        for b in reads:
            b.readers.append(op)
        for b in writes:
            b.last_w = op
            b.readers = []
        op.clock = dict(ck)
        lst.append(op)
        return op

    def op(self, eng, fn, reads=(), writes=()):
        return self._record(eng, fn, reads, writes)

    def dma(self, eng, fn, sem, reads=(), writes=()):
        return self._record(eng, fn, reads, writes, dma_sem=sem)

    def emit(self, final_bufs=()):
        nc = self.nc
        self._record("sp", None, list(final_bufs), [])
        with contextlib.ExitStack() as st:
            sems = {e: st.enter_context(nc.semaphore("s_" + e)) for e in ENGS}
            dsem = {n: st.enter_context(nc.semaphore("d_" + str(n))) for n in self.dma_sems}
            for e in ENGS:
                c = 0
                for op in self.ops[e]:
                    if op.dma_sem is None and op.inc:
                        c += 1
                        op.semval = c
            block = st.enter_context(nc.Block())

            def run(eng_name):
                def body(eng):
                    for op in self.ops[eng_name]:
                        for p in op.waits:
                            if p.dma_sem is not None:
                                eng.wait_ge(dsem[p.dma_sem], p.dma_val)
                            else:
                                eng.wait_ge(sems[p.eng], p.semval)
                        if op.fn is None:
                            continue
                        ins = op.fn(eng)
                        if DEBUG_SITES:
                            try:
                                SITES[ins.ins.name] = op.site
                            except Exception:
                                pass
                        if op.dma_sem is not None:
                            ins.then_inc(dsem[op.dma_sem], 16)
                        elif op.inc:
                            ins.then_inc(sems[eng_name], 1)
                return body

            block.tensor(run("pe"))
            block.scalar(run("act"))
            block.vector(run("dve"))
            block.gpsimd(run("pool"))
            block.sync(run("sp"))


class T:
    __slots__ = ("ap", "b")

    def __init__(self, ap, b):
        self.ap = ap
        self.b = b

    def __getitem__(self, k):
        return self.ap[k]


def build(debug=None, ntile=NTILE, nlayer=L, stage=9):
    NTILE = ntile
    SEQ = NTILE * NPT
    nc = bass.Bass("TRN2", target_bir_lowering=False)
    P = Prog(nc)
    st = contextlib.ExitStack()
    dbg_outs = {}

    def din(name, shape):
        return nc.dram_tensor(name, list(shape), F32, kind="ExternalInput").ap()

    def dout(name, shape):
        return nc.dram_tensor(name, list(shape), F32, kind="ExternalOutput").ap()

    xp = din("xp", [SEQ, D]); xs = din("xs", [NSM, D])
    s_hg = din("s_hg", [L, NSQ, 4, 128, 128]); s_rw = din("s_rw", [L, NSQ, 8, 64, 64])
    s_sh = din("s_sh", [L, NSQ, RW_COLS]); s_cf = din("s_cf", [L, NSQ, 30, W])
    s_lh = din("s_lh", [L, NSQ, W]); s_lc = din("s_lc", [L, NSQ, 3, W])
    cc = din("cc", [1 + NSQ, D])
    ada_w = din("ada_w", [L, D, 6 * D]); ada_b = din("ada_b", [L, 6 * D])
    norm_mix_g = din("norm_mix_g", [L, D]); norm_mlp_g = din("norm_mlp_g", [L, D]); norm_final_g = din("norm_final_g", [D])
    w_in = din("w_in", [L, D, IN_COLS])
    hg_lower = din("hg_lower", [L, W]); hg_norm_g = din("hg_norm_g", [L, W])
    rw_mu = din("rw_mu", [L, RW_COLS]); rw_w0 = din("rw_w0", [L, W]); rw_w_up = din("rw_w_up", [L, 64, W])
    rw_a0 = din("rw_a0", [L, W]); rw_a_up = din("rw_a_up", [L, 64, W]); rw_g_up = din("rw_g_up", [L, 128, W])
    rw_k_k = din("rw_k_k", [L, W]); rw_k_a = din("rw_k_a", [L, W]); rw_r_k = din("rw_r_k", [L, W])
    rw_ln_g = din("rw_ln_g", [L, W]); rw_ln_b = din("rw_ln_b", [L, W])
    cf_dw = din("cf_dw", [L, 31, W]); cf_dw_b = din("cf_dw_b", [L, W]); cf_ln_g = din("cf_ln_g", [L, W]); cf_ln_b = din("cf_ln_b", [L, W])
    lru_conv_w = din("lru_conv_w", [L, 4, W]); lru_conv_b = din("lru_conv_b", [L, W])
    lru_wa = din("lru_wa", [L, 8, 64, 64]); lru_ba = din("lru_ba", [L, W]); lru_wx = din("lru_wx", [L, 8, 64, 64]); lru_bx = din("lru_bx", [L, W])
    lru_lambda = din("lru_lambda", [L, W])
    w_branch = din("w_branch", [L, 4, W, D]); w_gate = din("w_gate", [L, D, 4 * D]); b_gate = din("b_gate", [L, 4 * D])
    w_out = din("w_out", [L, D, D]); w_mlp1 = din("w_mlp1", [L, D, 4 * D]); w_mlp2 = din("w_mlp2", [L, 4 * D, D])

    o_yp = dout("o_yp", [SEQ, D]); o_ys = dout("o_ys", [NSM, D])
    o_hg = dout("o_hg", [L, 1 + NSQ, 4, 128, 128]); o_rw = dout("o_rw", [L, 1 + NSQ, 8, 64, 64])
    o_sh = dout("o_sh", [L, 1 + NSQ, RW_COLS]); o_cf = dout("o_cf", [L, 1 + NSQ, 30, W])
    o_lh = dout("o_lh", [L, 1 + NSQ, W]); o_lc = dout("o_lc", [L, 1 + NSQ, 3, W])
    out_bufs = []

    def sb(name, shape, dt):
        return T(st.enter_context(nc.sbuf_tensor(name, list(shape), dt)), Buf(name))

    ARENA_E = 38 * 1024
    arena = st.enter_context(nc.sbuf_tensor("arena", [128, ARENA_E], BF16))
    ar = {"off": 0, "bufs": [], "fence": []}

    def aalloc(name, shape, dt, parts=128):
        n = 1
        for s_ in shape:
            n *= s_
        ne = n * (2 if dt == F32 else 1)
        ne = (ne + 15) // 16 * 16
        off = ar["off"]
        assert off + ne <= ARENA_E, (name, off, ne)
        ar["off"] = off + ne
        ar["hw"] = max(ar.get("hw", 0), off + ne)
        v = arena[0:parts, off:off + ne]
        if dt == F32:
            v = v.bitcast(F32)
        v = v[:, 0:n]
        if len(shape) == 1:
            v = v.rearrange("p (a b) -> p a b", a=1)
        elif len(shape) == 2:
            v = v.rearrange("p (a b) -> p a b", a=shape[0])
        elif len(shape) == 3:
            v = v.rearrange("p (a b c) -> p a b c", a=shape[0], b=shape[1])
        b = Buf(name, ar["fence"])
        ar["bufs"].append(b)
        return T(v, b)

    def areset():
        ops = []
        for b in ar["bufs"]:
            if b.last_w is not None:
                ops.append(b.last_w)
            ops.extend(b.readers)
        best = {}
        for o in ops + ar["fence"]:
            k, v = Prog._key(o)
            if k not in best or Prog._key(best[k])[1] < v:
                best[k] = o
        ar["fence"] = list(best.values())
        ar["bufs"] = []
        ar["off"] = 0

    psum = []
    for i in range(8):
        psum.append(T(st.enter_context(nc.psum_tensor("ps%d" % i, [128, 512], F32)), Buf("ps%d" % i)))
    pctr = [0]

    pheld = set()

    def PS(hold=False):
        assert len(pheld) < 8, 'all PSUM banks held'
        while (pctr[0] % 8) in pheld:
            pctr[0] += 1
        i = pctr[0] % 8
        pctr[0] += 1
        if hold:
            pheld.add(i)
        return psum[i]

    def PREL(t):
        pheld.discard(psum.index(t))

    def bl(ts_):
        return [t.b if isinstance(t, T) else t for t in ts_]

    def MM(out, lhsT, rhs, start, stop, R, Wr):
        P.op("pe", lambda e: e.matmul(out, lhsT=lhsT, rhs=rhs, start=start, stop=stop), bl(R), bl(Wr))

    def TR(out, in_, ident, R, Wr):
        P.op("pe", lambda e: e.transpose(out=out, in_=in_, identity=ident), bl(R), bl(Wr))

    def ACT(out, in_, func, R, Wr, scale=1.0, bias=0.0, eng="act"):
        P.op(eng, lambda e: e.activation(out=out, in_=in_, func=func, bias=bias, scale=scale), bl(R), bl(Wr))

    def TT(out, a, b, op, R, Wr, eng="dve"):
        P.op(eng, lambda e: e.tensor_tensor(out=out, in0=a, in1=b, op=op), bl(R), bl(Wr))

    def TSC(out, a, s1, s2, op0, op1, R, Wr, eng="dve"):
        if op1 is None:
            P.op(eng, lambda e: e.tensor_scalar(out=out, in0=a, scalar1=s1, scalar2=None, op0=op0), bl(R), bl(Wr))
        else:
            P.op(eng, lambda e: e.tensor_scalar(out=out, in0=a, scalar1=s1, scalar2=s2, op0=op0, op1=op1), bl(R), bl(Wr))

    def STT(out, a, s, b, op0, op1, R, Wr):
        P.op("dve", lambda e: e.scalar_tensor_tensor(out=out, in0=a, scalar=s, in1=b, op0=op0, op1=op1), bl(R), bl(Wr))

    def CP(out, in_, R, Wr, eng="dve"):
        if eng == "act":
            P.op("act", lambda e: e.copy(out=out, in_=in_), bl(R), bl(Wr))
        else:
            P.op(eng, lambda e: e.tensor_copy(out=out, in_=in_), bl(R), bl(Wr))

    def MSET(ap, val, Wr, eng="pool"):
        P.op(eng, lambda e: e.memset(ap, val), [], bl(Wr))

    def SCAN(out, d0, d1, init, R, Wr):
        P.op("dve", lambda e: e.tensor_tensor_scan(out=out, data0=d0, data1=d1, initial=init, op0=ALU.mult, op1=ALU.add), bl(R), bl(Wr))

    def RED(out, in_, R, Wr):
        P.op("dve", lambda e: e.tensor_reduce(out=out, in_=in_, axis=AX.X, op=ALU.add), bl(R), bl(Wr))

    def RCP(out, in_, R, Wr):
        P.op("dve", lambda e: e.reciprocal(out=out, in_=in_), bl(R), bl(Wr))

    dctr = [0]

    def DMA(out, in_, R, Wr, eng="sp", sem=None):
        if sem is None:
            sem = "g%d" % (dctr[0] % 12)
            dctr[0] += 1
        P.dma(eng, lambda e: e.dma_start(out=out, in_=in_), sem, bl(R), bl(Wr))

    def OUT(out, in_, R, name):
        b = Buf("out_" + name)
        out_bufs.append(b)
        sem = "o%d" % (dctr[0] % 8)
        dctr[0] += 1
        P.dma("sp", lambda e: e.dma_start(out=out, in_=in_), sem, bl(R), [b])

    def DBG(name, t, ap, shape):
        if debug is None or name not in debug:
            return
        d = dout("dbg_" + name, shape)
        dbg_outs[name] = shape
        OUT(d, ap, [t], "dbg_" + name)

    ident_f = sb("ident_f", [128, 128], F32)
    ident_b = sb("ident_b", [128, 128], BF16)
    ones_b = sb("ones_b", [128, 128], BF16)
    MSET(ident_f[:], 0.0, [ident_f])
    P.op("pool", lambda e: e.affine_select(out=ident_f[:], in_=ident_f[:], pattern=[[-1, 128]], compare_op=ALU.not_equal,
                                           fill=1.0, base=0, channel_multiplier=1), [ident_f.b], [ident_f.b])
    CP(ident_b[:], ident_f[:], [ident_f], [ident_b], eng="pool")
    MSET(ones_b[:], 1.0, [ones_b])
    mask_incl = sb("mask_incl", [64, 64], F32)
    mask_strict = sb("mask_strict", [64, 64], F32)
    for mt, base in ((mask_incl, 0), (mask_strict, -1)):
        MSET(mt[:], 1.0, [mt])
        P.op("pool", lambda e, mt=mt, base=base: e.affine_select(out=mt[:], in_=mt[:], pattern=[[1, 64]], compare_op=ALU.is_ge,
                                                                 fill=0.0, base=base, channel_multiplier=-1), [mt.b], [mt.b])
    cmask = sb("cmask", [128, NTMAX], BF16)
    MSET(cmask[:], 1.0, [cmask])
    MSET(cmask[:, 0:NPT].rearrange("p (c t) -> p c t", t=CH)[:, :, 0:1], 0.0, [cmask])
    MSET(cmask[:, NPT:NTMAX].rearrange("p (c t) -> p c t", t=TS)[:, :, 0:1], 0.0, [cmask])

    cmask_h = sb("cmask_h", [128, NTMAX], BF16)
    MSET(cmask_h[:], 1.0, [cmask_h])
    MSET(cmask_h[:, 0:NPT].rearrange("p (c t) -> p c t", t=32)[:, :, 0:1], 0.0, [cmask_h])
    MSET(cmask_h[:, NPT:NTMAX].rearrange("p (c t) -> p c t", t=TS)[:, :, 0:1], 0.0, [cmask_h])
    mask_incl_i = sb("mask_incl_i", [32, 8, 32], mybir.dt.uint8)
    CP(mask_incl_i[:], mask_incl[0:32, 0:32].unsqueeze(1).to_broadcast([32, 8, 32]), [mask_incl], [mask_incl_i], eng="pool")
    x = sb("x", [128, 8, NTMAX], F32)
    h = sb("h", [128, 8, NTMAX], BF16)
    y = sb("y", [128, 16, NTMAX], BF16)
    NSLOT = 4
    wslots = [sb("wslot%d" % i, [128, 4096], BF16) for i in range(NSLOT)]
    wctr = [0]
    mod = sb("mod", [128, L, 48, 1 + NSQ], F32)
    PCOLS = {}
    pc = [0]

    def pcol(name, n):
        PCOLS[name] = (pc[0], n)
        pc[0] += n

    for nm, n in (("ada_b", 48), ("norm_mix_g", 8), ("norm_mlp_g", 8), ("hg_lower", 4), ("hg_norm_g", 4), ("rw_mu", 14),
                  ("rw_w0", 4), ("rw_a0", 4), ("rw_k_k", 4), ("rw_k_a", 4), ("rw_r_k", 4), ("cf_dw", 124), ("cf_dw_b", 4),
                  ("cf_ln_g", 4), ("cf_ln_b", 4), ("lru_conv_w", 16), ("lru_conv_b", 4), ("lru_ba", 4), ("lru_bx", 4),
                  ("lru_lambda", 4), ("b_gate", 32), ("rw_ln_g", 4), ("rw_ln_b", 4)):
        pcol(nm, n)
    NPC = pc[0]
    ptab = sb("ptab", [128, L, NPC + 40], F32)
    DER = NPC

    def pv(l, name, j=0, n=1):
        o, _ = PCOLS[name]
        return ptab[:, l, o + j:o + j + n]

    def dv(l, j, n=1):
        return ptab[:, l, DER + j:DER + j + n]

    def WLOAD(src3, kc, cols):
        sl = wslots[wctr[0] % NSLOT]
        wctr[0] += 1
        v = sl.ap[:, 0:kc * cols].rearrange("p (k c) -> p k c", k=kc)
        P.dma("pool", lambda e: e.dma_start(out=v, in_=src3), "w%d" % ((wctr[0] - 1) % NSLOT), [], [sl.b])
        return T(v, sl.b)

    def kview(w2d, c0, cols):
        return w2d.rearrange("(k p) c -> p k c", p=128)[:, :, c0:c0 + cols]

    prow = [sb("prow%d" % i, [128, 128], F32) for i in range(3)]
    for l in range(L):
        plist = [("ada_b", ada_b[l]), ("norm_mix_g", norm_mix_g[l]), ("norm_mlp_g", norm_mlp_g[l]), ("hg_lower", hg_lower[l]),
                 ("hg_norm_g", hg_norm_g[l]), ("rw_mu", rw_mu[l]), ("rw_w0", rw_w0[l]), ("rw_a0", rw_a0[l]), ("rw_k_k", rw_k_k[l]),
                 ("rw_k_a", rw_k_a[l]), ("rw_r_k", rw_r_k[l]), ("cf_dw", cf_dw[l].rearrange("j c -> (j c)")), ("cf_dw_b", cf_dw_b[l]),
                 ("cf_ln_g", cf_ln_g[l]), ("cf_ln_b", cf_ln_b[l]), ("lru_conv_w", lru_conv_w[l].rearrange("j c -> (j c)")),
                 ("lru_conv_b", lru_conv_b[l]), ("lru_ba", lru_ba[l]), ("lru_bx", lru_bx[l]), ("lru_lambda", lru_lambda[l]),
                 ("b_gate", b_gate[l]), ("rw_ln_g", rw_ln_g[l]), ("rw_ln_b", rw_ln_b[l])]
        for nm, src in plist:
            o, n = PCOLS[nm]
            rows = src.rearrange("(r p) -> r p", p=128)
            r = 0
            while r < n:
                g = (o + r) // 128
                take = min(n - r, 128 - (o + r) % 128)
                DMA(prow[g][(o + r) % 128:(o + r) % 128 + take, :], rows[r:r + take, :], [], [prow[g]])
                r += take
        for g in range(3):
            nr = min(128, NPC - g * 128)
            ps = PS()
            TR(ps[:, 0:nr], prow[g][0:nr, :], ident_f[0:nr, 0:nr], [prow[g], ident_f], [ps])
            CP(ptab[:, l, g * 128:g * 128 + nr], ps[:, 0:nr], [ps], [ptab])
    for l in range(L):
        if l == 0:
            MSET(dv(0, 0, 4), 0.0, [ptab], eng="dve")
        else:
            TT(dv(l, 0, 4), pv(l, "hg_lower", 0, 4), pv(0, "hg_lower", 0, 4), ALU.subtract, [ptab], [ptab])
            ACT(dv(l, 0, 4), dv(l, 0, 4), AF.Sigmoid, [ptab], [ptab])
        TSC(dv(l, 4, 4), dv(l, 0, 4), -1.0, 1.0, ALU.mult, ALU.add, [ptab], [ptab])
        ACT(dv(l, 8, 4), pv(l, "lru_lambda", 0, 4), AF.Exp, [ptab], [ptab], scale=-1.0)
        ACT(dv(l, 8, 4), dv(l, 8, 4), AF.Ln, [ptab], [ptab], bias=1.0)
        TSC(dv(l, 12, 4), dv(l, 8, 4), -16.0, None, ALU.mult, None, [ptab], [ptab])
        TSC(dv(l, 8, 4), dv(l, 8, 4), -8.0, None, ALU.mult, None, [ptab], [ptab])
        TSC(dv(l, 16, 14), pv(l, "rw_mu", 0, 14), -1.0, 1.0, ALU.mult, ALU.add, [ptab], [ptab])
    gfin = sb("gfin", [128, D], F32)
    DMA(gfin[:], norm_final_g.rearrange("(o d) -> o d", o=1).to_broadcast([128, D]), [], [gfin])
    rwup = sb("rwup", [128, L, 3, W], BF16)
    lrug = sb("lrug", [128, L, 2, 4, 128], BF16)
    MSET(lrug[:], 0.0, [lrug])
    for l in range(L):
        DMA(rwup[0:64, l, 0, :], rw_w_up[l], [], [rwup], eng="pool", sem="rs")
        DMA(rwup[64:128, l, 1, :], rw_a_up[l], [], [rwup], eng="pool", sem="rs")
        DMA(rwup[:, l, 2, :], rw_g_up[l], [], [rwup], eng="pool", sem="rs")
        for gi, wsrc in enumerate((lru_wa, lru_wx)):
            for n in range(8):
                j, hf = n // 2, n % 2
                DMA(lrug[hf * 64:hf * 64 + 64, l, gi, j, hf * 64:hf * 64 + 64], wsrc[l, n], [], [lrug], eng="pool", sem="rs")

    ctok = aalloc("ctok", [D], F32, parts=1 + NSQ)
    cT = sb("cT", [128, 8, 1 + NSQ], BF16)
    DMA(ctok[:, 0, :], cc, [], [ctok])
    ps = PS()
    for kc in range(8):
        TR(ps[:, kc * 17:(kc + 1) * 17], ctok[:, 0, kc * 128:(kc + 1) * 128], ident_f[0:17, 0:17], [ctok, ident_f], [ps])
    ACT(cT[:].rearrange("p a b -> p (a b)"), ps[:, 0:136], AF.Silu, [ps], [cT])
    modA = sb("modA", [128, L, 2, 8, 1 + NSQ], F32)
    ada_state = {l: 0 for l in range(L)}

    def ADA_PIECES(l, n):
        while n > 0 and ada_state[l] < 12:
            g = ada_state[l]
            ada_state[l] += 1
            n -= 1
            wt = WLOAD(kview(ada_w[l], g * 512, 512), 8, 512)
            ps = PS()
            for j in range(4):
                for kc in range(8):
                    MM(ps[:, j * 17:(j + 1) * 17], wt[:, kc, j * 128:(j + 1) * 128], cT[:, kc, :], kc == 0, kc == 7, [wt, cT], [ps])
            for j in range(4):
                fc = g * 4 + j
                ACT(mod[:, l, fc, :], ps[:, j * 17:(j + 1) * 17], AF.Identity, [ps, ptab], [mod], bias=pv(l, "ada_b", fc))
            if ada_state[l] == 12:
                for which, gname, sc0 in ((0, "norm_mix_g", 8), (1, "norm_mlp_g", 32)):
                    for kc in range(8):
                        TSC(modA[:, l, which, kc, :], mod[:, l, sc0 + kc, :], 1.0, pv(l, gname, kc), ALU.add, ALU.mult, [mod, ptab], [modA])

    ADA_PIECES(0, 12)
    areset()

    hgS = sb("hgS", [128, L, 4, 128], F32)
    rwST = sb("rwST", [128, L, 4, 64], F32)
    rwprev = sb("rwprev", [128, L, 14], F32)
    cfhist = sb("cfhist", [128, L, 4, 30], BF16)
    lruh = sb("lruh", [128, L, 4], F32)
    lruhist = sb("lruhist", [128, L, 4, 3], F32)
    for t_ in (hgS, rwST, rwprev, cfhist, lruh, lruhist):
        MSET(t_[:], 0.0, [t_], eng="dve")
    shcol = sb("shcol", [128, L, 14, 1 + NSQ], F32)
    lhcol = sb("lhcol", [128, L, 4, 1 + NSQ], F32)

    def slabs(NT):
        return [(0, NPT)] + ([(NPT, NSM)] if NT > NPT else [])

    def NORM(l, which, NT):
        sh0 = 0 if which == 0 else 24
        for (t0, n) in slabs(NT):
            sq = aalloc("sq", [8, n], BF16)
            ACT(sq[:], x[:, :, t0:t0 + n], AF.Square, [x], [sq])
            ps = PS()
            for kc in range(8):
                MM(ps[:, 0:n], ones_b[:], sq[:, kc, :], kc == 0, kc == 7, [ones_b, sq], [ps])
            rstd = aalloc("rstd", [n], F32)
            ACT(rstd[:, 0, :], ps[:, 0:n], AF.Ln, [ps], [rstd], scale=1.0 / D, bias=EPS)
            ACT(rstd[:, 0, :], rstd[:, 0, :], AF.Exp, [rstd], [rstd], scale=-0.5)
            if t0 == 0:
                for kc in range(8):
                    xk = aalloc("xn%d" % kc, [n], F32)
                    TT(xk[:, 0, :], x[:, kc, t0:t0 + n], rstd[:, 0, :], ALU.mult, [x, rstd], [xk])
                    ACT(h[:, kc, 0:n], xk[:, 0, :], AF.Identity, [xk, modA, mod], [h],
                        scale=modA[:, l, which, kc, 0:1], bias=mod[:, l, sh0 + kc, 0:1])
                continue
            xn = aalloc("xn", [8, n], F32)
            TT(xn[:], x[:, :, t0:t0 + n], rstd[:, 0:1, :].to_broadcast([128, 8, n]), ALU.mult, [x, rstd], [xn])
            for kc in range(8):
                if t0 == 0:
                    TSC(h[:, kc, 0:n], xn[:, kc, :], modA[:, l, which, kc, 0:1], mod[:, l, sh0 + kc, 0:1], ALU.mult, ALU.add,
                        [xn, modA, mod], [h])
                else:
                    v3 = xn[:, kc, :].rearrange("p (q t) -> p q t", t=TS)
                    TT(v3, v3, modA[:, l, which, kc, 1:1 + NSQ].unsqueeze(2).to_broadcast([128, NSQ, TS]), ALU.mult, [xn, modA], [xn])
                    TT(h[:, kc, t0:t0 + n].rearrange("p (q t) -> p q t", t=TS), v3,
                       mod[:, l, sh0 + kc, 1:1 + NSQ].unsqueeze(2).to_broadcast([128, NSQ, TS]), ALU.add, [xn, mod], [h])

    def RESID(l, g0, dc, psb, NT):
        for (t0, n), (ps, c0) in zip(slabs(NT), psb):
            if t0 == 0:
                STT(x[:, dc, 0:n], ps[:, c0:c0 + n], mod[:, l, g0 + dc, 0:1], x[:, dc, 0:n], ALU.mult, ALU.add, [ps, mod, x], [x])
            else:
                tmp = aalloc("rtmp", [n], F32)
                TT(tmp[:, 0, :].rearrange("p (q t) -> p q t", t=TS), ps[:, c0:c0 + n].rearrange("p (q t) -> p q t", t=TS),
                   mod[:, l, g0 + dc, 1:1 + NSQ].unsqueeze(2).to_broadcast([128, NSQ, TS]), ALU.mult, [ps, mod], [tmp])
                TT(x[:, dc, t0:t0 + n], x[:, dc, t0:t0 + n], tmp[:, 0, :], ALU.add, [x, tmp], [x])

    def PROJ(wt, j, NT, rhs_t, nk, ps, c0=0):
        for kc in range(nk):
            MM(ps[:, c0:c0 + NT], wt[:, kc, j * 128:(j + 1) * 128], rhs_t[:, kc, 0:NT], kc == 0, kc == nk - 1, [wt, rhs_t], [ps])

    def MERGE_MLP(l, NT):
        sl = slabs(NT)
        mg = aalloc("mg", [8, NT], F32)
        mgb = aalloc("mgb", [8, NT], BF16)
        for b in range(4):
            wb = WLOAD(w_branch[l, b].rearrange("(k p) c -> p k c", p=128), 4, D)
            for half in range(2):
                wg = WLOAD(kview(w_gate[l], b * D + half * 512, 512), 8, 512)
                for jj in range(4):
                    dc = half * 4 + jj
                    for (t0, n) in sl:
                        pb = PS()
                        for kc in range(4):
                            MM(pb[:, 0:n], wb[:, kc, dc * 128:(dc + 1) * 128], y[:, b * 4 + kc, t0:t0 + n], kc == 0, kc == 3, [wb, y], [pb])
                        pg = PS()
                        for kc in range(8):
                            MM(pg[:, 0:n], wg[:, kc, jj * 128:(jj + 1) * 128], h[:, kc, t0:t0 + n], kc == 0, kc == 7, [wg, h], [pg])
                        sg = aalloc("sg", [n], F32) if False else None
                        sgt = sgbuf
                        ACT(sgt[:, 0:n], pg[:, 0:n], AF.Sigmoid, [pg, ptab], [sgt], bias=pv(l, "b_gate", b * 8 + dc))
                        if b == 0:
                            TT(mg[:, dc, t0:t0 + n], sgt[:, 0:n], pb[:, 0:n], ALU.mult, [sgt, pb], [mg])
                        else:
                            TT(sgt[:, 0:n], sgt[:, 0:n], pb[:, 0:n], ALU.mult, [sgt, pb], [sgt])
                            if b < 3:
                                TT(mg[:, dc, t0:t0 + n], mg[:, dc, t0:t0 + n], sgt[:, 0:n], ALU.add, [mg, sgt], [mg])
                            else:
                                TT(mgb[:, dc, t0:t0 + n], mg[:, dc, t0:t0 + n], sgt[:, 0:n], ALU.add, [mg, sgt], [mgb])
        for half in range(2):
            wo = WLOAD(kview(w_out[l], half * 512, 512), 8, 512)
            for jj in range(4):
                dc = half * 4 + jj
                psb = []
                for (t0, n) in sl:
                    po = PS()
                    for kc in range(8):
                        MM(po[:, 0:n], wo[:, kc, jj * 128:(jj + 1) * 128], mgb[:, kc, t0:t0 + n], kc == 0, kc == 7, [wo, mgb], [po])
                    psb.append((po, 0))
                RESID(l, 16, dc, psb, NT)
        areset()
        NORM(l, 1, NT)
        areset()
        hid = aalloc("hid", [32, NT], BF16)
        for pi in range(8):
            w1 = WLOAD(kview(w_mlp1[l], pi * 512, 512), 8, 512)
            for jj in range(4):
                for (t0, n) in sl:
                    pp = PS()
                    for kc in range(8):
                        MM(pp[:, 0:n], w1[:, kc, jj * 128:(jj + 1) * 128], h[:, kc, t0:t0 + n], kc == 0, kc == 7, [w1, h], [pp])
                    ACT(sgbuf[:, 0:n], pp[:, 0:n], AF.Relu, [pp], [sgbuf])
                    TT(hid[:, pi * 4 + jj, t0:t0 + n], sgbuf[:, 0:n], sgbuf[:, 0:n], ALU.mult, [sgbuf], [hid])
        for cb in range(2):
            for kb in range(4):
                w2 = WLOAD(w_mlp2[l].rearrange("(k p) c -> p k c", p=128)[:, kb * 8:(kb + 1) * 8, cb * 512:(cb + 1) * 512], 8, 512)
                for jj in range(4):
                    for kc in range(8):
                        first = (kb == 0 and kc == 0)
                        last = (kb == 3 and kc == 7)
                        MM(psum[jj][:, 0:NPT], w2[:, kc, jj * 128:(jj + 1) * 128], hid[:, kb * 8 + kc, 0:NPT], first, last, [w2, hid], [psum[jj]])
                        if NT > NPT:
                            MM(psum[4 + jj][:, 0:NSM], w2[:, kc, jj * 128:(jj + 1) * 128], hid[:, kb * 8 + kc, NPT:NT], first, last,
                               [w2, hid], [psum[4 + jj]])
            for jj in range(4):
                RESID(l, 40, cb * 4 + jj, [(psum[jj], 0), (psum[4 + jj], 0)], NT)
        pctr[0] = 0
        areset()

    sgbuf = sb("sgbuf", [128, NPT], F32)

    xtmp = sb("xtmp", [128, NSM], F32)
    sgd2 = sb("sgd2", [128, NTMAX], BF16)

    def arelease(mark):
        ops = []
        for b in ar["bufs"]:
            if b.last_w is not None:
                ops.append(b.last_w)
            ops.extend(b.readers)
        best = {}
        for o in ops + ar["fence"]:
            k, v = Prog._key(o)
            if k not in best or Prog._key(best[k])[1] < v:
                best[k] = o
        ar["fence"] = list(best.values())
        ar["off"] = mark

    diag = [sb("diag%d" % i, [128, 128], BF16) for i in range(6)]
    dgc = [0]
    GC = 0.7978845608028654

    def chunks_of(NT, CC=CH):
        lst = [(c * CC, CC, "p", c) for c in range(NPT // CC)]
        if NT > NPT:
            lst += [(NPT + q * TS, TS, "s", q) for q in range(NSQ)]
        return lst

    def bview(ap2, n, C):
        return ap2.rearrange("p (c t) -> p c t", t=C)

    def TOK_OUT(src_t, src_ap, ncols, dst_rows, nm):
        ps = PS()
        for j in range(4):
            TR(ps[0:ncols, j * 128:(j + 1) * 128], src_ap(j), ident_f[:], [src_t[j] if isinstance(src_t, list) else src_t, ident_f], [ps])
        stg = aalloc("stg_" + nm, [W], F32)
        CP(stg[0:ncols, 0, :], ps[0:ncols, :], [ps], [stg], eng="act")
        for (r0, r1, dst) in dst_rows:
            OUT(dst, stg[r0:r1, 0, :], [stg], nm)

    def MIX_CF(ti, l, NT):
        last = ti == NTILE - 1
        sl = slabs(NT)
        wv = WLOAD(kview(w_in[l], 3840, 512), 8, 512)
        wg = WLOAD(kview(w_in[l], 4352, 512), 8, 512)
        u32 = aalloc("u32", [4, NT], F32)
        uxp = aalloc("uxp", [4, 30 + NPT], BF16)
        uxs = aalloc("uxs", [4, NSQ * 38], BF16) if last else None
        if last:
            for g in range(4):
                hst = aalloc("hst%d" % g, [W], F32, parts=120)
                DMA(hst[:, 0, :], s_cf[l, g * 4:(g + 1) * 4].rearrange("q r c -> (q r) c"), [], [hst])
                ps = PS()
                for j in range(4):
                    TR(ps[:, j * 120:(j + 1) * 120], hst[:, 0, j * 128:(j + 1) * 128], ident_f[0:120, 0:120], [hst, ident_f], [ps])
                for j in range(4):
                    CP(bview(uxs[:, j, :], NSQ, 38)[:, g * 4:(g + 1) * 4, 0:30], bview(ps[:, j * 120:(j + 1) * 120], 4, 30), [ps], [uxs],
                       eng=("act" if j % 2 else "dve"))
            OUT(o_cf[l, 1:1 + NSQ, 0:22, :], s_cf[l, :, 8:30, :], [], "cfcopy")
        for j in range(4):
            CP(uxp[:, j, 0:30], cfhist[:, l, j, :], [cfhist], [uxp])
            for (t0, n) in sl:
                p1 = PS(); p2 = PS()
                for kc in range(8):
                    MM(p1[:, 0:n], wv[:, kc, j * 128:(j + 1) * 128], h[:, kc, t0:t0 + n], kc == 0, kc == 7, [wv, h], [p1])
                for kc in range(8):
                    MM(p2[:, 0:n], wg[:, kc, j * 128:(j + 1) * 128], h[:, kc, t0:t0 + n], kc == 0, kc == 7, [wg, h], [p2])
                ACT(sgbuf[:, 0:n], p2[:, 0:n], AF.Sigmoid, [p2], [sgbuf])
                TT(u32[:, j, t0:t0 + n], sgbuf[:, 0:n], p1[:, 0:n], ALU.mult, [sgbuf, p1], [u32])
                if t0 == 0:
                    CP(uxp[:, j, 30:30 + NPT], u32[:, j, 0:NPT], [u32], [uxp], eng="act")
                else:
                    CP(bview(uxs[:, j, :], NSQ, 38)[:, :, 30:38], bview(u32[:, j, NPT:NT], NSQ, TS), [u32], [uxs], eng="act")
            CP(cfhist[:, l, j, :], uxp[:, j, NPT:NPT + 30], [uxp], [cfhist])
        yc = aalloc("yc", [4, NT], F32)
        ycb = aalloc("ycb", [4, NT], BF16)
        ycs = aalloc("ycs", [4, NT], BF16)
        for j in range(4):
            p1 = PS(); p2 = PS() if last else None
            for tap in range(31):
                dg = diag[dgc[0] % 6]; dgc[0] += 1
                ACT(dg[:], ident_b[:], AF.Identity, [ident_b, ptab], [dg], scale=pv(l, "cf_dw", tap * 4 + j))
                MM(p1[:, 0:NPT], dg[:], uxp[:, j, tap:tap + NPT], tap == 0, tap == 30, [dg, uxp], [p1])
                if last:
                    MM(p2[:, 0:NSM], dg[:], bview(uxs[:, j, :], NSQ, 38)[:, :, tap:tap + TS], tap == 0, tap == 30, [dg, uxs], [p2])
            for (t0, n), pp in zip(sl, (p1, p2)):
                ACT(yc[:, j, t0:t0 + n], pp[:, 0:n], AF.Identity, [pp, ptab], [yc], bias=pv(l, "cf_dw_b", j))
                CP(ycb[:, j, t0:t0 + n], yc[:, j, t0:t0 + n], [yc], [ycb])
                ACT(ycs[:, j, t0:t0 + n], yc[:, j, t0:t0 + n], AF.Square, [yc], [ycs])
        for (t0, n) in sl:
            pm = PS(); pq = PS()
            for j in range(4):
                MM(pm[:, 0:n], ones_b[:], ycb[:, j, t0:t0 + n], j == 0, j == 3, [ones_b, ycb], [pm])
            for j in range(4):
                MM(pq[:, 0:n], ones_b[:], ycs[:, j, t0:t0 + n], j == 0, j == 3, [ones_b, ycs], [pq])
            mean = aalloc("cfmean", [n], F32)
            var = aalloc("cfvar", [n], F32)
            ACT(mean[:, 0, :], pm[:, 0:n], AF.Copy, [pm], [mean], scale=1.0 / W)
            TT(var[:, 0, :], mean[:, 0, :], mean[:, 0, :], ALU.mult, [mean], [var])
            STT(var[:, 0, :], pq[:, 0:n], 1.0 / W, var[:, 0, :], ALU.mult, ALU.subtract, [pq, var], [var])
            ACT(var[:, 0, :], var[:, 0, :], AF.Ln, [var], [var], bias=1e-5)
            ACT(var[:, 0, :], var[:, 0, :], AF.Exp, [var], [var], scale=-0.5)
            for j in range(4):
                TT(yc[:, j, t0:t0 + n], yc[:, j, t0:t0 + n], mean[:, 0, :], ALU.subtract, [yc, mean], [yc])
                TT(yc[:, j, t0:t0 + n], yc[:, j, t0:t0 + n], var[:, 0, :], ALU.mult, [yc, var], [yc])
                ACT(y[:, 8 + j, t0:t0 + n], yc[:, j, t0:t0 + n], AF.Silu, [yc, ptab], [y], scale=pv(l, "cf_ln_g", j), bias=pv(l, "cf_ln_b", j))
        if last:
            TOK_OUT(u32, lambda j: u32[:, j, NPT - 30:NPT], 30, [(0, 30, o_cf[l, 0])], "cfp")
            TOK_OUT(u32, lambda j: u32[:, j, NPT:NT], NSM, [(q * TS, (q + 1) * TS, o_cf[l, 1 + q, 22:30, :]) for q in range(NSQ)], "cfs")

    def MIX_LRU(ti, l, NT):
        last = ti == NTILE - 1
        sl = slabs(NT)
        wx_ = WLOAD(kview(w_in[l], 4864, 512), 8, 512)
        wgl = WLOAD(kview(w_in[l], 5376, 512), 8, 512)
        xl32 = [aalloc("xl32_%d" % j, [NT], F32) for j in range(4)]
        exp_ = [aalloc("lext%d" % j, [3 + NPT], F32) for j in range(4)]
        exs = aalloc("lexs", [4, NSQ * 11], F32) if last else None
        xc = [aalloc("lxc%d" % j, [NT], F32) for j in range(4)]
        xcb = [aalloc("lxcb%d" % j, [NT], BF16) for j in range(4)]
        hs = [aalloc("lhs%d" % j, [NT], F32) for j in range(4)]
        tsets = [(aalloc("lta%d" % i, [NT], F32), aalloc("ltb%d" % i, [NT], F32), aalloc("ltc%d" % i, [NT], F32)) for i in range(1 if last else 2)]
        hs0 = aalloc("lhs0", [4, NSQ], F32) if last else None
        if last:
            hst = aalloc("lhst", [W], F32, parts=48)
            DMA(hst[:, 0, :], s_lc[l].rearrange("q r c -> (q r) c"), [], [hst])
            ps = PS()
            for j in range(4):
                TR(ps[:, j * 48:(j + 1) * 48], hst[:, 0, j * 128:(j + 1) * 128], ident_f[0:48, 0:48], [hst, ident_f], [ps])
            for j in range(4):
                CP(bview(exs[:, j, :], NSQ, 11)[:, :, 0:3], bview(ps[:, j * 48:(j + 1) * 48], NSQ, 3), [ps], [exs])
            hh = aalloc("lhh", [W], F32, parts=NSQ)
            DMA(hh[:, 0, :], s_lh[l], [], [hh])
            ps = PS()
            for j in range(4):
                TR(ps[:, j * 16:(j + 1) * 16], hh[:, 0, j * 128:(j + 1) * 128], ident_f[0:16, 0:16], [hh, ident_f], [ps])
            CP(hs0[:].rearrange("p a b -> p (a b)"), ps[:, 0:64], [ps], [hs0])
        def LJ(j, ta, tb, tcc):
            CP(exp_[j][:, 0, 0:3], lruhist[:, l, j, :], [lruhist], [exp_[j]])
            for (t0, n) in sl:
                p1 = PS()
                for kc in range(8):
                    MM(p1[:, 0:n], wx_[:, kc, j * 128:(j + 1) * 128], h[:, kc, t0:t0 + n], kc == 0, kc == 7, [wx_, h], [p1])
                CP(xl32[j][:, 0, t0:t0 + n], p1[:, 0:n], [p1], [xl32[j]], eng="act")
                if t0 == 0:
                    CP(exp_[j][:, 0, 3:3 + NPT], xl32[j][:, 0, 0:NPT], [xl32[j]], [exp_[j]])
                    src = lambda tap: exp_[j][:, 0, tap:tap + NPT]
                    dst = xc[j][:, 0, 0:NPT]
                    rd = exp_[j]
                else:
                    CP(bview(exs[:, j, :], NSQ, 11)[:, :, 3:11], bview(xl32[j][:, 0, NPT:NT], NSQ, TS), [xl32[j]], [exs])
                    src = lambda tap: bview(exs[:, j, :], NSQ, 11)[:, :, tap:tap + TS]
                    dst = bview(xc[j][:, 0, NPT:NT], NSQ, TS)
                    rd = exs
                TSC(dst, src(0), pv(l, "lru_conv_w", 0 * 4 + j), pv(l, "lru_conv_b", j), ALU.mult, ALU.add, [rd, ptab], [xc[j]])
                for tap in range(1, 4):
                    STT(dst, src(tap), pv(l, "lru_conv_w", tap * 4 + j), dst, ALU.mult, ALU.add, [rd, ptab, xc[j]], [xc[j]])
            CP(lruhist[:, l, j, :], exp_[j][:, 0, NPT:NPT + 3], [exp_[j]], [lruhist])
            yield
            CP(xcb[j][:, 0, 0:NT], xc[j][:, 0, 0:NT], [xc[j]], [xcb[j]], eng="act")
            for (t0, n) in sl:
                pa = PS(); px = PS(); pgl = PS()
                MM(pa[:, 0:n], lrug[:, l, 0, j, :], xcb[j][:, 0, t0:t0 + n], True, True, [lrug, xcb[j]], [pa])
                MM(px[:, 0:n], lrug[:, l, 1, j, :], xcb[j][:, 0, t0:t0 + n], True, True, [lrug, xcb[j]], [px])
                for kc in range(8):
                    MM(pgl[:, 0:n], wgl[:, kc, j * 128:(j + 1) * 128], h[:, kc, t0:t0 + n], kc == 0, kc == 7, [wgl, h], [pgl])
                A_ = ta[:, 0, t0:t0 + n]; B_ = tb[:, 0, t0:t0 + n]; C_ = tcc[:, 0, t0:t0 + n]
                yield
                ACT(C_, pa[:, 0:n], AF.Sigmoid, [pa, ptab], [tcc], bias=pv(l, "lru_ba", j))
                ACT(A_, C_, AF.Exp, [tcc, ptab], [ta], scale=dv(l, 8 + j))
                ACT(C_, C_, AF.Exp, [tcc, ptab], [tcc], scale=dv(l, 12 + j))
                ACT(C_, C_, AF.Sqrt, [tcc], [tcc], scale=-1.0, bias=1.0)
                ACT(B_, px[:, 0:n], AF.Sigmoid, [px, ptab], [tb], bias=pv(l, "lru_bx", j))
                yield
                TT(B_, B_, xc[j][:, 0, t0:t0 + n], ALU.mult, [tb, xc[j]], [tb])
                TT(B_, B_, C_, ALU.mult, [tb, tcc], [tb])
                if t0 == 0:
                    SCAN(hs[j][:, 0, 0:n], A_, B_, lruh[:, l, j:j + 1], [ta, tb, lruh], [hs[j]])
                    CP(lruh[:, l, j:j + 1], hs[j][:, 0, n - 1:n], [hs[j]], [lruh])
                else:
                    a3 = bview(A_, NSQ, TS); b3 = bview(B_, NSQ, TS)
                    TT(C_[:, 0:NSQ], a3[:, :, 0], hs0[:, j, :], ALU.mult, [ta, hs0], [tcc])
                    TT(b3[:, :, 0], b3[:, :, 0], C_[:, 0:NSQ], ALU.add, [tb, tcc], [tb])
                    MSET(a3[:, :, 0], 0.0, [ta], eng="dve")
                    SCAN(hs[j][:, 0, t0:t0 + n], A_, B_, 0.0, [ta, tb], [hs[j]])
                yield
                ACT(A_, pgl[:, 0:n], AF.Copy, [pgl], [ta])
                TT(B_, A_, A_, ALU.mult, [ta], [tb])
                TSC(B_, B_, 2.0 * GC * 0.044715, 2.0 * GC, ALU.mult, ALU.add, [tb], [tb])
                TT(B_, B_, A_, ALU.mult, [tb, ta], [tb])
                ACT(B_, B_, AF.Sigmoid, [tb], [tb])
                TT(B_, B_, A_, ALU.mult, [tb, ta], [tb])
                TT(y[:, 12 + j, t0:t0 + n], hs[j][:, 0, t0:t0 + n], B_, ALU.mult, [hs[j], tb], [y])
            if last:
                CP(lhcol[:, l, j, 0:1], hs[j][:, 0, NPT - 1:NPT], [hs[j]], [lhcol])
                CP(lhcol[:, l, j, 1:1 + NSQ], bview(hs[j][:, 0, NPT:NT], NSQ, TS)[:, :, TS - 1], [hs[j]], [lhcol])
        if len(tsets) == 2:
            for (ja, jb) in ((0, 1), (2, 3)):
                gens = [LJ(ja, *tsets[0]), LJ(jb, *tsets[1])]
                alive = [True, True]
                while any(alive):
                    for gi in range(2):
                        if alive[gi]:
                            try:
                                next(gens[gi])
                            except StopIteration:
                                alive[gi] = False
        else:
            for j in range(4):
                for _ in LJ(j, *tsets[0]):
                    pass
        if last:
            TOK_OUT(xl32, lambda j: xl32[j][:, 0, NPT - 3:NPT], 3, [(0, 3, o_lc[l, 0])], "lcp")
            TOK_OUT(xl32, lambda j: xl32[j][:, 0, NPT:NT], NSM, [(q * TS + 5, q * TS + 8, o_lc[l, 1 + q]) for q in range(NSQ)], "lcs")
    def MIX_HG(ti, l, NT):
        last = ti == NTILE - 1
        sl = slabs(NT)
        HC = 32
        chs = chunks_of(NT, HC)
        ncp = NPT // HC
        nch = len(chs)
        wq = WLOAD(kview(w_in[l], 0, 512), 8, 512)
        wf = WLOAD(kview(w_in[l], 512, 512), 8, 512)
        wi = WLOAD(kview(w_in[l], 1024, 512), 8, 512)
        wo_ = WLOAD(kview(w_in[l], 1536, 512), 8, 512)
        t1 = aalloc("hg_t1", [NT], F32); t2 = aalloc("hg_t2", [NT], F32); t3 = aalloc("hg_t3", [NT], F32)
        t4 = aalloc("hg_t4", [NT], F32)
        qbc = aalloc("hg_qbc", [NT], BF16); kbc = aalloc("hg_kbc", [NT], BF16); kdc = aalloc("hg_kdc", [NT], BF16)
        st_ = aalloc("hg_st", [4, nch], F32)
        vtok = aalloc("hg_vtok", [nch, 128], BF16, parts=HC)
        kdtok = aalloc("hg_kdtok", [nch, 128], BF16, parts=HC)
        scm = aalloc("hg_scm", [nch, HC], BF16, parts=HC)
        MSET(scm[:], 0.0, [scm], eng="dve")
        Sall = aalloc("hg_Sall", [17, 128], F32)
        Sbf = aalloc("hg_Sbf", [16, 128], BF16)
        pSsb = aalloc("hg_pSsb", [16, 128], F32)
        pSg = []
        for g_ in range(4):
            b_ = Buf("hg_pSsb_g%d" % g_, ar["fence"]); ar["bufs"].append(b_); pSg.append(T(pSsb.ap, b_))
        o32 = aalloc("hg_o32", [NT], F32)
        osq = aalloc("hg_osq", [NT], BF16)
        def emit_proj(hd):
            lst = []
            for (t0, n) in sl:
                p1 = PS(hold=True); p2 = PS(hold=True)
                for kc in range(8):
                    MM(p1[:, 0:n], wq[:, kc, hd * 128:(hd + 1) * 128], h[:, kc, t0:t0 + n], kc == 0, kc == 7, [wq, h], [p1])
                for kc in range(8):
                    MM(p2[:, 0:n], wf[:, kc, hd * 128:(hd + 1) * 128], h[:, kc, t0:t0 + n], kc == 0, kc == 7, [wf, h], [p2])
                lst.append((p1, p2, t0, n))
            return lst
        pend_proj = emit_proj(0)
        for hd in range(4):
            A_ = t1[:, 0, :]; B_ = t2[:, 0, :]; C_ = t3[:, 0, :]; D_ = t4[:, 0, :]
            for (p1, p2, t0, n) in pend_proj:
                ACT(A_[:, t0:t0 + n], p1[:, 0:n], AF.Silu, [p1], [t1])
                PREL(p1)
                ACT(B_[:, t0:t0 + n], p2[:, 0:n], AF.Sigmoid, [p2], [t2])
                PREL(p2)
            TSC(B_[:, 0:NT], B_[:, 0:NT], dv(l, 4 + hd), dv(l, hd), ALU.mult, ALU.add, [t2, ptab], [t2])
            ACT(C_[:, 0:NT], B_[:, 0:NT], AF.Ln, [t2], [t3])
            TSC(B_[:, 0:NT], B_[:, 0:NT], -1.0, 1.0, ALU.mult, ALU.add, [t2], [t2])
            SCAN(D_[:, 0:NT], cmask_h[:, 0:NT], C_[:, 0:NT], 0.0, [cmask_h, t3], [t4])
            bp = bview(D_[:, 0:NPT], ncp, HC)
            CP(st_[:, 0, 0:ncp], bp[:, :, HC // 2 - 1], [t4], [st_])
            CP(st_[:, 1, 0:ncp], bp[:, :, HC - 1], [t4], [st_])
            if last:
                bs_ = bview(D_[:, NPT:NT], NSQ, TS)
                CP(st_[:, 0, ncp:nch], bs_[:, :, TS // 2 - 1], [t4], [st_])
                CP(st_[:, 1, ncp:nch], bs_[:, :, TS - 1], [t4], [st_])
            TT(st_[:, 3, :], st_[:, 1, :], st_[:, 0, :], ALU.subtract, [st_], [st_])
            ACT(st_[:, 1:4, :], st_[:, 1:4, :], AF.Exp, [st_], [st_]) if False else None
            ACT(st_[:, 2, :], st_[:, 0, :], AF.Exp, [st_], [st_])
            ACT(st_[:, 1, :], st_[:, 1, :], AF.Exp, [st_], [st_])
            ACT(st_[:, 3, :], st_[:, 3, :], AF.Exp, [st_], [st_])
            TT(bp, bp, st_[:, 0, 0:ncp].unsqueeze(2).to_broadcast([128, ncp, HC]), ALU.subtract, [t4, st_], [t4])
            if last:
                TT(bs_, bs_, st_[:, 0, ncp:nch].unsqueeze(2).to_broadcast([128, NSQ, TS]), ALU.subtract, [t4, st_], [t4])
            ACT(C_[:, 0:NT], D_[:, 0:NT], AF.Exp, [t4], [t3])
            TT(qbc[:, 0, 0:NT], A_[:, 0:NT], C_[:, 0:NT], ALU.mult, [t1, t3], [qbc])
            ACT(C_[:, 0:NT], D_[:, 0:NT], AF.Exp, [t4], [t3], scale=-1.0)
            TT(kbc[:, 0, 0:NT], B_[:, 0:NT], C_[:, 0:NT], ALU.mult, [t2, t3], [kbc])
            TT(bview(kdc[:, 0, 0:NPT], ncp, HC), bview(kbc[:, 0, 0:NPT], ncp, HC),
               st_[:, 3, 0:ncp].unsqueeze(2).to_broadcast([128, ncp, HC]), ALU.mult, [kbc, st_], [kdc])
            if last:
                TT(bview(kdc[:, 0, NPT:NT], NSQ, TS), bview(kbc[:, 0, NPT:NT], NSQ, TS),
                   st_[:, 3, ncp:nch].unsqueeze(2).to_broadcast([128, NSQ, TS]), ALU.mult, [kbc, st_], [kdc])
            for g0 in range(0, nch, 4):
                grp = chs[g0:g0 + 4]
                pv_ = PS()
                for gi, (t0, C, kind, ci) in enumerate(grp):
                    for kc in range(8):
                        MM(pv_[0:C, gi * 128:(gi + 1) * 128], h[:, kc, t0:t0 + C], wi[:, kc, hd * 128:(hd + 1) * 128], kc == 0, kc == 7, [h, wi], [pv_])
                C = grp[0][1]
                CP(vtok[0:C, g0:g0 + len(grp), :], bview(pv_[0:C, 0:len(grp) * 128], len(grp), 128), [pv_], [vtok], eng="act")
            for g0 in range(0, nch, 8):
                grp = chs[g0:g0 + 8]
                pt = PS(); ptb = pt.ap[:, :].bitcast(BF16)
                psc = PS()
                for gi, (t0, C, kind, ci) in enumerate(grp):
                    TR(ptb[0:C, gi * 128:(gi + 1) * 128], kdc[:, 0, t0:t0 + C], ident_b[:], [kdc, ident_b], [pt])
                    MM(psc[0:C, gi * HC:gi * HC + C], kbc[:, 0, t0:t0 + C], qbc[:, 0, t0:t0 + C], True, True, [kbc, qbc], [psc])
                C = grp[0][1]
                ng = len(grp)
                CP(kdtok[0:C, g0:g0 + ng, :], bview(ptb[0:C, 0:ng * 128], ng, 128), [pt], [kdtok], eng="act")
                P.op("dve", lambda e, C=C, g0=g0, ng=ng, psc=psc: e.copy_predicated(
                    out=scm[0:C, g0:g0 + ng, 0:C], mask=mask_incl_i[0:C, 0:ng, 0:C],
                    data=bview(psc[0:C, 0:ng * HC], ng, HC)[:, :, 0:C]), bl([psc, mask_incl_i, scm]), bl([scm]))
            for g0 in range(0, ncp, 4):
                pb = PS()
                for gi in range(4):
                    k_ = g0 + gi
                    MM(pb[:, gi * 128:(gi + 1) * 128], kdtok[0:HC, k_, :], vtok[0:HC, k_, :], True, True, [kdtok, vtok], [pb])
                CP(pSsb[:, g0:g0 + 4, :], bview(pb[:, :], 4, 128), [pb], [pSg[g0 // 4]], eng="act")
            if hd + 1 < 4:
                pend_proj = emit_proj(hd + 1)
            CP(Sall[:, 0, :], hgS[:, l, hd, :], [hgS], [Sall])
            for c in range(ncp):
                STT(Sall[:, c + 1, :], Sall[:, c, :], st_[:, 1, c:c + 1], pSsb[:, c, :], ALU.mult, ALU.add, [Sall, st_, pSg[c // 4]], [Sall])
            CP(hgS[:, l, hd, :], Sall[:, ncp, :], [Sall], [hgS])
            for hf in range(2):
                cs = slice(hf * (ncp // 2), (hf + 1) * (ncp // 2))
                TT(Sbf[:, cs, :], Sall[:, cs, :], st_[:, 2, cs].unsqueeze(2).to_broadcast([128, ncp // 2, 128]), ALU.mult, [Sall, st_], [Sbf],
                   eng=("dve" if hf == 0 else "pool"))
            po_p = PS(hold=True)
            for k_ in range(ncp):
                t0 = k_ * HC
                MM(po_p[:, t0:t0 + HC], vtok[0:HC, k_, :], scm[0:HC, k_, 0:HC], True, False, [vtok, scm], [po_p])
                MM(po_p[:, t0:t0 + HC], Sbf[:, k_, :], qbc[:, 0, t0:t0 + HC], False, True, [Sbf, qbc], [po_p])
            po_s = None
            if last:
                DMA(Sall[:, 0:NSQ, :], s_hg[l, :, hd].rearrange("q k v -> k q v"), [], [Sall])
                TT(Sbf[:, 0:NSQ, :], Sall[:, 0:NSQ, :], st_[:, 2, ncp:nch].unsqueeze(2).to_broadcast([128, NSQ, 128]), ALU.mult, [Sall, st_], [Sbf])
                po_s = PS(hold=True)
                for q in range(NSQ):
                    k_ = ncp + q
                    t0 = NPT + q * TS
                    MM(po_s[:, q * TS:(q + 1) * TS], vtok[0:TS, k_, :], scm[0:TS, k_, 0:TS], True, False, [vtok, scm], [po_s])
                    MM(po_s[:, q * TS:(q + 1) * TS], Sbf[:, q, :], qbc[:, 0, t0:t0 + TS], False, True, [Sbf, qbc], [po_s])
                TT(Sall[:, 0:NSQ, :], Sall[:, 0:NSQ, :], st_[:, 1, ncp:nch].unsqueeze(2).to_broadcast([128, NSQ, 128]), ALU.mult, [Sall, st_], [Sall])
                for g0 in range(0, NSQ, 4):
                    pb = PS()
                    for gi in range(4):
                        k_ = ncp + g0 + gi
                        MM(pb[:, gi * 128:(gi + 1) * 128], kdtok[0:TS, k_, :], vtok[0:TS, k_, :], True, True, [kdtok, vtok], [pb])
                    TT(Sall[:, g0:g0 + 4, :], Sall[:, g0:g0 + 4, :], bview(pb[:, :], 4, 128), ALU.add, [Sall, pb], [Sall])
                OUT(o_hg[l, 1:1 + NSQ, hd].rearrange("q k v -> k q v"), Sall[:, 0:NSQ, :], [Sall], "hgs")
            if last:
                OUT(o_hg[l, 0, hd], hgS[:, l, hd, :], [hgS], "hgp")
            for (t0, n), pp in zip(sl, (po_p, po_s)):
                CP(o32[:, 0, t0:t0 + n], pp[:, 0:n], [pp], [o32], eng="act")
            PREL(po_p)
            if last:
                PREL(po_s)
            for (t0, n), pp in zip(sl, (po_p, po_s)):
                TT(osq[:, 0, t0:t0 + n], o32[:, 0, t0:t0 + n], o32[:, 0, t0:t0 + n], ALU.mult, [o32], [osq])
                pn = PS(); pg = PS()
                MM(pn[:, 0:n], ones_b[:], osq[:, 0, t0:t0 + n], True, True, [ones_b, osq], [pn])
                for kc in range(8):
                    MM(pg[:, 0:n], wo_[:, kc, hd * 128:(hd + 1) * 128], h[:, kc, t0:t0 + n], kc == 0, kc == 7, [wo_, h], [pg])
                ACT(A_[:, t0:t0 + n], pn[:, 0:n], AF.Ln, [pn], [t1], scale=1.0 / 128, bias=EPS)
                ACT(A_[:, t0:t0 + n], A_[:, t0:t0 + n], AF.Exp, [t1], [t1], scale=-0.5)
                ACT(B_[:, t0:t0 + n], pg[:, 0:n], AF.Silu, [pg], [t2])
                STT(A_[:, t0:t0 + n], o32[:, 0, t0:t0 + n], pv(l, "hg_norm_g", hd), A_[:, t0:t0 + n], ALU.mult, ALU.mult, [o32, ptab, t1], [t1])
                TT(y[:, hd, t0:t0 + n], A_[:, t0:t0 + n], B_[:, t0:t0 + n], ALU.mult, [t1, t2], [y])
    bones = sb("bones", [128, 128], BF16)
    bones2 = sb("bones2", [128, 2], BF16)
    MSET(bones[:], 0.0, [bones]); MSET(bones2[:], 0.0, [bones2])
    for par in range(2):
        MSET(bones[par * 64:(par + 1) * 64, par * 64:(par + 1) * 64], 1.0, [bones])
        MSET(bones2[par * 64:(par + 1) * 64, par:par + 1], 1.0, [bones2])
    mask_lower = sb("mask_lower", [64, 64], F32)
    MSET(mask_lower[:], 1.0, [mask_lower])
    P.op("pool", lambda e: e.affine_select(out=mask_lower[:], in_=mask_lower[:], pattern=[[-1, 64]], compare_op=ALU.is_ge,
                                           fill=0.0, base=-1, channel_multiplier=1), [mask_lower.b], [mask_lower.b])
    mask_incl_neg = sb("mask_incl_neg", [64, 64], F32)
    TSC(mask_incl_neg[:], mask_incl[:], -1.0, None, ALU.mult, None, [mask_incl], [mask_incl_neg])
    for l in range(L):
        TSC(dv(l, 30, 4), pv(l, "rw_k_a", 0, 4), -1.0, 1.0, ALU.mult, ALU.add, [ptab], [ptab])
    rwst_bd = sb("rwst_bd", [128, 4, 128], BF16)
    MSET(rwst_bd[:], 0.0, [rwst_bd])

    def MIX_RW(ti, l, NT):
        last = ti == NTILE - 1
        sl = slabs(NT)
        chs = chunks_of(NT)
        ncp = NPT // CH
        nch = len(chs)
        wts = [WLOAD(kview(w_in[l], 2048 + i * 512, 512), 8, 512) for i in range(3)]
        wl_ = WLOAD(kview(w_in[l], 3584, 256), 8, 256)
        At = aalloc("rw_At", [4, NT], BF16); Rt = aalloc("rw_Rt", [4, NT], BF16)
        Kt = aalloc("rw_Kt", [4, NT], BF16); Bt = aalloc("rw_Bt", [4, NT], BF16)
        rk = aalloc("rw_rk", [4, NT], BF16); vb = aalloc("rw_vb", [4, NT], BF16)
        sgd = aalloc("rw_sgd", [NT], BF16)
        ecl = aalloc("rw_ecl", [4, nch], F32)
        mark = ar["off"]
        r32 = aalloc("rw_r32", [4, NT], F32); k32 = aalloc("rw_k32", [4, NT], F32)
        twd = aalloc("rw_twd", [NT], BF16)
        mark2 = ar["off"]
        rexp = [aalloc("rw_rexp%d" % i, [1 + NPT], F32) for i in range(2)]
        rexs = [aalloc("rw_rexs%d" % i, [NSQ * (TS + 1)], F32) for i in range(2)] if last else None
        shs = aalloc("rw_shs", [14, NSQ], F32) if last else None
        if last:
            hh_ap = y.ap[0:NSQ, 4:10, :].rearrange("p a b -> p (a b)").bitcast(F32)[:, 0:RW_COLS]
            hh = T(hh_ap.rearrange("p (a b) -> p a b", a=1), y.b)
            DMA(hh[:, 0, :], s_sh[l], [], [hh])
            for g in range(2):
                ps = PS()
                for kc in range(7):
                    TR(ps[:, kc * 16:(kc + 1) * 16], hh[:, 0, (g * 7 + kc) * 128:(g * 7 + kc + 1) * 128], ident_f[0:16, 0:16], [hh, ident_f], [ps])
                CP(shs[:, g * 7:(g + 1) * 7, :].rearrange("p a b -> p (a b)"), ps[:, 0:112], [ps], [shs])
        for fc in range(14):
            wt = wts[fc // 4] if fc < 12 else wl_
            jj = fc % 4 if fc < 12 else fc - 12
            rp = rexp[fc % 2]
            rs_ = rexs[fc % 2] if last else None
            CP(rp[:, 0, 0:1], rwprev[:, l, fc:fc + 1], [rwprev], [rp])
            if last:
                CP(bview(rs_[:, 0, :], NSQ, TS + 1)[:, :, 0], shs[:, fc, :], [shs], [rs_])
            for (t0, n) in sl:
                pp = PS()
                for kc in range(8):
                    MM(pp[:, 0:n], wt[:, kc, jj * 128:(jj + 1) * 128], h[:, kc, t0:t0 + n], kc == 0, kc == 7, [wt, h], [pp])
                if t0 == 0:
                    CP(rp[:, 0, 1:1 + NPT], pp[:, 0:n], [pp], [rp], eng="act")
                else:
                    CP(bview(rs_[:, 0, :], NSQ, TS + 1)[:, :, 1:TS + 1], bview(pp[:, 0:n], NSQ, TS), [pp], [rs_], eng="act")
            CP(rwprev[:, l, fc:fc + 1], rp[:, 0, NPT:NPT + 1], [rp], [rwprev])
            if last:
                CP(shcol[:, l, fc, 0:1], rp[:, 0, NPT:NPT + 1], [rp], [shcol])
                CP(shcol[:, l, fc, 1:1 + NSQ], bview(rs_[:, 0, :], NSQ, TS + 1)[:, :, TS], [rs_], [shcol])
            if fc < 4:
                dst, dst_t = r32[:, fc, :], r32
            elif fc < 8:
                dst, dst_t = k32[:, fc - 4, :], k32
            elif fc < 12:
                dst, dst_t = vb[:, fc - 8, :], vb
            elif fc == 12:
                dst, dst_t = sgbuf[:, :], sgbuf
            else:
                dst, dst_t = sgbuf[:, :], sgbuf
            for (t0, n) in sl:
                if t0 == 0:
                    raw = rp[:, 0, 1:1 + NPT]; prv = rp[:, 0, 0:NPT]; rd = rp
                    d_ = dst[:, 0:NPT] if fc < 12 else sgbuf[:, 0:NPT]
                    tm = sgbuf[:, 0:NPT] if fc < 12 and fc >= 8 else d_
                else:
                    raw = bview(rs_[:, 0, :], NSQ, TS + 1)[:, :, 1:TS + 1]; prv = bview(rs_[:, 0, :], NSQ, TS + 1)[:, :, 0:TS]; rd = rs_
                    d_ = bview(dst[:, NPT:NT], NSQ, TS) if fc < 12 else bview(xtmp[:, 0:NSM], NSQ, TS)
                    tm = bview(xtmp[:, 0:NSM], NSQ, TS) if fc < 12 and fc >= 8 else d_
                tmt = sgbuf if t0 == 0 else xtmp
                wr = [dst_t] if fc < 8 else [tmt]
                TSC(tm, raw, dv(l, 16 + fc), None, ALU.mult, None, [rd, ptab], wr)
                STT(tm, prv, pv(l, "rw_mu", fc), tm, ALU.mult, ALU.add, [rd, ptab] + wr, wr)
                if 8 <= fc < 12:
                    CP(d_, tm, wr, [vb], eng="act")
                elif fc == 12:
                    src = tm
                    if t0 == 0:
                        ACT(twd[0:64, 0, 0:NPT], sgbuf[0:64, 0:NPT], AF.Tanh, [sgbuf], [twd])
                        CP(twd[64:128, 0, 0:NPT], sgbuf[64:128, 0:NPT], [sgbuf], [twd], eng="act")
                    else:
                        ACT(twd[0:64, 0, NPT:NT], xtmp[0:64, 0:NSM], AF.Tanh, [xtmp], [twd])
                        CP(twd[64:128, 0, NPT:NT], xtmp[64:128, 0:NSM], [xtmp], [twd], eng="act")
                elif fc == 13:
                    if t0 == 0:
                        ACT(sgd[:, 0, 0:NPT], sgbuf[:, 0:NPT], AF.Sigmoid, [sgbuf], [sgd])
                    else:
                        ACT(sgd[:, 0, NPT:NT], xtmp[:, 0:NSM], AF.Sigmoid, [xtmp], [sgd])
        if stage < 2.2:
            return
        arelease(mark2)
        ta = aalloc("rw_ta", [NT], F32); tb = aalloc("rw_tb", [NT], F32); tcc = aalloc("rw_tc", [NT], F32)
        td = aalloc("rw_td", [NT], F32); te = aalloc("rw_te", [NT], F32)
        for j in range(4):
            A_ = ta[:, 0, :]; B_ = tb[:, 0, :]; C_ = tcc[:, 0, :]; D_ = td[:, 0, :]; E_ = te[:, 0, :]
            for (t0, n) in sl:
                pw = PS(); pa = PS()
                MM(pw[:, 0:n], rwup[0:64, l, 0, j * 128:(j + 1) * 128], twd[0:64, 0, t0:t0 + n], True, True, [rwup, twd], [pw])
                MM(pa[:, 0:n], rwup[64:128, l, 1, j * 128:(j + 1) * 128], twd[64:128, 0, t0:t0 + n], True, True, [rwup, twd], [pa])
                ACT(A_[:, t0:t0 + n], pw[:, 0:n], AF.Sigmoid, [pw, ptab], [ta], bias=pv(l, "rw_w0", j))
                ACT(B_[:, t0:t0 + n], pa[:, 0:n], AF.Sigmoid, [pa, ptab], [tb], bias=pv(l, "rw_a0", j))
            TSC(A_[:, 0:NT], A_[:, 0:NT], -0.606531, None, ALU.mult, None, [ta], [ta])
            SCAN(C_[:, 0:NT], cmask[:, 0:NT], A_[:, 0:NT], 0.0, [cmask, ta], [tcc])
            CP(ecl[:, j, 0:ncp], bview(C_[:, 0:NPT], ncp, CH)[:, :, CH - 1], [tcc], [ecl])
            if last:
                CP(ecl[:, j, ncp:nch], bview(C_[:, NPT:NT], NSQ, TS)[:, :, TS - 1], [tcc], [ecl])
            ACT(ecl[:, j, :], ecl[:, j, :], AF.Exp, [ecl], [ecl])
            TT(A_[:, 0:NT], C_[:, 0:NT], A_[:, 0:NT], ALU.subtract, [tcc, ta], [ta])
            ACT(A_[:, 0:NT], A_[:, 0:NT], AF.Exp, [ta], [ta])
            TSC(D_[:, 0:NT], k32[:, j, 0:NT], pv(l, "rw_k_k", j), None, ALU.mult, None, [k32, ptab], [td])
            TT(sgd2[:, 0:NT], D_[:, 0:NT], D_[:, 0:NT], ALU.mult, [td], [sgd2])
            for (t0, n) in sl:
                pn = PS()
                MM(pn[:, 0:n], bones[:], sgd2[:, t0:t0 + n], True, True, [bones, sgd2], [pn])
                TSC(E_[:, t0:t0 + n], pn[:, 0:n], 6e-20, None, ALU.max, None, [pn], [te])
            ACT(E_[:, 0:NT], E_[:, 0:NT], AF.Ln, [te], [te])
            ACT(E_[:, 0:NT], E_[:, 0:NT], AF.Exp, [te], [te], scale=-0.5)
            TT(D_[:, 0:NT], D_[:, 0:NT], E_[:, 0:NT], ALU.mult, [td, te], [td])
            TT(At[:, j, 0:NT], D_[:, 0:NT], A_[:, 0:NT], ALU.mult, [td, ta], [At])
            ACT(A_[:, 0:NT], C_[:, 0:NT], AF.Exp, [tcc], [ta], scale=-1.0)
            TT(D_[:, 0:NT], D_[:, 0:NT], B_[:, 0:NT], ALU.mult, [td, tb], [td])
            TT(Bt[:, j, 0:NT], D_[:, 0:NT], A_[:, 0:NT], ALU.mult, [td, ta], [Bt])
            TSC(B_[:, 0:NT], B_[:, 0:NT], pv(l, "rw_k_a", j), dv(l, 30 + j), ALU.mult, ALU.add, [tb, ptab], [tb])
            TT(B_[:, 0:NT], B_[:, 0:NT], k32[:, j, 0:NT], ALU.mult, [tb, k32], [tb])
            TT(Kt[:, j, 0:NT], B_[:, 0:NT], A_[:, 0:NT], ALU.mult, [tb, ta], [Kt])
            TT(B_[:, 0:NT], B_[:, 0:NT], r32[:, j, 0:NT], ALU.mult, [tb, r32], [tb])
            TSC(rk[:, j, 0:NT], B_[:, 0:NT], pv(l, "rw_r_k", j), None, ALU.mult, None, [tb, ptab], [rk])
            ACT(C_[:, 0:NT], C_[:, 0:NT], AF.Exp, [tcc], [tcc])
            TT(Rt[:, j, 0:NT], r32[:, j, 0:NT], C_[:, 0:NT], ALU.mult, [r32, tcc], [Rt])
        arelease(mark)
        if stage < 2.4:
            return
        yraw = aalloc("rw_yraw", [4, NT], BF16)
        mark3 = ar["off"]
        gM = aalloc("rw_gM", [8, CH], BF16, parts=CH); gMT = aalloc("rw_gMT", [8, CH], BF16, parts=CH)
        gQ = aalloc("rw_gQ", [8, CH], BF16, parts=CH); gN = aalloc("rw_gN", [8, CH], BF16, parts=CH); gP = aalloc("rw_gP", [8, CH], BF16, parts=CH)
        Tb = [aalloc("rw_T%d" % i, [8, CH], BF16, parts=CH) for i in range(2)]
        Pb = [aalloc("rw_P%d" % i, [8, CH], BF16, parts=CH) for i in range(2)]
        PTb = [aalloc("rw_PT%d" % i, [8, CH], BF16, parts=CH) for i in range(2)]
        vtok = aalloc("rw_vtok", [W], BF16, parts=CH); ktok = aalloc("rw_ktok", [W], BF16, parts=CH); btok = aalloc("rw_btok", [W], BF16, parts=CH)
        gN2 = aalloc("rw_gN2", [8, CH], BF16, parts=CH); gQ2 = aalloc("rw_gQ2", [8, CH], BF16, parts=CH); gP2 = aalloc("rw_gP2", [8, CH], BF16, parts=CH)
        Tb2 = [aalloc("rw_T2%d" % i, [8, CH], BF16, parts=CH) for i in range(2)]
        vtok2 = aalloc("rw_vtok2", [W], BF16, parts=CH); ktok2 = aalloc("rw_ktok2", [W], BF16, parts=CH); btok2 = aalloc("rw_btok2", [W], BF16, parts=CH)
        gNs, gQs, gPs = [gN, gN2], [gQ, gQ2], [gP, gP2]
        vtoks, ktoks, btoks = [vtok, vtok2], [ktok, ktok2], [btok, btok2]
        Tbs = [Tb, Tb2]
        gt = aalloc("rw_gt", [W], BF16, parts=CH); ut = aalloc("rw_ut", [W], BF16, parts=CH)
        yA = aalloc("rw_yA", [W], F32, parts=CH); yB = aalloc("rw_yB", [W], F32, parts=CH)
        yst = aalloc("rw_yst", [4, 8], F32, parts=CH)
        yob = aalloc("rw_yob", [W], BF16, parts=CH)
        sts = aalloc("rw_sts", [4, 64], F32) if last else None
        stl = aalloc("rw_stl", [8, 64], F32, parts=64) if last else None
        sto = aalloc("rw_sto", [W], F32, parts=64) if last else None
        stmp = aalloc("rw_stmp", [4, 64], F32)

        def state_out(ST_t, ST_ap, dst):
            ps = PS()
            for j in range(4):
                TR(ps[0:64, j * 128:(j + 1) * 128], ST_ap[:, j, :], ident_f[:], [ST_t, ident_f], [ps])
            CP(sto[:, 0, :], ps[0:64, :], [ps], [sto], eng="act")
            OUT(dst.rearrange("h v k -> v h k"), sto[:, 0, :].rearrange("v (h k) -> v h k", h=8), [sto], "rwst")

        def SI(k_, t0, C, par, hook):
            nlev = {64: 5, 8: 2}[C]
            gN, gQ, gP = gNs[par], gQs[par], gPs[par]
            vtok, ktok, btok = vtoks[par], ktoks[par], btoks[par]
            Tb = Tbs[par]
            pM = PS(); pMT = PS(); pQ = PS(hold=True); pN = PS(hold=True); pP = PS(hold=True)
            for hh_ in range(8):
                j, par = hh_ // 2, hh_ % 2
                rows = slice(par * 64, par * 64 + 64)
                o_ = slice(hh_ * CH, hh_ * CH + C)
                a_ = At[rows, j, t0:t0 + C]; r_ = Rt[rows, j, t0:t0 + C]; k__ = Kt[rows, j, t0:t0 + C]; b_ = Bt[rows, j, t0:t0 + C]
                MM(pM[0:C, o_], b_, a_, True, True, [Bt, At], [pM])
                MM(pMT[0:C, o_], a_, b_, True, True, [Bt, At], [pMT])
                MM(pQ[0:C, o_], b_, r_, True, True, [Bt, Rt], [pQ])
                MM(pN[0:C, o_], k__, a_, True, True, [Kt, At], [pN])
                MM(pP[0:C, o_], k__, r_, True, True, [Kt, Rt], [pP])

            for src_, dst_, neg in ((vb, vtok, False), (Kt, ktok, False), (Bt, btok, True)):
                pt = PS(); ptb = pt.ap[:, :].bitcast(BF16)
                for j in range(4):
                    TR(ptb[0:C, j * 128:(j + 1) * 128], src_[:, j, t0:t0 + C], ident_b[:], [src_, ident_b], [pt])
                if neg:
                    ACT(dst_[0:C, 0, :], ptb[0:C, 0:W], AF.Copy, [pt], [dst_], scale=-1.0)
                else:
                    CP(dst_[0:C, 0, :], ptb[0:C, 0:W], [pt], [dst_], eng="act")
            def g3(t_):
                return t_[0:C, :, 0:C]

            def p3(p_):
                return bview(p_[0:C, :], 8, CH)[:, :, 0:C]

            def mk(m_):
                return m_[0:C, 0:C].unsqueeze(1).to_broadcast([C, 8, C])
            TT(g3(gM), p3(pM), mk(mask_strict), ALU.mult, [pM, mask_strict], [gM])
            TT(g3(gMT), p3(pMT), mk(mask_lower), ALU.mult, [pMT, mask_lower], [gMT])
            if hook is not None and C == TS:
                hook()
            late = [(gN, pN, mask_strict), (gQ, pQ, mask_incl_neg), (gP, pP, mask_incl)]

            def late_evac():
                if late:
                    g_, p_, m_ = late.pop(0)
                    TT(g3(g_), p3(p_), mk(m_), ALU.mult, [p_, m_], [g_])
                    PREL(p_)
            Tc = Tb[0]
            Pc, PTc = gM, gMT

            def emit_PP(lev, Pc, PTc):
                lastlev = lev == nlev
                pp1 = PS() if not lastlev else None
                pp2 = PS()
                for hh_ in range(8):
                    o_ = slice(hh_ * CH, hh_ * CH + C)
                    if not lastlev:
                        MM(pp1[0:C, o_], PTc[0:C, hh_, 0:C], Pc[0:C, hh_, 0:C], True, True, [PTc, Pc], [pp1])
                    MM(pp2[0:C, o_], Pc[0:C, hh_, 0:C], PTc[0:C, hh_, 0:C], True, True, [PTc, Pc], [pp2])
                return pp1, pp2
            pend = emit_PP(1, Pc, PTc)
            TT(g3(Tc), mk(ident_f), g3(gM), ALU.subtract, [ident_f, gM], [Tc])
            if hook is not None and C == TS:
                hook()
            tpend = None
            for lev in range(1, nlev + 1):
                Pn, PTn, Tn = Pb[lev % 2], PTb[lev % 2], Tb[lev % 2]
                lastlev = lev == nlev
                pp1, pp2 = pend
                if not lastlev:
                    CP(g3(Pn), p3(pp1), [pp1], [Pn], eng="act")
                CP(g3(PTn), p3(pp2), [pp2], [PTn], eng="dve")
                if tpend is not None:
                    pp3_, Told_, Tnew_ = tpend
                    TT(g3(Tnew_), p3(pp3_), g3(Told_), ALU.add, [pp3_, Told_], [Tnew_])
                    PREL(pp3_)
                    Tc = Tnew_
                    tpend = None
                late_evac()
                if not lastlev:
                    pend = emit_PP(lev + 1, Pn, PTn)
                pp3 = PS(hold=True)
                for hh_ in range(8):
                    o_ = slice(hh_ * CH, hh_ * CH + C)
                    MM(pp3[0:C, o_], PTn[0:C, hh_, 0:C], Tc[0:C, hh_, 0:C], True, True, [PTn, Tc], [pp3])
                tpend = (pp3, Tc, Tn)
                Pc, PTc = Pn, PTn
                if hook is not None and not lastlev:
                    hook()
                if lastlev:
                    pp3_, Told_, Tnew_ = tpend
                    TT(g3(Tnew_), p3(pp3_), g3(Told_), ALU.add, [pp3_, Told_], [Tnew_])
                    Tc = Tnew_
                    tpend = None
                    PREL(pp3_)
            while late:
                late_evac()
            return Tc

        def SD(k_, t0, C, kind, ci, par, Tc):
            gN, gQ, gP = gNs[par], gQs[par], gPs[par]
            vtok, ktok, btok = vtoks[par], ktoks[par], btoks[par]
            nlev = {64: 5, 8: 2}[C]
            if kind == "p":
                ST_t, ST_ap = rwST, rwST[:, l, :, :]
            else:
                ST_t, ST_ap = sts, sts[:, :, :]
                DMA(stl[:, :, :], s_rw[l, ci].rearrange("h v k -> v h k"), [], [stl])
                ps = PS()
                for j in range(4):
                    TR(ps[:, j * 64:(j + 1) * 64], stl[:, 2 * j:2 * j + 2, :].rearrange("v h k -> v (h k)"), ident_f[0:64, 0:64], [stl, ident_f], [ps])
                CP(sts[:, :, :].rearrange("p a b -> p (a b)"), ps[:, 0:256], [ps], [sts])
            for par in range(2):
                rows = slice(par * 64, par * 64 + 64)
                CP(rwst_bd[rows, :, par * 64:par * 64 + 64], ST_ap[rows], [ST_t], [rwst_bd], eng="act")
            pG = PS(hold=True)
            for j in range(4):
                MM(pG[0:C, j * 128:(j + 1) * 128], At[:, j, t0:t0 + C], rwst_bd[:, j, :], True, False, [At, rwst_bd], [pG])
                for par in range(2):
                    hh_ = 2 * j + par
                    vs = slice(hh_ * 64, hh_ * 64 + 64)
                    MM(pG[0:C, vs], gN[0:C, hh_, 0:C], vtok[0:C, 0, vs], False, par == 1, [gN, vtok], [pG])
            yield
            CP(gt[0:C, 0, :], pG[0:C, :], [pG], [gt], eng="act")
            PREL(pG)
            pU = PS(hold=True)
            for hh_ in range(8):
                vs = slice(hh_ * 64, hh_ * 64 + 64)
                MM(pU[0:C, vs], Tc[0:C, hh_, 0:C], gt[0:C, 0, vs], True, True, [Tc, gt], [pU])
            yield
            CP(ut[0:C, 0, :], pU[0:C, :], [pU], [ut], eng="act")
            PREL(pU)
            pY = PS(hold=True)
            for j in range(4):
                MM(pY[0:C, j * 128:(j + 1) * 128], Rt[:, j, t0:t0 + C], rwst_bd[:, j, :], True, False, [Rt, rwst_bd], [pY])
                for par in range(2):
                    hh_ = 2 * j + par
                    vs = slice(hh_ * 64, hh_ * 64 + 64)
                    MM(pY[0:C, vs], gP[0:C, hh_, 0:C], vtok[0:C, 0, vs], False, False, [gP, vtok], [pY])
                    MM(pY[0:C, vs], gQ[0:C, hh_, 0:C], ut[0:C, 0, vs], False, par == 1, [gQ, ut], [pY])
            pS = PS(hold=True)
            for j in range(4):
                js = slice(j * 128, (j + 1) * 128)
                MM(pS[:, js], ktok[0:C, 0, js], vtok[0:C, 0, js], True, False, [ktok, vtok], [pS])
                MM(pS[:, js], btok[0:C, 0, js], ut[0:C, 0, js], False, True, [btok, ut], [pS])
            yield
            for par in range(2):
                rows = slice(par * 64, par * 64 + 64)
                TT(stmp[rows, :, :], ST_ap[rows], bview(pS[rows, :], 4, 128)[:, :, par * 64:par * 64 + 64], ALU.add, [ST_t, pS], [stmp])
            PREL(pS)
            TT(ST_ap, stmp[:, :, :], ecl[:, :, k_:k_ + 1].to_broadcast([128, 4, 64]), ALU.mult, [stmp, ecl], [ST_t])
            if kind == "s":
                state_out(ST_t, ST_ap, o_rw[l, 1 + ci])
            if ti == 0 and l == 0 and L > 1:
                ADA_PIECES(1, 2)
            CP(yob[0:C, 0, :], pY[0:C, :], [pY], [yob], eng="act")
            PREL(pY)
            yield
            pt = PS(); ptb = pt.ap[:, :].bitcast(BF16)
            for j in range(4):
                TR(ptb[:, j * 64:j * 64 + C], yob[0:C, 0, j * 128:(j + 1) * 128], ident_b[0:C, 0:C], [yob, ident_b], [pt])
            CP(yraw[:, :, t0:t0 + C], bview(ptb[:, 0:256], 4, 64)[:, :, 0:C], [pt], [yraw], eng="act")

        seq = list(enumerate(chs))
        Tc_cur = SI(seq[0][0], seq[0][1][0], seq[0][1][1], 0, None)
        for i, (k_, (t0, C, kind, ci)) in enumerate(seq):
            sdg = SD(k_, t0, C, kind, ci, i % 2, Tc_cur)
            if i + 1 < len(seq):
                k2, (t02, C2, _, _) = seq[i + 1]
                Tc_cur = SI(k2, t02, C2, (i + 1) % 2, lambda: next(sdg, None))
            for _ in sdg:
                pass
        if last:
            state_out(rwST, rwST[:, l, :, :], o_rw[l, 0])
        arelease(mark3)
        psets = [[aalloc("rw_p%d%d" % (i, k), [NT], F32) for k in range(4)] + [aalloc("rw_psq%d" % i, [NT], BF16)] for i in range(2)]
        for j in range(4):
            yr = T(yraw.ap, Buf("yr%d" % j, ar["fence"]))
            yr.b.last_w = yraw.b.last_w
            ar["bufs"].append(yr.b)
            tA_, tB_, tC_, tD_, tsq = psets[j % 2]
            for (t0, n) in sl:
                tsl = slice(t0, t0 + n)
                ACT(tsq[:, 0, tsl], yraw[:, j, tsl], AF.Square, [yr], [tsq])
                pm = PS(); pq = PS(); pb = PS(); pg = PS()
                MM(pm[:, 0:n], bones[:], yraw[:, j, tsl], True, True, [bones, yr], [pm])
                MM(pq[:, 0:n], bones[:], tsq[:, 0, tsl], True, True, [bones, tsq], [pq])
                MM(pb[:, 0:n], bones[:], rk[:, j, tsl], True, True, [bones, rk], [pb])
                MM(pg[:, 0:n], rwup[:, l, 2, j * 128:(j + 1) * 128], sgd[:, 0, tsl], True, True, [rwup, sgd], [pg])
                A_ = tA_[:, 0, tsl]; B_ = tB_[:, 0, tsl]; C_ = tC_[:, 0, tsl]; D_ = tD_[:, 0, tsl]
                ACT(A_, pm[:, 0:n], AF.Copy, [pm], [tA_], scale=1.0 / 64)
                TT(B_, A_, A_, ALU.mult, [tA_], [tB_])
                STT(B_, pq[:, 0:n], 1.0 / 64, B_, ALU.mult, ALU.subtract, [pq, tB_], [tB_])
                ACT(B_, B_, AF.Ln, [tB_], [tB_], bias=64e-5)
                ACT(B_, B_, AF.Exp, [tB_], [tB_], scale=-0.5)
                TT(C_, yraw[:, j, tsl], A_, ALU.subtract, [yr, tA_], [tC_])
                TT(C_, C_, B_, ALU.mult, [tC_, tB_], [tC_])
                ACT(C_, C_, AF.Identity, [tC_, ptab], [tC_], scale=pv(l, "rw_ln_g", j), bias=pv(l, "rw_ln_b", j))
                TT(D_, pb[:, 0:n], vb[:, j, tsl], ALU.mult, [pb, vb], [tD_])
                TT(C_, C_, D_, ALU.add, [tC_, tD_], [tC_])
                TT(y[:, 4 + j, tsl], C_, pg[:, 0:n], ALU.mult, [tC_, pg], [y])

    def MIXERS(ti, l, NT):
        if stage >= 2:
            MIX_HG(ti, l, NT); areset()
        if stage > 2:
            MIX_RW(ti, l, NT); areset()
        if stage >= 4:
            MIX_CF(ti, l, NT); areset()
        if stage >= 5:
            MIX_LRU(ti, l, NT)

    def FINAL_STATES():
        for l in range(nlayer):
            stg = aalloc("fs_sh%d" % l, [RW_COLS], F32, parts=1 + NSQ)
            for g in range(4):
                ps = PS()
                nk = 4 if g < 3 else 2
                for kk_ in range(nk):
                    kc = g * 4 + kk_
                    TR(ps[0:17, kk_ * 128:(kk_ + 1) * 128], shcol[:, l, kc, :], ident_f[:], [shcol, ident_f], [ps])
                CP(stg[:, 0, g * 512:g * 512 + nk * 128], ps[0:17, 0:nk * 128], [ps], [stg])
            OUT(o_sh[l], stg[:, 0, :], [stg], "sh")
            stg2 = aalloc("fs_lh%d" % l, [W], F32, parts=1 + NSQ)
            ps = PS()
            for j in range(4):
                TR(ps[0:17, j * 128:(j + 1) * 128], lhcol[:, l, j, :], ident_f[:], [lhcol, ident_f], [ps])
            CP(stg2[:, 0, :], ps[0:17, :], [ps], [stg2])
            OUT(o_lh[l], stg2[:, 0, :], [stg2], "lh")

    for ti in range(NTILE):
        NT = NPT + (NSM if ti == NTILE - 1 else 0)
        nblk = NT // 128
        for blk in range(nblk):
            xt = aalloc("xtok%d" % (blk % 2), [D], F32) if blk < 2 else xt_bufs[blk % 2]
            if blk < 2:
                if blk == 0:
                    xt_bufs = []
                xt_bufs.append(xt)
            src = xp[ti * NPT + blk * 128: ti * NPT + (blk + 1) * 128, :] if blk < 4 else xs[:, :]
            DMA(xt[:, 0, :], src, [], [xt])
            for half in range(2):
                ps = PS()
                for j in range(4):
                    kc = half * 4 + j
                    TR(ps[:, j * 128:(j + 1) * 128], xt[:, 0, kc * 128:(kc + 1) * 128], ident_f[:], [xt, ident_f], [ps])
                CP(x[:, half * 4:half * 4 + 4, blk * 128:(blk + 1) * 128], ps[:, :].rearrange("p (j t) -> p j t", j=4), [ps], [x],
                   eng=("act" if half else "dve"))
        areset()
        for l in range(nlayer):
            ADA_PIECES(l, 12)
            if stage >= 1:
                NORM(l, 0, NT)
            areset()
            MIXERS(ti, l, NT)
            areset()
            if stage >= 6:
                MERGE_MLP(l, NT)
        for blk in range(nblk):
            xo = aalloc("xo%d" % (blk % 2), [D], F32) if blk < 2 else xo_bufs[blk % 2]
            if blk < 2:
                if blk == 0:
                    xo_bufs = []
                    fstat = aalloc("fstat", [8], F32)
                    junk = aalloc("junk", [D], F32)
                xo_bufs.append(xo)
            for half in range(2):
                ps = PS()
                for j in range(4):
                    kc = half * 4 + j
                    TR(ps[:, j * 128:(j + 1) * 128], x[:, kc, blk * 128:(blk + 1) * 128], ident_f[:], [x, ident_f], [ps])
                CP(xo[:, 0, half * 512:(half + 1) * 512], ps[:, :], [ps], [xo], eng=("act" if half else "dve"))
            P.op("act", lambda e, xo=xo, junk=junk, fstat=fstat: e.activation(out=junk[:, 0, :], in_=xo[:, 0, :], func=AF.Square,
                                                                              accum_out=fstat[:, 0, 0:1]), bl([xo]), bl([junk, fstat]))
            ACT(fstat[:, 0, 1:2], fstat[:, 0, 0:1], AF.Sqrt, [fstat], [fstat], scale=1.0 / D, bias=EPS)
            RCP(fstat[:, 0, 2:3], fstat[:, 0, 1:2], [fstat], [fstat])
            STT(xo[:, 0, :], xo[:, 0, :], fstat[:, 0, 2:3], gfin[:], ALU.mult, ALU.mult, [xo, fstat, gfin], [xo])
            dst = o_yp[ti * NPT + blk * 128: ti * NPT + (blk + 1) * 128, :] if blk < 4 else o_ys[:, :]
            OUT(dst, xo[:, 0, :], [xo], "y")
        areset()

    if stage >= 6:
        FINAL_STATES()
    print('arena high-water', ar.get('hw'), 'of', ARENA_E, 'sbuf left', nc.sbuf_bytes_remaining)
    P.emit(out_bufs)
    return nc, dbg_outs


_CACHE = {}


def kernel(**inp):
    f = lambda a: np.ascontiguousarray(np.asarray(a, dtype=np.float32))
    if "nc" not in _CACHE:
        _CACHE["nc"] = build()
    nc, _ = _CACHE["nc"]
    shared = {k: f(inp[k]) for k in ("ada_w", "ada_b", "norm_mix_g", "norm_mlp_g", "norm_final_g", "w_in", "hg_lower", "hg_norm_g",
                                     "rw_mu", "rw_w0", "rw_w_up", "rw_a0", "rw_a_up", "rw_g_up", "rw_k_k", "rw_k_a", "rw_r_k",
                                     "rw_ln_g", "rw_ln_b", "cf_dw", "cf_dw_b", "cf_ln_g", "cf_ln_b", "lru_conv_w", "lru_conv_b",
                                     "lru_wa", "lru_ba", "lru_wx", "lru_bx", "lru_lambda", "w_branch", "w_gate", "b_gate", "w_out",
                                     "w_mlp1", "w_mlp2")}
    xp = f(inp["x_prompt"]); xs = f(inp["x_sample"])
    in_maps = []
    for c in range(NCORE):
        sq = slice(c * NSQ, (c + 1) * NSQ)
        m = dict(shared)
        m["xp"] = xp[c]
        m["xs"] = np.ascontiguousarray(xs[sq].reshape(NSM, D))
        m["s_hg"] = f(inp["state_hgrn"][:, sq]); m["s_rw"] = f(inp["state_rwkv"][:, sq])
        m["s_sh"] = f(inp["state_rwkv_shift"][:, sq]); m["s_cf"] = f(inp["state_conv"][:, sq])
        m["s_lh"] = f(inp["state_lru_h"][:, sq]); m["s_lc"] = f(inp["state_lru_conv"][:, sq])
        m["cc"] = np.ascontiguousarray(np.concatenate([f(inp["c_prompt"])[c:c + 1], f(inp["c_sample"])[sq]], axis=0))
        in_maps.append(m)
    res = run_bass_kernel_spmd(nc, in_maps, core_ids=list(range(NCORE)))
    R = res.results
    _CACHE["last"] = R
    y_p = np.stack([R[c]["o_yp"] for c in range(NCORE)], axis=0)
    y_s = np.concatenate([R[c]["o_ys"].reshape(NSQ, TS, D) for c in range(NCORE)], axis=0)
    outs = [y_p, y_s]
    for nm in ("o_hg", "o_rw", "o_sh", "o_cf", "o_lh", "o_lc"):
        outs.append(np.stack([R[c][nm][:, 0] for c in range(NCORE)], axis=1))
    for nm in ("o_hg", "o_rw", "o_sh", "o_cf", "o_lh", "o_lc"):
        outs.append(np.concatenate([R[c][nm][:, 1:] for c in range(NCORE)], axis=1))
    return tuple(np.ascontiguousarray(o.astype(np.float32)) for o in outs)
```
